# Optimizing a Trainium2 kernel written in Bass

```python
import math
import jax, jax.numpy as jnp
from jax import lax
import numpy as np

D_MODEL = 1024
BATCH = 16
SEQ = 256
DEPTH = 4
DEC_BATCH = 4
DEC_SEQ = 4096
PAST_LEN = 256

GRID_W = 64
N_EVEN = (DEPTH + 1) // 2
N_ODD = DEPTH // 2

FNET_W = D_MODEL // 2
FNET_GROUPS = 4
FNET_GW = FNET_W // FNET_GROUPS
CONV_W = D_MODEL // 2
CONV_K = 3
EVEN_IN = FNET_W + 3 * CONV_W
EVEN_MIX = FNET_W + CONV_W

LRU_W = D_MODEL // 2
LRU_HEADS = 8
LRU_HD = LRU_W // LRU_HEADS
LRU_CONV_K = 4
LRU_C = 8.0
WKV_W = D_MODEL // 2
WKV_N = 64
WKV_H = WKV_W // WKV_N
DECAY_RANK = 64
ICLR_RANK = 64
GATE_RANK = 128
WKV_IN = 3 * WKV_W + 2 * DECAY_RANK + 2 * ICLR_RANK + GATE_RANK
ODD_IN = 2 * LRU_W + WKV_IN
ODD_MIX = LRU_W + WKV_W
DECAY_SCALE = math.exp(-0.5)
WKV_GN_EPS = 64e-5

PEER_HEADS = 8
N_KEYS = 128
N_EXPERTS = N_KEYS * N_KEYS
PEER_TOPK = 16
PEER_DK = 256
PEER_HALF = PEER_DK // 2
PEER_BLOCK = 128

ALPHA = (2 * DEPTH) ** 0.25
BETA = (8 * DEPTH) ** -0.25
LN_EPS = 1e-6

kernel_name = 'hybrid_fnet_conv_rglru_rwkv7_peer_diffusion_step'


def _layer_norm(x, g=None, b=None, eps=LN_EPS):
    xf = x.astype(jnp.float32)
    mu = xf.mean(-1, keepdims=True)
    var = jnp.square(xf - mu).mean(-1, keepdims=True)
    y = (xf - mu) * lax.rsqrt(var + eps)
    if g is not None:
        y = y * g.astype(jnp.float32) + b.astype(jnp.float32)
    return y.astype(x.dtype)


def _sincos(pos, dim):
    omega = 1.0 / (10000.0 ** (jnp.arange(dim // 2, dtype=jnp.float32) / (dim // 2)))
    ang = pos.astype(jnp.float32)[:, None] * omega[None, :]
    return jnp.concatenate([jnp.sin(ang), jnp.cos(ang)], -1)


def _grid_pos_embed(n_tok):
    rows = n_tok // GRID_W
    half = D_MODEL // 2
    er = _sincos(jnp.arange(rows), half)
    ec = _sincos(jnp.arange(GRID_W), half)
    emb = jnp.concatenate([jnp.broadcast_to(er[:, None, :], (rows, GRID_W, half)),
                           jnp.broadcast_to(ec[None, :, :], (rows, GRID_W, half))], -1)
    return emb.reshape(rows * GRID_W, D_MODEL)


def _modulation(cvec, w_mod, b_mod):
    m = jax.nn.silu(cvec) @ w_mod + b_mod
    return jnp.split(m[:, None, :], 6, axis=-1)


def _fourier_mix(a):
    B, S, _ = a.shape
    ag = a.astype(jnp.float32).reshape(B, S, FNET_GROUPS, FNET_GW)
    y = jnp.fft.fft2(ag, axes=(1, 3), norm='ortho').real
    return y.reshape(B, S, FNET_W).astype(a.dtype)


def _short_conv_mix(bg, cg, xin, w, bias):
    S = xin.shape[1]
    zp = jnp.pad(cg * xin, ((0, 0), (1, 1), (0, 0)))
    y = zp[:, 0:S] * w[0] + zp[:, 1:S + 1] * w[1] + zp[:, 2:S + 2] * w[2] + bias
    return bg * y


def _linear_scan(a, b, h0):
    def comb(l, r):
        return (l[0] * r[0], r[0] * l[1] + r[1])
    a_cum, b_cum = lax.associative_scan(comb, (a, b), axis=1)
    h = a_cum * h0[:, None, :] + b_cum
    return h, h[:, -1]


def _rglru_dir(xs, conv_w, conv_b, wa, ba, wx, bx, lam, h0):
    f32 = jnp.float32
    B, S, _ = xs.shape
    xs = xs.astype(f32)
    xp = jnp.pad(xs, ((0, 0), (LRU_CONV_K - 1, 0), (0, 0)))
    xc = conv_b.astype(f32) + xp[:, 0:S] * conv_w[0].astype(f32)
    for j in range(1, LRU_CONV_K):
        xc = xc + xp[:, j:j + S] * conv_w[j].astype(f32)
    xh = xc.reshape(B, S, LRU_HEADS, LRU_HD)
    gate_r = jax.nn.sigmoid(jnp.einsum('bshi,hij->bshj', xh, wa.astype(f32)).reshape(B, S, LRU_W) + ba.astype(f32))
    gate_i = jax.nn.sigmoid(jnp.einsum('bshi,hij->bshj', xh, wx.astype(f32)).reshape(B, S, LRU_W) + bx.astype(f32))
    log_a = -LRU_C * gate_r * jax.nn.softplus(-lam.astype(f32))
    a = jnp.exp(log_a)
    b = jnp.sqrt(-jnp.expm1(2.0 * log_a)) * (gate_i * xc)
    return _linear_scan(a, b, h0)


def _lru_mix(xb, gb, conv_w, conv_b, wa, ba, wx, bx, lam, h0):
    y = 0.0
    finals = []
    for d in range(2):
        xs = xb if d == 0 else xb[:, ::-1]
        h, hf = _rglru_dir(xs, conv_w[d], conv_b[d], wa[d], ba[d], wx[d], bx[d], lam[d],
                           h0[:, d].astype(jnp.float32))
        y = y + (h if d == 0 else h[:, ::-1])
        finals.append(hf)
    y = y * jax.nn.gelu(gb.astype(jnp.float32))
    return y, jnp.stack(finals, 1)


def _wkv_scan(r, w, k, v, kk, a, s0):
    def step(state, inp):
        r_t, w_t, k_t, v_t, kk_t, a_t = inp
        sk = jnp.einsum('bhij,bhj->bhi', state, kk_t)
        state = (state * w_t[:, :, None, :]
                 - sk[..., None] * (kk_t * a_t)[:, :, None, :]
                 + v_t[..., None] * k_t[:, :, None, :])
        return state, jnp.einsum('bhij,bhj->bhi', state, r_t)
    xs = tuple(jnp.moveaxis(t, 1, 0) for t in (r, w, k, v, kk, a))
    s_fin, ys = lax.scan(step, s0, xs)
    return jnp.moveaxis(ys, 0, 1), s_fin


def _rwkv_mix(z, mu, w0, w2, a0, a2, k_k, k_a, r_k, g2, gn_g, gn_b, s0):
    f32 = jnp.float32
    B, S, _ = z.shape
    z = z.astype(f32)
    zp = jnp.pad(z, ((0, 0), (1, 1), (0, 0)))
    z = z + mu.astype(f32) * (0.5 * (zp[:, 0:S] + zp[:, 2:S + 2]) - z)
    o1 = WKV_W
    o2 = o1 + WKV_W
    o3 = o2 + WKV_W
    o4 = o3 + 2 * DECAY_RANK
    o5 = o4 + 2 * ICLR_RANK
    r, k, v = z[..., 0:o1], z[..., o1:o2], z[..., o2:o3]
    wd, ad, gd = z[..., o3:o4], z[..., o4:o5], z[..., o5:]
    g = jax.nn.sigmoid(gd) @ g2.astype(f32)
    heads = lambda t: t.reshape(B, S, WKV_H, WKV_N)
    r_h, v_h = heads(r), heads(v)
    y_sum = 0.0
    bonus = 0.0
    finals = []
    for d in range(2):
        wz = w0[d] + jnp.tanh(wd[..., d * DECAY_RANK:(d + 1) * DECAY_RANK]) @ w2[d].astype(f32)
        decay = heads(jnp.exp(-DECAY_SCALE * jax.nn.sigmoid(wz)))
        iclr = jax.nn.sigmoid(a0[d] + ad[..., d * ICLR_RANK:(d + 1) * ICLR_RANK] @ a2[d].astype(f32))
        kk = heads(k * k_k[d])
        kk = kk * lax.rsqrt(jnp.maximum(jnp.sum(kk * kk, -1, keepdims=True), 1e-24))
        km = heads(k * (1.0 + (iclr - 1.0) * k_a[d]))
        seqs = (r_h, decay, km, v_h, kk, heads(iclr))
        if d == 1:
            seqs = tuple(t[:, ::-1] for t in seqs)
        y, s_fin = _wkv_scan(*seqs, s0[:, d].astype(f32))
        y_sum = y_sum + (y if d == 0 else y[:, ::-1])
        bonus = bonus + jnp.sum(r_h * km * r_k.astype(f32), -1, keepdims=True) * v_h
        finals.append(s_fin)
    m = y_sum.mean(-1, keepdims=True)
    var = jnp.square(y_sum - m).mean(-1, keepdims=True)
    yn = ((y_sum - m) * lax.rsqrt(var + WKV_GN_EPS)).reshape(B, S, WKV_W) * gn_g + gn_b
    out = (yn + bonus.reshape(B, S, WKV_W)) * g
    return out, jnp.stack(finals, 1)


def _peer(h, wq, keys, u_tab, v_tab):
    B, S, D = h.shape
    T = B * S
    ht = h.reshape(T, D)
    q = (ht @ wq).reshape(T, PEER_HEADS, 2, PEER_HALF)
    s = jnp.einsum('thpc,hpkc->thpk', q, keys).astype(jnp.float32)
    v1, i1 = lax.top_k(s[:, :, 0], PEER_TOPK)
    v2, i2 = lax.top_k(s[:, :, 1], PEER_TOPK)
    cand = (v1[..., :, None] + v2[..., None, :]).reshape(T, PEER_HEADS, PEER_TOPK * PEER_TOPK)
    sc, ci = lax.top_k(cand, PEER_TOPK)
    e1 = jnp.take_along_axis(i1, ci // PEER_TOPK, -1)
    e2 = jnp.take_along_axis(i2, ci % PEER_TOPK, -1)
    idx = e1 * N_KEYS + e2
    gate = jax.nn.softmax(sc, axis=-1).astype(h.dtype)
    nb = T // PEER_BLOCK

    def block(args):
        hb, ib, gb = args
        act = jax.nn.gelu(jnp.einsum('td,thkd->thk', hb, u_tab[ib]))
        return jnp.einsum('thk,thkd->td', gb * act, v_tab[ib])

    out = lax.map(block, (ht.reshape(nb, PEER_BLOCK, D),
                          idx.reshape(nb, PEER_BLOCK, PEER_HEADS, PEER_TOPK),
                          gate.reshape(nb, PEER_BLOCK, PEER_HEADS, PEER_TOPK)))
    return out.reshape(B, S, D)


def _trunk(x, cvec, lru_state, wkv_state, P):
    lru_out = []
    wkv_out = []
    for l in range(DEPTH):
        sh1, sc1, g1, sh2, sc2, g2 = _modulation(cvec, P['w_mod'][l], P['b_mod'][l])
        h = _layer_norm(x) * (1.0 + sc1) + sh1
        j = l // 2
        if l % 2 == 0:
            z = h @ P['w_in_e'][j]
            a_in = z[..., 0:FNET_W]
            b_g = z[..., FNET_W:FNET_W + CONV_W]
            c_g = z[..., FNET_W + CONV_W:FNET_W + 2 * CONV_W]
            x_in = z[..., FNET_W + 2 * CONV_W:]
            y = jnp.concatenate([_fourier_mix(a_in),
                                 _short_conv_mix(b_g, c_g, x_in, P['sconv_w'][j], P['sconv_b'][j])], -1)
            mix = y.astype(x.dtype) @ P['w_out_e'][j]
        else:
            z = h @ P['w_in_o'][j]
            y_lru, h_fin = _lru_mix(z[..., 0:LRU_W], z[..., LRU_W:2 * LRU_W],
                                    P['lru_conv_w'][j], P['lru_conv_b'][j], P['lru_wa'][j], P['lru_ba'][j],
                                    P['lru_wx'][j], P['lru_bx'][j], P['lru_lambda'][j], lru_state[:, j])
            y_wkv, s_fin = _rwkv_mix(z[..., 2 * LRU_W:], P['wkv_mu'][j], P['wkv_w0'][j], P['wkv_w2'][j],
                                     P['wkv_a0'][j], P['wkv_a2'][j], P['wkv_kk'][j], P['wkv_ka'][j],
                                     P['wkv_rk'][j], P['wkv_g2'][j], P['wkv_gn_g'][j], P['wkv_gn_b'][j],
                                     wkv_state[:, j])
            mix = jnp.concatenate([y_lru, y_wkv], -1).astype(x.dtype) @ P['w_out_o'][j]
            lru_out.append(h_fin)
            wkv_out.append(s_fin)
        x = _layer_norm(ALPHA * x + g1 * mix, P['ln1_g'][l], P['ln1_b'][l])
        h2 = _layer_norm(x) * (1.0 + sc2) + sh2
        ffn = _peer(h2, P['peer_wq'][l], P['peer_keys'][l], P['peer_u'][l], P['peer_v'][l])
        x = _layer_norm(ALPHA * x + g2 * ffn, P['ln2_g'][l], P['ln2_b'][l])
    return x, jnp.stack(lru_out, 1), jnp.stack(wkv_out, 1)


def setup_inputs(seed: int = 0) -> dict:
    key = jax.random.key(seed)
    ks = list(jax.random.split(key, 48))
    f32 = jnp.float32

    def nrm(i, shape, scale):
        return jax.random.normal(ks[i], shape, f32) * scale

    a_init = 0.9 + 0.099 * jax.random.uniform(ks[40], (N_ODD, 2, LRU_W), f32)
    return {
        'x_prompt': nrm(0, (BATCH, SEQ, D_MODEL), 1.0),
        'x_sample': nrm(1, (DEC_BATCH, DEC_SEQ, D_MODEL), 1.0),
        'state_lru': nrm(2, (DEC_BATCH, N_ODD, 2, LRU_W), 0.5),
        'state_wkv': nrm(3, (DEC_BATCH, N_ODD, 2, WKV_H, WKV_N, WKV_N), 0.5),
        'c': nrm(4, (DEC_BATCH, D_MODEL), 1.0),
        'c_ctx': nrm(5, (D_MODEL,), 1.0),
        'w_mod': nrm(6, (DEPTH, D_MODEL, 6 * D_MODEL), 0.2 * D_MODEL ** -0.5),
        'b_mod': nrm(7, (DEPTH, 6 * D_MODEL), 0.01),
        'ln1_g': 1.0 + nrm(8, (DEPTH, D_MODEL), 0.01),
        'ln1_b': nrm(9, (DEPTH, D_MODEL), 0.01),
        'ln2_g': 1.0 + nrm(10, (DEPTH, D_MODEL), 0.01),
        'ln2_b': nrm(11, (DEPTH, D_MODEL), 0.01),
        'w_in_e': nrm(12, (N_EVEN, D_MODEL, EVEN_IN), D_MODEL ** -0.5),
        'w_out_e': nrm(13, (N_EVEN, EVEN_MIX, D_MODEL), BETA * EVEN_MIX ** -0.5),
        'sconv_w': nrm(14, (N_EVEN, CONV_K, CONV_W), CONV_K ** -0.5),
        'sconv_b': nrm(15, (N_EVEN, CONV_W), 0.01),
        'w_in_o': nrm(16, (N_ODD, D_MODEL, ODD_IN), D_MODEL ** -0.5),
        'w_out_o': nrm(17, (N_ODD, ODD_MIX, D_MODEL), BETA * ODD_MIX ** -0.5),
        'lru_conv_w': nrm(18, (N_ODD, 2, LRU_CONV_K, LRU_W), 0.5),
        'lru_conv_b': nrm(19, (N_ODD, 2, LRU_W), 0.01),
        'lru_wa': nrm(20, (N_ODD, 2, LRU_HEADS, LRU_HD, LRU_HD), LRU_HD ** -0.5),
        'lru_ba': nrm(21, (N_ODD, 2, LRU_W), 0.01),
        'lru_wx': nrm(22, (N_ODD, 2, LRU_HEADS, LRU_HD, LRU_HD), LRU_HD ** -0.5),
        'lru_bx': nrm(23, (N_ODD, 2, LRU_W), 0.01),
        'lru_lambda': jnp.log(a_init) - jnp.log1p(-a_init),
        'wkv_mu': jax.random.uniform(ks[24], (N_ODD, WKV_IN), f32),
        'wkv_w0': nrm(25, (N_ODD, 2, WKV_W), 0.5),
        'wkv_w2': nrm(26, (N_ODD, 2, DECAY_RANK, WKV_W), 0.1),
        'wkv_a0': nrm(27, (N_ODD, 2, WKV_W), 0.5),
        'wkv_a2': nrm(28, (N_ODD, 2, ICLR_RANK, WKV_W), 0.1),
        'wkv_kk': 1.0 + nrm(29, (N_ODD, 2, WKV_W), 0.1),
        'wkv_ka': 1.0 + nrm(30, (N_ODD, 2, WKV_W), 0.1),
        'wkv_rk': nrm(31, (N_ODD, WKV_H, WKV_N), 0.1),
        'wkv_g2': nrm(32, (N_ODD, GATE_RANK, WKV_W), GATE_RANK ** -0.5),
        'wkv_gn_g': 1.0 + nrm(33, (N_ODD, WKV_W), 0.01),
        'wkv_gn_b': nrm(34, (N_ODD, WKV_W), 0.01),
        'peer_wq': nrm(35, (DEPTH, D_MODEL, PEER_HEADS * PEER_DK), D_MODEL ** -0.5),
        'peer_keys': nrm(36, (DEPTH, PEER_HEADS, 2, N_KEYS, PEER_HALF), PEER_HALF ** -0.5),
        'peer_u': nrm(37, (DEPTH, N_EXPERTS, D_MODEL), D_MODEL ** -0.5),
        'peer_v': nrm(38, (DEPTH, N_EXPERTS, D_MODEL), BETA),
    }


def reference(x_prompt, x_sample, state_lru, state_wkv, c, c_ctx, w_mod, b_mod, ln1_g, ln1_b, ln2_g, ln2_b,
              w_in_e, w_out_e, sconv_w, sconv_b, w_in_o, w_out_o, lru_conv_w, lru_conv_b, lru_wa, lru_ba,
              lru_wx, lru_bx, lru_lambda, wkv_mu, wkv_w0, wkv_w2, wkv_a0, wkv_a2, wkv_kk, wkv_ka, wkv_rk,
              wkv_g2, wkv_gn_g, wkv_gn_b, peer_wq, peer_keys, peer_u, peer_v):
    P = dict(w_mod=w_mod, b_mod=b_mod, ln1_g=ln1_g, ln1_b=ln1_b, ln2_g=ln2_g, ln2_b=ln2_b,
             w_in_e=w_in_e, w_out_e=w_out_e, sconv_w=sconv_w, sconv_b=sconv_b,
             w_in_o=w_in_o, w_out_o=w_out_o, lru_conv_w=lru_conv_w, lru_conv_b=lru_conv_b,
             lru_wa=lru_wa, lru_ba=lru_ba, lru_wx=lru_wx, lru_bx=lru_bx, lru_lambda=lru_lambda,
             wkv_mu=wkv_mu, wkv_w0=wkv_w0, wkv_w2=wkv_w2, wkv_a0=wkv_a0, wkv_a2=wkv_a2,
             wkv_kk=wkv_kk, wkv_ka=wkv_ka, wkv_rk=wkv_rk, wkv_g2=wkv_g2, wkv_gn_g=wkv_gn_g,
             wkv_gn_b=wkv_gn_b, peer_wq=peer_wq, peer_keys=peer_keys, peer_u=peer_u, peer_v=peer_v)

    bp = x_prompt.shape[0]
    lru0 = jnp.zeros((bp, N_ODD, 2, LRU_W), jnp.float32)
    wkv0 = jnp.zeros((bp, N_ODD, 2, WKV_H, WKV_N, WKV_N), jnp.float32)
    y_prompt, lru_new, wkv_new = _trunk(x_prompt, c_ctx[None, :], lru0, wkv0, P)

    xs = x_sample + _grid_pos_embed(x_sample.shape[1]).astype(x_sample.dtype)[None]
    y_sample, _, _ = _trunk(xs, c, state_lru, state_wkv, P)

    return (y_prompt, y_sample, lru_new.astype(x_prompt.dtype), wkv_new.astype(x_prompt.dtype))
```

```python
import math
import os
from contextlib import ExitStack

import numpy as np
import concourse.bass as bass
import concourse.mybir as mybir
from concourse.bass_utils import run_bass_kernel_spmd

F32 = mybir.dt.float32
BF16 = mybir.dt.bfloat16
U32 = mybir.dt.uint32
I32 = mybir.dt.int32
AF = mybir.ActivationFunctionType
ALU = mybir.AluOpType
AX = mybir.AxisListType

D = 1024
NS = 4096
NP = 256
NT = NS + 2 * NP
TT = 256
NTILES = NT // TT
SEQS = [(0, NS), (NS, NS + NP), (NS + NP, NT)]
DEPTH = 4
ALPHA = (2 * DEPTH) ** 0.25
LN_EPS = 1e-6
N_KEYS = 128
DECAY_SCALE = math.exp(-0.5)
WKV_GN_EPS = 64e-5
EVEN_IN = 2048
ODD_IN = 2944

KDEPTH = int(os.environ.get("KDEPTH", "4"))
KDEBUG = int(os.environ.get("KDEBUG", "0"))
KSKIP_PEER = int(os.environ.get("KSKIP_PEER", "0"))
KSTART = int(os.environ.get("KSTART", "0"))
KODD = int(os.environ.get("KODD", "15"))
_DBG = {}


class Key:
    __slots__ = ("w", "r", "dsem", "dcnt", "name")

    def __init__(self, name):
        self.name = name
        self.w = {}
        self.r = {}
        self.dsem = None
        self.dcnt = 0


class KB:
    def __init__(self):
        self.nc = bass.Bass("TRN2", target_bir_lowering=False)
        self.es = ExitStack()
        nc = self.nc
        self.eng = {"pe": nc.tensor, "act": nc.scalar, "dve": nc.vector, "pool": nc.gpsimd, "sp": nc.sync}
        self.sem = {}
        self.cnt = {}
        self.sems = {}
        for e in ("pe", "act", "dve", "pool"):
            s = self.es.enter_context(nc.semaphore("c_" + e))
            self.sem[e] = s
            self.cnt[e] = 0
            self.sems[id(s)] = s
        self.waited = {e: {} for e in self.eng}
        self.keys = {}
        self.ninst = 0
        self.out_events = []
        self.phase = None
        self.loop_dma = None
        self.depth = 0
        self.lsems = []
        self.ldma_pool = []

    def key(self, name):
        k = self.keys.get(name)
        if k is None:
            k = Key(name)
            self.keys[name] = k
        return k

    def sb(self, name, shape, dt=F32, glob=False):
        st = self.es if (glob or self.phase is None) else self.phase
        self.uid = getattr(self, "uid", 0) + 1
        return st.enter_context(self.nc.sbuf_tensor("%s_u%d" % (name, self.uid), shape, dt))

    def _waits(self, e, reads, writes):
        need = {}
        for kn in reads:
            for s, v in self.key(kn).w.items():
                if need.get(s, 0) < v:
                    need[s] = v
        for kn in writes:
            k = self.key(kn)
            for s, v in k.w.items():
                if need.get(s, 0) < v:
                    need[s] = v
            for s, v in k.r.items():
                if need.get(s, 0) < v:
                    need[s] = v
        wd = self.waited[e]
        own = id(self.sem[e]) if e in self.sem else None
        for s, v in need.items():
            if e == "pe" and s == own:
                continue
            if wd.get(s, 0) < v:
                self.eng[e].wait_ge(self.sems[s], v)
                wd[s] = v

    def op(self, e, fn, reads=(), writes=()):
        self._waits(e, reads, writes)
        ins = fn()
        self.cnt[e] += 1
        c = self.cnt[e]
        ins.then_inc(self.sem[e], 1)
        s = id(self.sem[e])
        for kn in writes:
            k = self.key(kn)
            k.w = {s: c}
            k.r = {}
        for kn in reads:
            k = self.key(kn)
            if k.r.get(s, 0) < c:
                k.r[s] = c
        self.ninst += 1
        return ins

    def dma(self, q, out, in_, reads=(), writes=(), semkey=None, is_output=False, **kw):
        self._waits(q, reads, writes)
        kn = semkey if semkey is not None else (writes[0] if writes else reads[0])
        if self.loop_dma is None:
            k = self.key(kn)
            if k.dsem is None:
                k.dsem = self.es.enter_context(self.nc.semaphore("d_%d" % len(self.sems)))
                self.sems[id(k.dsem)] = k.dsem
            ins = self.eng[q].dma_start(out=out, in_=in_, **kw)
            k.dcnt += 16
            ins.then_inc(k.dsem, 16)
            s = id(k.dsem)
            dc = k.dcnt
            if is_output:
                self.out_events.append((s, dc))
        else:
            ent = self.loop_dma.get(kn)
            if ent is None:
                pool = self.ldma_pool[self.depth - 1]
                idx = len(self.loop_dma)
                if idx >= len(pool):
                    sm = self.es.enter_context(self.nc.semaphore("ld%d_%d" % (self.depth, idx)))
                    self.sems[id(sm)] = sm
                    pool.append(sm)
                ent = [pool[idx], 0]
                self.loop_dma[kn] = ent
            ins = self.eng[q].dma_start(out=out, in_=in_, **kw)
            ent[1] += 16
            ins.then_inc(ent[0], 16)
            s = id(ent[0])
            dc = ent[1]
        for wn in writes:
            w = self.key(wn)
            w.w = {s: dc}
            w.r = {}
        for rn in reads:
            r = self.key(rn)
            if r.r.get(s, 0) < dc:
                r.r[s] = dc
        self.ninst += 1
        return ins

    def _scope_targets(self):
        tgt = {}
        for e, s in self.sem.items():
            if self.cnt[e] > 0:
                tgt[id(s)] = self.cnt[e]
        if self.loop_dma is None:
            for k in self.keys.values():
                if k.dsem is not None and k.dcnt > 0:
                    tgt[id(k.dsem)] = k.dcnt
        else:
            for (sm, c) in self.loop_dma.values():
                if c > 0:
                    tgt[id(sm)] = c
        return tgt

    def barrier(self):
        tgt = self._scope_targets()
        for e in self.eng:
            wd = self.waited[e]
            for s, v in tgt.items():
                if wd.get(s, 0) < v:
                    self.eng[e].wait_ge(self.sems[s], v)
                    wd[s] = v
        self.nc.all_engine_barrier()
        for k in self.keys.values():
            k.w = {}
            k.r = {}

    def loop(self, n):
        return _Loop(self, n)

    def begin_phase(self):
        self.phase = ExitStack()

    def end_phase(self):
        self.barrier()
        self.phase.close()
        self.phase = None

    def finish(self):
        self.barrier()


class _Loop:
    def __init__(self, kb, n):
        self.kb = kb
        self.n = n

    def __enter__(self):
        kb = self.kb
        kb.barrier()
        self.saved = (kb.sem, kb.cnt, kb.waited, kb.loop_dma, kb.depth)
        d = kb.depth + 1
        kb.depth = d
        while len(kb.lsems) < d:
            lv = {}
            for e in ("pe", "act", "dve", "pool"):
                sm = kb.es.enter_context(kb.nc.semaphore("l%d_%s" % (len(kb.lsems), e)))
                kb.sems[id(sm)] = sm
                lv[e] = sm
            kb.lsems.append(lv)
            kb.ldma_pool.append([])
        kb.sem = kb.lsems[d - 1]
        kb.cnt = {e: 0 for e in kb.sem}
        kb.waited = {e: {} for e in kb.eng}
        kb.loop_dma = {}
        self.cm = kb.nc.Fori(0, self.n)
        return self.cm.__enter__()

    def __exit__(self, *a):
        kb = self.kb
        kb.barrier()
        for sm in list(kb.sem.values()) + [v[0] for v in kb.loop_dma.values()]:
            kb.nc.gpsimd.sem_clear(sm)
        kb.nc.all_engine_barrier()
        r = self.cm.__exit__(*a)
        kb.sem, kb.cnt, kb.waited, kb.loop_dma, kb.depth = self.saved
        for k in kb.keys.values():
            k.w = {}
            k.r = {}
        return r


class Pack:
    def __init__(self):
        self.cols = []
        self.off = {}
        self.n = 0

    def add(self, name, arr):
        a = np.asarray(arr, np.float32)
        C = a.shape[-1]
        lead = a.shape[:-1]
        cols = a.reshape(-1, 128).T
        self.off[name] = (self.n, lead, C // 128)
        self.cols.append(cols)
        self.n += cols.shape[1]

    def col(self, name, *idx):
        base, lead, nt = self.off[name]
        flat = 0
        for i, d in zip(idx[:-1], lead):
            flat = flat * d + i
        return base + flat * nt + idx[-1]

    def array(self):
        return np.ascontiguousarray(np.concatenate(self.cols, axis=1))


def _dft_tables(S):
    s = np.arange(S, dtype=np.int64)
    m = (s[:, None] * s[None, :]) % S
    ang = 2.0 * np.pi * m.astype(np.float64) / S
    sc = 1.0 / math.sqrt(S * 128.0)
    return (np.cos(ang) * sc).astype(np.float32), (-np.sin(ang) * sc).astype(np.float32)


def _sincos(pos, dim):
    omega = 1.0 / (10000.0 ** (np.arange(dim // 2, dtype=np.float32) / (dim // 2)))
    ang = pos.astype(np.float32)[:, None] * omega[None, :]
    return np.concatenate([np.sin(ang), np.cos(ang)], -1).astype(np.float32)


def _grid_pos_embed(n_tok):
    rows = n_tok // 64
    half = D // 2
    er = _sincos(np.arange(rows), half)
    ec = _sincos(np.arange(64), half)
    emb = np.concatenate([np.broadcast_to(er[:, None, :], (rows, 64, half)),
                          np.broadcast_to(ec[None, :, :], (rows, 64, half))], -1)
    return emb.reshape(rows * 64, D)


def build(pack):
    kb = KB()
    nc = kb.nc
    ins = {}

    def din(name, shape, dt=F32):
        ins[name] = nc.dram_tensor(name, list(shape), dt, kind="ExternalInput").ap()
        return ins[name]

    def dout(name, shape, dt=F32):
        return nc.dram_tensor(name, list(shape), dt, kind="ExternalOutput").ap()

    def dscr(name, shape, dt=F32):
        ap = nc.dram_tensor(name, list(shape), dt, kind="Internal").ap()
        scr_list.append((name, ap, list(shape), dt))
        return ap

    scr_list = []
    xT_in = din("xT", [D, NT])
    posT = din("posT", [D, NS])
    cv_in = din("cv", [128, 8, 2])
    pv_in = din("pvec", [128, pack.n])
    ident_in = din("ident", [128, 128])
    iota_in = din("iota", [128, 128])
    w_mod = din("w_mod", [DEPTH, D, 6 * D])
    w_in_e = din("w_in_e", [2, D, EVEN_IN])
    w_out_e = din("w_out_e", [2, D, D])
    w_in_o = din("w_in_o", [2, D, ODD_IN])
    w_out_o = din("w_out_o", [2, D, D])
    peer_wq = din("peer_wq", [DEPTH, D, 2048])
    keysT = din("keysT", [DEPTH, 128, 16, 128])
    NE2 = 1 if KSKIP_PEER else 128
    peer_uv = din("peer_uv", [DEPTH, NE2, 128, 2048])
    dftc = din("dftc", [128, 256])
    EVEN_NEEDED = any(l_ % 2 == 0 for l_ in range(KSTART, KDEPTH))
    dftS = din("dftS", [2, NS, NS] if EVEN_NEEDED else [2, 128, 128])
    dftP = din("dftP", [2, NP, NP])
    lru_wbd = din("lru_wbd", [2, 2, 2, 4, 128, 128])
    din("wkv_w2", [2, 2, 64, 512])
    din("wkv_a2", [2, 2, 64, 512])
    din("wkv_g2", [2, 128, 512])
    din("wkv0", [2, 2, 8, 64, 64])
    cmask_in = din("cmask", [128, 2, 768])
    bones_in = din("bones", [128, 128])

    yT_out = dout("yT", [D, NT])
    lru_out = dout("lru_new", [128, 32])
    wkv_out = dout("wkv_new", [2, 2, 2, 8, 64, 64])

    x_dram = dscr("x_scr", [D, NT])
    z_dram = dscr("z_scr", [ODD_IN, NT])
    y_dram = dscr("y_scr", [D, NT], BF16)
    h2_dram = dscr("h2_scr", [D, NT], BF16)
    sel_dram = nc.dram_tensor("sel_scr", [128, 3, NT], F32, kind="Internal").ap()
    uv_bf = nc.dram_tensor("uv_bf", [DEPTH, 128, 128, 2048], BF16, kind="Internal").ap()
    dftS_bf = nc.dram_tensor("dftS_bf", [2, NS, NS], BF16, kind="Internal").ap()
    scr = {"wk": nc.dram_tensor("wk_scr", [28 * 128, NT], F32, kind="Internal").ap(),
           "ywk": nc.dram_tensor("ywk_scr", [2, 512, NT], F32, kind="Internal").ap()}

    ps = [kb.es.enter_context(nc.psum_tensor("ps%d" % i, [128, 512], F32)) for i in range(8)]
    PS = ["ps%d" % i for i in range(8)]
    pv = kb.sb("pv", [128, pack.n])
    ident = kb.sb("ident", [128, 128])
    identb = kb.sb("identb", [128, 128], BF16)
    iota = kb.sb("iota", [128, 128])
    ones_s = kb.sb("ones_s", [128, 128])
    zer_b = kb.sb("zer_b", [128, 512], BF16)
    modT = kb.sb("modT", [128, DEPTH, 48, 2])
    scv = kb.sb("scv", [128, 8, 2])

    lruo = kb.sb("lruo", [128, 32])
    cmask = kb.sb("cmask", [128, 2, 768])
    bones = kb.sb("bones", [128, 128])
    dv = kb.sb("dv", [128, 96])
    DV = {"cneg": 0, "mu1": 16, "muh": 46, "ka1": 76, "one": 92}
    kb.op("dve", lambda: nc.vector.memset(lruo[:], 0.0), writes=["lruo"])
    kb.dma("sp", cmask[:], cmask_in, writes=["mask4", "maskL"])
    kb.dma("sp", bones[:], bones_in, writes=["bones"])
    kb.dma("sp", pv[:], pv_in, writes=["pv"])
    kb.dma("sp", ident[:], ident_in, writes=["ident"])
    kb.dma("sp", iota[:], iota_in, writes=["iota"])
    kb.dma("sp", scv[:], cv_in, writes=["scv"])
    kb.op("dve", lambda: nc.vector.memset(ones_s[:], 1.0 / D), writes=["ones_s"])
    kb.op("dve", lambda: nc.vector.memset(zer_b[:], 0.0), writes=["zer_b"])
    kb.op("dve", lambda: nc.vector.tensor_copy(out=identb[:], in_=ident[:]), reads=["ident"], writes=["identb"])
    kb.op("act", lambda: nc.scalar.activation(out=scv[:], in_=scv[:], func=AF.Silu), reads=["scv"], writes=["scv"])

    def pcol(name, *idx):
        c = pack.col(name, *idx)
        return pv[:, c:c + 1]

    def cast_uv(l_):
        for q in range(4):
            kb.dma("pool", uv_bf[l_, q * 32:(q + 1) * 32].rearrange("a b c -> (a b) c"),
                   peer_uv[l_, q * 32:(q + 1) * 32].rearrange("a b c -> (a b) c"), semkey="castu", max_dma_last_dim=4096)

    if not KSKIP_PEER:
        cast_uv(KSTART)
    for t in range(2 if EVEN_NEEDED else 0):
        for q in range(8):
            kb.dma("pool", dftS_bf[t, q * 512:(q + 1) * 512, :], dftS[t, q * 512:(q + 1) * 512, :],
                   semkey="castd", max_dma_last_dim=4096)

    kb.begin_phase()
    wm = [kb.sb("wm%d" % i, [128, 8, 512]) for i in range(2)]
    n = 0
    for l in range(KDEPTH):
        for g in range(12):
            wt = wm[n % 2]
            wk = "wm%d" % (n % 2)
            kb.dma("sp", wt[:], w_mod[l, :, g * 512:(g + 1) * 512].rearrange("(k p) n -> p k n", p=128), writes=[wk])
            pb = n % 2
            for oc in range(4):
                for k in range(8):
                    kb.op("pe", lambda: nc.tensor.matmul(ps[pb][:, oc * 2:oc * 2 + 2], lhsT=wt[:, k, oc * 128:(oc + 1) * 128],
                                                         rhs=scv[:, k, :], start=(k == 0), stop=(k == 7)),
                          reads=[wk, "scv"], writes=[PS[pb]])
            for oc in range(4):
                occ = g * 4 + oc
                bc = pack.col("b_mod", l, occ, 0)
                add1 = 1.0 if (8 <= occ < 16 or 32 <= occ < 40) else 0.0
                kb.op("dve", lambda: nc.vector.tensor_scalar(out=modT[:, l, occ, :], in0=ps[pb][:, oc * 2:oc * 2 + 2],
                                                             scalar1=pv[:, bc:bc + 1], scalar2=add1, op0=ALU.add, op1=ALU.add),
                      reads=[PS[pb], "pv"], writes=["modT"])
            n += 1
    kb.end_phase()

    def mcol(l, oc, v):
        return modT[:, l, oc, v:v + 1]

    def ln_normalize(xt, xk, sq, sqk, xn, xnk, tmp, tmpk, pA, pB):
        kb.op("act", lambda: nc.scalar.activation(out=sq[:], in_=xt[:], func=AF.Square), reads=[xk], writes=[sqk])
        for k in range(8):
            kb.op("pe", lambda: nc.tensor.matmul(ps[pA][:, 0:TT], lhsT=ones_s[:], rhs=xt[:, k, :], start=(k == 0), stop=(k == 7)),
                  reads=["ones_s", xk], writes=[PS[pA]])
        for k in range(8):
            kb.op("pe", lambda: nc.tensor.matmul(ps[pB][:, 0:TT], lhsT=ones_s[:], rhs=sq[:, k, :], start=(k == 0), stop=(k == 7)),
                  reads=["ones_s", sqk], writes=[PS[pB]])
        mean = tmp[:, 0, :]
        var = tmp[:, 1, :]
        rstd = tmp[:, 2, :]
        nmr = tmp[:, 3, :]
        kb.op("act", lambda: nc.scalar.activation(out=mean, in_=ps[pA][:, 0:TT], func=AF.Copy), reads=[PS[pA]], writes=[tmpk])
        kb.op("dve", lambda: nc.vector.tensor_tensor(out=var, in0=mean, in1=mean, op=ALU.mult), reads=[tmpk], writes=[tmpk])
        kb.op("dve", lambda: nc.vector.tensor_tensor(out=var, in0=ps[pB][:, 0:TT], in1=var, op=ALU.subtract), reads=[PS[pB], tmpk], writes=[tmpk])
        kb.op("act", lambda: nc.scalar.activation(out=var, in_=var, func=AF.Sqrt, bias=pv_eps[:, 0:1], scale=1.0), reads=[tmpk, "pv_eps"], writes=[tmpk])
        kb.op("dve", lambda: nc.vector.reciprocal(out=rstd, in_=var), reads=[tmpk], writes=[tmpk])
        kb.op("dve", lambda: nc.vector.scalar_tensor_tensor(out=nmr, in0=mean, scalar=-1.0, in1=rstd, op0=ALU.mult, op1=ALU.mult),
              reads=[tmpk], writes=[tmpk])
        kb.op("dve", lambda: nc.vector.tensor_tensor(out=xn[:], in0=xt[:], in1=rstd.unsqueeze(1).to_broadcast([128, 8, TT]), op=ALU.mult),
              reads=[xk, tmpk], writes=[xnk])
        kb.op("dve", lambda: nc.vector.tensor_tensor(out=xn[:], in0=xn[:], in1=nmr.unsqueeze(1).to_broadcast([128, 8, TT]), op=ALU.add),
              reads=[xnk, tmpk], writes=[xnk])

    pv_eps = kb.sb("pv_eps", [128, 4])
    kb.op("dve", lambda: nc.vector.memset(pv_eps[:, 0:1], LN_EPS), writes=["pv_eps"])
    kb.op("dve", lambda: nc.vector.memset(pv_eps[:, 1:2], 1.0), writes=["pv_eps"])
    kb.op("dve", lambda: nc.vector.memset(pv_eps[:, 2:3], WKV_GN_EPS), writes=["pv_eps"])
    kb.op("dve", lambda: nc.vector.memset(dv[:, 92:96], 1.0), writes=["dv"])
    lam0 = pack.col("lru_lambda", 0, 0, 0)
    kb.op("act", lambda: nc.scalar.activation(out=dv[:, 0:16], in_=pv[:, lam0:lam0 + 16], func=AF.Exp, scale=-1.0), reads=["pv", "dv"], writes=["dv"])
    kb.op("act", lambda: nc.scalar.activation(out=dv[:, 0:16], in_=dv[:, 0:16], func=AF.Ln, bias=pv_eps[:, 1:2], scale=1.0), reads=["dv", "pv_eps"], writes=["dv"])
    kb.op("dve", lambda: nc.vector.tensor_scalar(out=dv[:, 0:16], in0=dv[:, 0:16], scalar1=-8.0, scalar2=None, op0=ALU.mult), reads=["dv"], writes=["dv"])
    mu0 = pack.col("wkv_mu", 0, 0)
    kb.op("dve", lambda: nc.vector.tensor_scalar(out=dv[:, 16:46], in0=pv[:, mu0:mu0 + 30], scalar1=-1.0, scalar2=1.0, op0=ALU.mult, op1=ALU.add), reads=["pv", "dv"], writes=["dv"])
    kb.op("dve", lambda: nc.vector.tensor_scalar(out=dv[:, 46:76], in0=pv[:, mu0:mu0 + 30], scalar1=0.5, scalar2=None, op0=ALU.mult), reads=["pv", "dv"], writes=["dv"])
    ka0 = pack.col("wkv_ka", 0, 0, 0)
    kb.op("dve", lambda: nc.vector.tensor_scalar(out=dv[:, 76:92], in0=pv[:, ka0:ka0 + 16], scalar1=-1.0, scalar2=1.0, op0=ALU.mult, op1=ALU.add), reads=["pv", "dv"], writes=["dv"])
    cst = {"mask4": cmask[:, :, 0:512], "maskL": cmask[:, :, 512:640], "bones": bones, "eps": pv_eps}

    def load_w_bf16(wt, wk, src, ncols):
        for k in range(8):
            kb.dma("pool", wt[:, k, 0:ncols], src[k * 128:(k + 1) * 128, :], writes=[wk], max_dma_last_dim=4096)

    evac_rr = [0]

    def evac(out_ap, in_ap, reads, writes):
        evac_rr[0] += 1
        if evac_rr[0] % 2:
            kb.op("act", lambda: nc.scalar.activation(out=out_ap, in_=in_ap, func=AF.Copy), reads=reads, writes=writes)
        else:
            kb.op("dve", lambda: nc.vector.tensor_copy(out=out_ap, in_=in_ap), reads=reads, writes=writes)

    for l in range(KSTART, KDEPTH):
        j = l // 2
        even = (l % 2 == 0)
        nin = EVEN_IN if even else ODD_IN
        noc = nin // 128
        x_src = xT_in if l == KSTART else x_dram

        kb.begin_phase()
        win = kb.sb("win", [128, 8, ODD_IN], BF16)
        load_w_bf16(win, "win", (w_in_e if even else w_in_o)[j], nin)
        xa = [kb.sb("A_x%d" % i, [128, 8, TT]) for i in range(2)]
        pa = [kb.sb("A_p%d" % i, [128, 8, TT]) for i in range(2)]
        sqa = kb.sb("A_sq", [128, 8, TT])
        xna = kb.sb("A_xn", [128, 8, TT])
        tmpa = kb.sb("A_tmp", [128, 4, TT])
        ha = [kb.sb("A_h%d" % i, [128, 8, TT], BF16) for i in range(2)]
        zst = [kb.sb("A_z%d" % i, [128, 4, TT]) for i in range(2)]
        zn = 0
        for t in range(NTILES):
            t0 = t * TT
            v = 0 if t < NS // TT else 1
            xt = xa[t % 2]
            xk = "A_x%d" % (t % 2)
            kb.dma("sp", xt[:], x_src[:, t0:t0 + TT].rearrange("(k p) t -> p k t", p=128), writes=[xk])
            if l == 0 and v == 0 and KSTART == 0:
                pt = pa[t % 2]
                pk = "A_p%d" % (t % 2)
                kb.dma("sp", pt[:], posT[:, t0:t0 + TT].rearrange("(k p) t -> p k t", p=128), writes=[pk])
                kb.op("dve", lambda: nc.vector.tensor_tensor(out=xt[:], in0=xt[:], in1=pt[:], op=ALU.add), reads=[xk, pk], writes=[xk])
            if l == KSTART:
                kb.dma("pool", x_dram[:, t0:t0 + TT].rearrange("(k p) t -> p k t", p=128), xt[:], reads=[xk], semkey="A_xst%d" % (t % 2))
            ln_normalize(xt, xk, sqa, "A_sq", xna, "A_xn", tmpa, "A_tmp", 6, 7)
            h = ha[t % 2]
            hk = "A_h%d" % (t % 2)
            for k in range(8):
                kb.op("act", lambda: nc.scalar.activation(out=h[:, k, :], in_=xna[:, k, :], func=AF.Identity,
                                                          scale=mcol(l, 8 + k, v), bias=mcol(l, k, v)),
                      reads=["A_xn", "modT"], writes=[hk])
            for g0 in range(0, noc, 4):
                gn = min(4, noc - g0)
                zs = zst[zn % 2]
                zk = "A_z%d" % (zn % 2)
                for gi in range(gn):
                    oc = g0 + gi
                    pb = (oc // 2) % 4
                    half = oc % 2
                    for k in range(8):
                        kb.op("pe", lambda: nc.tensor.matmul(ps[pb][:, half * TT:(half + 1) * TT], lhsT=win[:, k, oc * 128:(oc + 1) * 128],
                                                             rhs=h[:, k, :], start=(k == 0), stop=(k == 7)),
                              reads=["win", hk], writes=[PS[pb]])
                    evac(zs[:, gi, :], ps[pb][:, half * TT:(half + 1) * TT], [PS[pb]], [zk])
                kb.dma("pool", z_dram[g0 * 128:(g0 + gn) * 128, t0:t0 + TT].rearrange("(g p) t -> p g t", p=128), zs[:, 0:gn, :],
                       reads=[zk], semkey="A_zst%d" % (zn % 2))
                zn += 1
        kb.end_phase()

        if even:
            phase_b_even(kb, nc, ps, PS, pack, pv, j, z_dram, y_dram, dftc, dftS_bf, dftP, evac)
        else:
            phase_b_odd(kb, nc, ps, PS, pack, pv, dv, DV, j, z_dram, y_dram, ins, scr, lruo, wkv_out, evac, ident, cst)

        kb.begin_phase()
        wout = kb.sb("wout", [128, 8, D], BF16)
        load_w_bf16(wout, "wout", (w_out_e if even else w_out_o)[j], D)
        yc = [kb.sb("C_y%d" % i, [128, 8, TT], BF16) for i in range(2)]
        xc = [kb.sb("C_x%d" % i, [128, 8, TT]) for i in range(2)]
        x1p = kb.sb("C_x1p", [128, 8, TT])
        sqc = kb.sb("C_sq", [128, 8, TT])
        xnc = kb.sb("C_xn", [128, 8, TT])
        x1s = [kb.sb("C_x1_%d" % i, [128, 8, TT]) for i in range(2)]
        tmpc = kb.sb("C_tmp", [128, 4, TT])
        h2s = [kb.sb("C_h2_%d" % i, [128, 8, TT], BF16) for i in range(2)]
        for t in range(NTILES):
            t0 = t * TT
            v = 0 if t < NS // TT else 1
            yt = yc[t % 2]
            yk = "C_y%d" % (t % 2)
            xt = xc[t % 2]
            xk = "C_x%d" % (t % 2)
            x1 = x1s[t % 2]
            x1k = "C_x1_%d" % (t % 2)
            h2 = h2s[t % 2]
            h2k = "C_h2_%d" % (t % 2)
            kb.dma("sp", yt[:], y_dram[:, t0:t0 + TT].rearrange("(k p) t -> p k t", p=128), writes=[yk])
            kb.dma("sp", xt[:], x_dram[:, t0:t0 + TT].rearrange("(k p) t -> p k t", p=128), writes=[xk])
            kb.op("act", lambda: nc.scalar.activation(out=xt[:], in_=xt[:], func=AF.Copy, scale=ALPHA), reads=[xk], writes=[xk])
            for oc in range(8):
                pb = 4 + (oc // 2) % 2
                half = oc % 2
                for k in range(8):
                    kb.op("pe", lambda: nc.tensor.matmul(ps[pb][:, half * TT:(half + 1) * TT], lhsT=wout[:, k, oc * 128:(oc + 1) * 128],
                                                         rhs=yt[:, k, :], start=(k == 0), stop=(k == 7)),
                          reads=["wout", yk], writes=[PS[pb]])
                kb.op("dve", lambda: nc.vector.scalar_tensor_tensor(out=x1p[:, oc, :], in0=ps[pb][:, half * TT:(half + 1) * TT],
                                                                    scalar=mcol(l, 16 + oc, v), in1=xt[:, oc, :], op0=ALU.mult, op1=ALU.add),
                      reads=[PS[pb], "modT", xk], writes=["C_x1p"])
            ln_normalize(x1p, "C_x1p", sqc, "C_sq", xnc, "C_xn", tmpc, "C_tmp", 6, 7)
            for k in range(8):
                kb.op("act", lambda: nc.scalar.activation(out=x1[:, k, :], in_=xnc[:, k, :], func=AF.Identity,
                                                          scale=pcol("ln1_g", l, k), bias=pcol("ln1_b", l, k)),
                      reads=["C_xn", "pv"], writes=[x1k])
            kb.dma("pool", x_dram[:, t0:t0 + TT].rearrange("(k p) t -> p k t", p=128), x1[:], reads=[x1k], semkey="C_x1st%d" % (t % 2))
            ln_normalize(x1, x1k, sqc, "C_sq", xnc, "C_xn", tmpc, "C_tmp", 6, 7)
            for k in range(8):
                kb.op("act", lambda: nc.scalar.activation(out=h2[:, k, :], in_=xnc[:, k, :], func=AF.Identity,
                                                          scale=mcol(l, 32 + k, v), bias=mcol(l, 24 + k, v)),
                      reads=["C_xn", "modT"], writes=[h2k])
            kb.dma("pool", h2_dram[:, t0:t0 + TT].rearrange("(k p) t -> p k t", p=128), h2[:], reads=[h2k], semkey="C_h2st%d" % (t % 2))
        kb.end_phase()

        if not KSKIP_PEER:
            peer_select(kb, nc, ps, PS, l, peer_wq, keysT, h2_dram, sel_dram, ident, iota, evac, load_w_bf16)

        kb.begin_phase()
        h2b = [kb.sb("P_h2_%d" % i, [128, 8, TT], BF16) for i in range(2)]
        x1b = [kb.sb("P_x1_%d" % i, [128, 8, TT]) for i in range(2)]
        sqc = kb.sb("P_sq", [128, 8, TT])
        xnc = kb.sb("P_xn", [128, 8, TT])
        tmpc = kb.sb("P_tmp", [128, 4, TT])
        xo = kb.sb("P_xo", [128, 8, TT])
        pst = peer_alloc(kb, nc) if not KSKIP_PEER else None
        if not KSKIP_PEER and l + 1 < KDEPTH:
            cast_uv(l + 1)
        for t in range(NTILES):
            t0 = t * TT
            v = 0 if t < NS // TT else 1
            h2 = h2b[t % 2]
            h2k = "P_h2_%d" % (t % 2)
            x1 = x1b[t % 2]
            x1k = "P_x1_%d" % (t % 2)
            kb.dma("sp", x1[:], x_dram[:, t0:t0 + TT].rearrange("(k p) t -> p k t", p=128), writes=[x1k])
            kb.op("act", lambda: nc.scalar.activation(out=x1[:], in_=x1[:], func=AF.Copy, scale=ALPHA), reads=[x1k], writes=[x1k])
            if not KSKIP_PEER:
                kb.dma("sp", h2[:], h2_dram[:, t0:t0 + TT].rearrange("(k p) t -> p k t", p=128), writes=[h2k])
                peer_mix(kb, nc, ps, PS, pst, l, t, h2, h2k, uv_bf, sel_dram, iota, zer_b, evac)
                for oc in range(8):
                    pb = oc // 2
                    half = oc % 2
                    kb.op("dve", lambda: nc.vector.scalar_tensor_tensor(out=x1[:, oc, :], in0=ps[pb][:, half * TT:(half + 1) * TT],
                                                                        scalar=mcol(l, 40 + oc, v), in1=x1[:, oc, :], op0=ALU.mult, op1=ALU.add),
                          reads=[PS[pb], "modT", x1k], writes=[x1k])
            ln_normalize(x1, x1k, sqc, "P_sq", xnc, "P_xn", tmpc, "P_tmp", 6, 7)
            for k in range(8):
                kb.op("act", lambda: nc.scalar.activation(out=xo[:, k, :], in_=xnc[:, k, :], func=AF.Identity,
                                                          scale=pcol("ln2_g", l, k), bias=pcol("ln2_b", l, k)),
                      reads=["P_xn", "pv"], writes=["P_xo"])
            last = (l == KDEPTH - 1)
            dst = yT_out if last else x_dram
            kb.dma("sp", dst[:, t0:t0 + TT].rearrange("(k p) t -> p k t", p=128), xo[:], reads=["P_xo"],
                   semkey="P_xst", is_output=last)
        kb.end_phase()

    kb.dma("pool", lru_out, lruo[:], reads=["lruo"], semkey="lruost", is_output=True)
    if KDEBUG:
        kb.barrier()
        for (name, ap, shape, dt) in scr_list:
            d = nc.dram_tensor("dbg_" + name, [shape[0], 512], dt, kind="ExternalOutput").ap()
            kb.dma("sp", d[:, 0:256], ap[:, 0:256], semkey="dbgtap", is_output=True)
            kb.dma("sp", d[:, 256:512], ap[:, NS:NS + 256], semkey="dbgtap", is_output=True)
    kb.finish()
    return kb


def _dbg_tap(kb, nc, name, tile, key, dt):
    shp = list(tile.shape)
    d = nc.dram_tensor(name, shp, dt, kind="ExternalOutput").ap()
    kb.dma("pool", d, tile[:], reads=[key], semkey=name, is_output=True)


def phase_b_even(kb, nc, ps, PS, pack, pv, j, z_dram, y_dram, dftc, dftS_bf, dftP, evac):
    NTB = NT // 128
    pqstack = ExitStack()
    pq = pqstack.enter_context(nc.sbuf_tensor("E_pq_j%d" % j, [128, 4, NTB, 256], BF16))
    ccb = pqstack.enter_context(nc.sbuf_tensor("E_ccb_j%d" % j, [128, 256], BF16))
    tpb = pqstack.enter_context(nc.sbuf_tensor("E_tpb_j%d" % j, [128, 2, 2, NP], BF16))
    kb.begin_phase()
    cc32 = kb.sb("E_cc32", [128, 256])
    kb.dma("sp", cc32[:], dftc, writes=["E_cc32"])
    kb.op("dve", lambda: nc.vector.tensor_copy(out=ccb[:], in_=cc32[:]), reads=["E_cc32"], writes=["E_ccb"])
    tp32 = kb.sb("E_tp32", [128, 2, 2, NP])
    kb.dma("sp", tp32[:], dftP.rearrange("t (b p) s -> p t b s", p=128), writes=["E_tp32"])
    kb.op("dve", lambda: nc.vector.tensor_copy(out=tpb[:], in_=tp32[:]), reads=["E_tp32"], writes=["E_tpb"])
    a32 = [kb.sb("E_a32_%d" % i, [128, NT]) for i in range(2)]
    ab = [kb.sb("E_ab_%d" % i, [128, NT], BF16) for i in range(2)]
    for g in range(4):
        at = a32[g % 2]
        ak = "E_a32_%d" % (g % 2)
        bt = ab[g % 2]
        bk = "E_ab_%d" % (g % 2)
        kb.dma("sp", at[:], z_dram[g * 128:(g + 1) * 128, :], writes=[ak])
        kb.op("act", lambda: nc.scalar.activation(out=bt[:], in_=at[:], func=AF.Copy), reads=[ak], writes=[bk])
        for tb in range(NTB):
            pb = tb % 4
            kb.op("pe", lambda: nc.tensor.matmul(ps[pb][:, 0:256], lhsT=bt[:, tb * 128:(tb + 1) * 128], rhs=ccb[:], start=True, stop=True),
                  reads=[bk, "E_ccb"], writes=[PS[pb]])
            evac(pq[:, g, tb, :], ps[pb][:, 0:256], [PS[pb]], ["E_pq%d" % g])
    kb.end_phase()
    kb.begin_phase()
    tab = [kb.sb("E_tab%d" % i, [128, 2, 32, 256], BF16) for i in range(2)]
    yst = [kb.sb("E_y%d" % i, [128, 4, 256], BF16) for i in range(2)]
    for sb_ in range(NS // 256):
        tt_ = tab[sb_ % 2]
        tk = "E_tab%d" % (sb_ % 2)
        for tbl in range(2):
            kb.dma("sp", tt_[:, tbl, :, :], dftS_bf[tbl, :, sb_ * 256:(sb_ + 1) * 256].rearrange("(b p) s -> p b s", p=128), writes=[tk])
        ys = yst[sb_ % 2]
        ysk = "E_y%d" % (sb_ % 2)
        for g in range(4):
            pb = 4 + g % 2
            for tb in range(32):
                for tbl in range(2):
                    kb.op("pe", lambda: nc.tensor.matmul(ps[pb][:, 0:256], lhsT=pq[:, g, tb, tbl * 128:(tbl + 1) * 128], rhs=tt_[:, tbl, tb, :],
                                                         start=(tb == 0 and tbl == 0), stop=(tb == 31 and tbl == 1)),
                          reads=["E_pq%d" % g, tk], writes=[PS[pb]])
            evac(ys[:, g, :], ps[pb][:, 0:256], [PS[pb]], [ysk])
        kb.dma("pool", y_dram[0:512, sb_ * 256:(sb_ + 1) * 256].rearrange("(g p) t -> p g t", p=128), ys[:], reads=[ysk], semkey="E_yst%d" % (sb_ % 2))
    for pi in range(2):
        ys = yst[pi % 2]
        ysk = "E_y%d" % (pi % 2)
        for g in range(4):
            pb = 4 + g % 2
            for tb in range(2):
                for tbl in range(2):
                    kb.op("pe", lambda: nc.tensor.matmul(ps[pb][:, 0:256], lhsT=pq[:, g, 32 + 2 * pi + tb, tbl * 128:(tbl + 1) * 128],
                                                         rhs=tpb[:, tbl, tb, :], start=(tb == 0 and tbl == 0), stop=(tb == 1 and tbl == 1)),
                          reads=["E_pq%d" % g, "E_tpb"], writes=[PS[pb]])
            evac(ys[:, g, :], ps[pb][:, 0:256], [PS[pb]], [ysk])
        c0 = NS + pi * NP
        kb.dma("pool", y_dram[0:512, c0:c0 + NP].rearrange("(g p) t -> p g t", p=128), ys[:], reads=[ysk], semkey="E_yst%d" % (pi % 2))
    kb.end_phase()
    pqstack.close()

    kb.begin_phase()
    bg = kb.sb("S_bg", [128, NT])
    cg = kb.sb("S_cg", [128, NT])
    xi = kb.sb("S_xi", [128, NT])
    yy = kb.sb("S_y", [128, NT])
    yb = kb.sb("S_yb", [128, NT], BF16)
    for ct in range(4):
        kb.dma("sp", bg[:], z_dram[512 + ct * 128:512 + (ct + 1) * 128, :], writes=["S_bg"])
        kb.dma("sp", cg[:], z_dram[1024 + ct * 128:1024 + (ct + 1) * 128, :], writes=["S_cg"])
        kb.dma("sp", xi[:], z_dram[1536 + ct * 128:1536 + (ct + 1) * 128, :], writes=["S_xi"])
        kb.op("dve", lambda: nc.vector.tensor_tensor(out=xi[:], in0=xi[:], in1=cg[:], op=ALU.mult), reads=["S_xi", "S_cg"], writes=["S_xi"])
        w0 = pack.col("sconv_w", j, 0, ct)
        w1 = pack.col("sconv_w", j, 1, ct)
        w2 = pack.col("sconv_w", j, 2, ct)
        bb = pack.col("sconv_b", j, ct)
        kb.op("act", lambda: nc.scalar.activation(out=yy[:], in_=xi[:], func=AF.Identity, scale=pv[:, w1:w1 + 1], bias=pv[:, bb:bb + 1]),
              reads=["S_xi", "pv"], writes=["S_y"])
        for (s0, s1) in SEQS:
            kb.op("dve", lambda: nc.vector.scalar_tensor_tensor(out=yy[:, s0 + 1:s1], in0=xi[:, s0:s1 - 1], scalar=pv[:, w0:w0 + 1],
                                                                in1=yy[:, s0 + 1:s1], op0=ALU.mult, op1=ALU.add),
                  reads=["S_xi", "S_y", "pv"], writes=["S_y"])
            kb.op("dve", lambda: nc.vector.scalar_tensor_tensor(out=yy[:, s0:s1 - 1], in0=xi[:, s0 + 1:s1], scalar=pv[:, w2:w2 + 1],
                                                                in1=yy[:, s0:s1 - 1], op0=ALU.mult, op1=ALU.add),
                  reads=["S_xi", "S_y", "pv"], writes=["S_y"])
        kb.op("dve", lambda: nc.vector.tensor_tensor(out=yb[:], in0=yy[:], in1=bg[:], op=ALU.mult), reads=["S_y", "S_bg"], writes=["S_yb"])
        kb.dma("pool", y_dram[512 + ct * 128:512 + (ct + 1) * 128, :], yb[:], reads=["S_yb"], semkey="S_yst")
    kb.end_phase()


def phase_b_odd(kb, nc, ps, PS, pack, pv, dv, DV, j, z_dram, y_dram, ins, scr, lruo, wkv_out, evac, ident, cst):
    if KODD & 1:
        lru_part(kb, nc, ps, PS, pack, pv, dv, DV, j, z_dram, y_dram, ins["lru_wbd"], lruo)
    if KODD & 2:
        rwkv_prep(kb, nc, ps, PS, pack, pv, dv, DV, j, z_dram, ins, scr, evac)
    if KODD & 4:
        rwkv_scan(kb, nc, ps, PS, pack, pv, dv, DV, j, z_dram, ins, scr, wkv_out, evac, ident, cst)
    if KODD & 8:
        rwkv_post(kb, nc, ps, PS, pack, pv, j, y_dram, scr, evac, ident, cst)


def lru_part(kb, nc, ps, PS, pack, pv, dv, DV, j, z_dram, y_dram, lru_wbd, lruo):
    kb.begin_phase()
    w32 = kb.sb("L_w32", [128, 16, 128])
    wb = kb.sb("L_wb", [128, 16, 128], BF16)
    kb.dma("sp", w32[:], lru_wbd[j].rearrange("d a c i o -> i (d a c) o"), writes=["L_w32"])
    kb.op("dve", lambda: nc.vector.tensor_copy(out=wb[:], in_=w32[:]), reads=["L_w32"], writes=["L_wb"])
    xb = kb.sb("L_xb", [128, NT])
    gb = kb.sb("L_gb", [128, NT])
    xc = kb.sb("L_xc", [128, NT])
    xcb = kb.sb("L_xcb", [128, NT], BF16)
    T2 = kb.sb("L_T2", [128, NT])
    T3 = kb.sb("L_T3", [128, NT])
    T4 = kb.sb("L_T4", [128, NT])
    ys = kb.sb("L_ys", [128, NT])
    yb = kb.sb("L_yb", [128, NT], BF16)
    for ct in range(4):
        kb.dma("sp", xb[:], z_dram[ct * 128:(ct + 1) * 128, :], writes=["L_xb"])
        kb.dma("sp", gb[:], z_dram[512 + ct * 128:512 + (ct + 1) * 128, :], writes=["L_gb"])
        for d in range(2):
            cw = [pack.col("lru_conv_w", j, d, tap, ct) for tap in range(4)]
            cb = pack.col("lru_conv_b", j, d, ct)
            kb.op("act", lambda: nc.scalar.activation(out=xc[:], in_=xb[:], func=AF.Identity, scale=pv[:, cw[3]:cw[3] + 1], bias=pv[:, cb:cb + 1]),
                  reads=["L_xb", "pv"], writes=["L_xc"])
            for tap in range(3):
                sh = 3 - tap
                for (s0, s1) in SEQS:
                    if d == 0:
                        o_, i_ = xc[:, s0 + sh:s1], xb[:, s0:s1 - sh]
                    else:
                        o_, i_ = xc[:, s0:s1 - sh], xb[:, s0 + sh:s1]
                    kb.op("dve", lambda: nc.vector.scalar_tensor_tensor(out=o_, in0=i_, scalar=pv[:, cw[tap]:cw[tap] + 1], in1=o_, op0=ALU.mult, op1=ALU.add),
                          reads=["L_xb", "L_xc", "pv"], writes=["L_xc"])
            kb.op("act", lambda: nc.scalar.activation(out=xcb[:], in_=xc[:], func=AF.Copy), reads=["L_xc"], writes=["L_xcb"])
            ba = pack.col("lru_ba", j, d, ct)
            bx = pack.col("lru_bx", j, d, ct)
            for blk in range(NT // 512):
                c0 = blk * 512
                for ai, (dst, dk, bcol) in enumerate(((T2, "L_T2", ba), (T3, "L_T3", bx))):
                    pb = (2 * blk + ai) % 4
                    kb.op("pe", lambda: nc.tensor.matmul(ps[pb][:, 0:512], lhsT=wb[:, (d * 2 + ai) * 4 + ct, :], rhs=xcb[:, c0:c0 + 512], start=True, stop=True),
                          reads=["L_wb", "L_xcb"], writes=[PS[pb]])
                    kb.op("act", lambda: nc.scalar.activation(out=dst[:, c0:c0 + 512], in_=ps[pb][:, 0:512], func=AF.Sigmoid, bias=pv[:, bcol:bcol + 1], scale=1.0),
                          reads=[PS[pb], "pv"], writes=[dk])
            cn = DV["cneg"] + (j * 2 + d) * 4 + ct
            kb.op("act", lambda: nc.scalar.activation(out=T2[:], in_=T2[:], func=AF.Exp, scale=dv[:, cn:cn + 1]), reads=["L_T2", "dv"], writes=["L_T2"])
            kb.op("dve", lambda: nc.vector.tensor_tensor(out=T4[:], in0=T2[:], in1=T2[:], op=ALU.mult), reads=["L_T2"], writes=["L_T4"])
            kb.op("act", lambda: nc.scalar.activation(out=T4[:], in_=T4[:], func=AF.Sqrt, scale=-1.0, bias=dv[:, DV["one"]:DV["one"] + 1]),
                  reads=["L_T4", "dv"], writes=["L_T4"])
            kb.op("dve", lambda: nc.vector.tensor_tensor(out=T3[:], in0=T3[:], in1=xc[:], op=ALU.mult), reads=["L_T3", "L_xc"], writes=["L_T3"])
            kb.op("dve", lambda: nc.vector.tensor_tensor(out=T3[:], in0=T3[:], in1=T4[:], op=ALU.mult), reads=["L_T3", "L_T4"], writes=["L_T3"])
            for si, (s0, s1) in enumerate(SEQS):
                if si == 0:
                    hc = pack.col("state_lru", j, d, ct)
                    init = pv[:, hc:hc + 1]
                else:
                    init = 0.0
                if d == 0:
                    o_, a_, b_ = T4[:, s0:s1], T2[:, s0:s1], T3[:, s0:s1]
                else:
                    o_, a_, b_ = T4[:, s0:s1][:, ::-1], T2[:, s0:s1][:, ::-1], T3[:, s0:s1][:, ::-1]
                kb.op("dve", lambda: nc.vector.tensor_tensor_scan(out=o_, data0=a_, data1=b_, initial=init, op0=ALU.mult, op1=ALU.add),
                      reads=["L_T2", "L_T3", "pv"], writes=["L_T4"])
                if si > 0:
                    col = (si - 1) * 16 + j * 8 + d * 4 + ct
                    fin = T4[:, s1 - 1:s1] if d == 0 else T4[:, s0:s0 + 1]
                    kb.op("act", lambda: nc.scalar.activation(out=lruo[:, col:col + 1], in_=fin, func=AF.Copy), reads=["L_T4"], writes=["lruo"])
            if d == 0:
                kb.op("act", lambda: nc.scalar.activation(out=ys[:], in_=T4[:], func=AF.Copy), reads=["L_T4"], writes=["L_ys"])
            else:
                kb.op("dve", lambda: nc.vector.tensor_tensor(out=ys[:], in0=ys[:], in1=T4[:], op=ALU.add), reads=["L_ys", "L_T4"], writes=["L_ys"])
        kb.op("act", lambda: nc.scalar.activation(out=gb[:], in_=gb[:], func=AF.Gelu_apprx_tanh), reads=["L_gb"], writes=["L_gb"])
        kb.op("dve", lambda: nc.vector.tensor_tensor(out=yb[:], in0=ys[:], in1=gb[:], op=ALU.mult), reads=["L_ys", "L_gb"], writes=["L_yb"])
        kb.dma("pool", y_dram[ct * 128:(ct + 1) * 128, :], yb[:], reads=["L_yb"], semkey="L_yst")
    kb.end_phase()


WK0 = 1024
PADW = NT + 6
PSEQ = [(1, 0, NS), (NS + 3, NS, NS + NP), (NS + NP + 5, NS + NP, NT)]


def rwkv_prep(kb, nc, ps, PS, pack, pv, dv, DV, j, z_dram, ins, scr, evac):
    wk = scr["wk"]
    kb.begin_phase()
    zp = [kb.sb("W_zp%d" % i, [128, PADW]) for i in range(2)]
    sm = kb.sb("W_sm", [128, PADW])
    zo = [kb.sb("W_zo%d" % i, [128, PADW]) for i in range(2)]
    for i in range(2):
        kb.op("dve", lambda: nc.vector.memset(zp[i][:], 0.0), writes=["W_zp%d" % i])
    for q in range(15):
        z_ = zp[q % 2]
        zk = "W_zp%d" % (q % 2)
        o_ = zo[q % 2]
        ok = "W_zo%d" % (q % 2)
        r0 = WK0 + q * 128
        for (p0, s0, s1) in PSEQ:
            kb.dma("sp", z_[:, p0:p0 + (s1 - s0)], z_dram[r0:r0 + 128, s0:s1], writes=[zk])
        kb.op("dve", lambda: nc.vector.tensor_tensor(out=sm[:, 0:PADW - 2], in0=z_[:, 0:PADW - 2], in1=z_[:, 2:PADW], op=ALU.add), reads=[zk], writes=["W_sm"])
        m1 = DV["mu1"] + j * 15 + q
        mh = DV["muh"] + j * 15 + q
        kb.op("act", lambda: nc.scalar.activation(out=o_[:, 0:PADW - 2], in_=z_[:, 1:PADW - 1], func=AF.Copy, scale=dv[:, m1:m1 + 1]),
              reads=[zk, "dv"], writes=[ok])
        kb.op("dve", lambda: nc.vector.scalar_tensor_tensor(out=o_[:, 0:PADW - 2], in0=sm[:, 0:PADW - 2], scalar=dv[:, mh:mh + 1], in1=o_[:, 0:PADW - 2],
                                                            op0=ALU.mult, op1=ALU.add), reads=["W_sm", ok, "dv"], writes=[ok])
        for (p0, s0, s1) in PSEQ:
            kb.dma("pool", z_dram[r0:r0 + 128, s0:s1], o_[:, p0 - 1:p0 - 1 + (s1 - s0)], reads=[ok], semkey="W_zst%d" % (q % 2))
    kb.end_phase()

    kb.begin_phase()
    w32 = kb.sb("W_w32", [128, 3, 512])
    wlr = kb.sb("W_wlr", [128, 3, 512], BF16)
    kb.dma("sp", w32[:, 0, :], ins["wkv_w2"][j].rearrange("d r c -> (d r) c"), writes=["W_w32"])
    kb.dma("sp", w32[:, 1, :], ins["wkv_a2"][j].rearrange("d r c -> (d r) c"), writes=["W_w32"])
    kb.dma("sp", w32[:, 2, :], ins["wkv_g2"][j], writes=["W_w32"])
    kb.op("dve", lambda: nc.vector.tensor_copy(out=wlr[:], in_=w32[:]), reads=["W_w32"], writes=["W_wlr"])
    lr32 = kb.sb("W_lr32", [128, NT])
    lrb = [kb.sb("W_lrb%d" % i, [128, NT], BF16) for i in range(3)]
    for i, (fn, r0) in enumerate(((AF.Tanh, WK0 + 1536), (AF.Copy, WK0 + 1664), (AF.Sigmoid, WK0 + 1792))):
        kb.dma("sp", lr32[:], z_dram[r0:r0 + 128, :], writes=["W_lr32"])
        kb.op("act", lambda: nc.scalar.activation(out=lrb[i][:], in_=lr32[:], func=fn), reads=["W_lr32"], writes=["W_lrb%d" % i])
    ot = [kb.sb("W_ot%d" % i, [128, NT]) for i in range(2)]
    n = 0
    for kind in range(3):
        for d in range(2 if kind < 2 else 1):
            for ct in range(4):
                o_ = ot[n % 2]
                ok = "W_ot%d" % (n % 2)
                for blk in range(NT // 512):
                    c0 = blk * 512
                    pb = blk % 4
                    if kind < 2:
                        lhs = wlr[d * 64:(d + 1) * 64, kind, ct * 128:(ct + 1) * 128]
                        rhs = lrb[kind][d * 64:(d + 1) * 64, c0:c0 + 512]
                    else:
                        lhs = wlr[:, 2, ct * 128:(ct + 1) * 128]
                        rhs = lrb[2][:, c0:c0 + 512]
                    kb.op("pe", lambda: nc.tensor.matmul(ps[pb][:, 0:512], lhsT=lhs, rhs=rhs, start=True, stop=True),
                          reads=["W_wlr", "W_lrb%d" % kind], writes=[PS[pb]])
                    if kind == 0:
                        bc = pack.col("wkv_w0", j, d, ct)
                        kb.op("act", lambda: nc.scalar.activation(out=o_[:, c0:c0 + 512], in_=ps[pb][:, 0:512], func=AF.Sigmoid, bias=pv[:, bc:bc + 1], scale=1.0),
                              reads=[PS[pb], "pv"], writes=[ok])
                    elif kind == 1:
                        bc = pack.col("wkv_a0", j, d, ct)
                        kb.op("act", lambda: nc.scalar.activation(out=o_[:, c0:c0 + 512], in_=ps[pb][:, 0:512], func=AF.Sigmoid, bias=pv[:, bc:bc + 1], scale=1.0),
                              reads=[PS[pb], "pv"], writes=[ok])
                    else:
                        evac(o_[:, c0:c0 + 512], ps[pb][:, 0:512], [PS[pb]], [ok])
                if kind == 0:
                    kb.op("dve", lambda: nc.vector.tensor_scalar(out=o_[:], in0=o_[:], scalar1=-DECAY_SCALE, scalar2=None, op0=ALU.mult), reads=[ok], writes=[ok])
                    row = (d * 4 + ct) * 128
                elif kind == 1:
                    row = (8 + d * 4 + ct) * 128
                else:
                    row = (16 + ct) * 128
                kb.dma("pool", wk[row:row + 128, :], o_[:], reads=[ok], semkey="W_ost%d" % (n % 2))
                n += 1
    kb.end_phase()


SEGS = [(0, 2048), (2048, 4096), (4096, 4608)]
CH = 128


def rwkv_scan(kb, nc, ps, PS, pack, pv, dv, DV, j, z_dram, ins, scr, wkv_out, evac, ident, cst):
    wk = scr["wk"]
    ywk = scr["ywk"]
    kb.begin_phase()
    SW = 2048
    T = [kb.sb("R_T%d" % i, [128, SW]) for i in range(10)]
    TK = ["R_T%d" % i for i in range(10)]
    gC = kb.sb("R_gC", [128, SW // CH])
    maskf = kb.sb("R_maskf", [128, SW])
    maskb = kb.sb("R_maskb", [128, SW])
    kb.op("dve", lambda: nc.vector.memset(maskf[:], 1.0), writes=["R_maskf"])
    kb.op("dve", lambda: nc.vector.memset(maskf[:].rearrange("p (c t) -> p c t", t=CH)[:, :, 0:1], 0.0), writes=["R_maskf"])
    kb.op("dve", lambda: nc.vector.memset(maskb[:], 1.0), writes=["R_maskb"])
    kb.op("dve", lambda: nc.vector.memset(maskb[:].rearrange("p (c t) -> p c t", t=CH)[:, :, CH - 1:CH], 0.0), writes=["R_maskb"])
    Mst = kb.sb("R_M", [128, 3, 64])
    tok = kb.sb("R_tok", [128, 4, 128])
    XT = [kb.sb("R_XT%d" % i, [128, 512]) for i in range(2)]
    Lm = [kb.sb("R_L%d" % i, [128, 128]) for i in range(2)]
    LT = [kb.sb("R_LT%d" % i, [128, 128]) for i in range(2)]
    X = [kb.sb("R_X%d" % i, [128, 128]) for i in range(2)]
    PT = kb.sb("R_PT", [128, 64])
    Qt = kb.sb("R_Q", [128, 64])
    GT = kb.sb("R_GT", [128, 128])
    ytok = [kb.sb("R_y%d" % i, [128, 128]) for i in range(2)]
    ytT = [kb.sb("R_yT%d" % i, [128, 128]) for i in range(2)]
    mask4 = cst["mask4"]
    maskL = cst["maskL"]
    bones = cst["bones"]
    yn = 0
    for ct in range(4):
        for d in range(2):
            kkc = pack.col("wkv_kk", j, d, ct)
            kac = pack.col("wkv_ka", j, d, ct)
            ka1 = DV["ka1"] + (j * 2 + d) * 4 + ct
            rkc = pack.col("wkv_rk", j, ct)
            for hh in range(2):
                kb.dma("sp", Mst[hh * 64:(hh + 1) * 64, 0, :], ins["wkv0"][j, d, 2 * ct + hh], writes=["R_M%d" % hh])
                kb.op("dve", lambda: nc.vector.memset(Mst[hh * 64:(hh + 1) * 64, 1:3, :], 0.0), writes=["R_M%d" % hh])
            segs = SEGS if d == 0 else [SEGS[1], SEGS[0], SEGS[2]]
            for (g0, g1) in segs:
                W = g1 - g0
                nch = W // CH
                v_ = lambda i: T[i][:, 0:W]
                kb.dma("sp", v_(0), z_dram[WK0 + 512 + ct * 128:WK0 + 512 + (ct + 1) * 128, g0:g1], writes=[TK[0]])
                kb.dma("sp", v_(1), z_dram[WK0 + ct * 128:WK0 + (ct + 1) * 128, g0:g1], writes=[TK[1]])
                kb.dma("sp", v_(2), z_dram[WK0 + 1024 + ct * 128:WK0 + 1024 + (ct + 1) * 128, g0:g1], writes=[TK[2]])
                kb.dma("sp", v_(3), wk[(d * 4 + ct) * 128:(d * 4 + ct + 1) * 128, g0:g1], writes=[TK[3]])
                kb.dma("sp", v_(4), wk[(8 + d * 4 + ct) * 128:(8 + d * 4 + ct + 1) * 128, g0:g1], writes=[TK[4]])
                kb.op("act", lambda: nc.scalar.activation(out=v_(5), in_=v_(0), func=AF.Copy, scale=pv[:, kkc:kkc + 1]), reads=[TK[0], "pv"], writes=[TK[5]])
                kb.op("dve", lambda: nc.vector.tensor_tensor(out=v_(6), in0=v_(5), in1=v_(5), op=ALU.mult), reads=[TK[5]], writes=[TK[6]])
                for blk in range(W // 512):
                    c0 = blk * 512
                    pb = blk % 4
                    kb.op("pe", lambda: nc.tensor.matmul(ps[pb][:, 0:512], lhsT=bones[:], rhs=T[6][:, c0:c0 + 512], start=True, stop=True),
                          reads=["bones", TK[6]], writes=[PS[pb]])
                    kb.op("dve", lambda: nc.vector.tensor_scalar(out=T[7][:, c0:c0 + 512], in0=ps[pb][:, 0:512], scalar1=1e-24, scalar2=None, op0=ALU.max),
                          reads=[PS[pb]], writes=[TK[7]])
                kb.op("act", lambda: nc.scalar.activation(out=v_(7), in_=v_(7), func=AF.Sqrt), reads=[TK[7]], writes=[TK[7]])
                kb.op("dve", lambda: nc.vector.reciprocal(out=v_(7), in_=v_(7)), reads=[TK[7]], writes=[TK[7]])
                kb.op("dve", lambda: nc.vector.tensor_tensor(out=v_(5), in0=v_(5), in1=v_(7), op=ALU.mult), reads=[TK[5], TK[7]], writes=[TK[5]])
                kb.op("act", lambda: nc.scalar.activation(out=v_(6), in_=v_(4), func=AF.Identity, scale=pv[:, kac:kac + 1], bias=dv[:, ka1:ka1 + 1]),
                      reads=[TK[4], "pv", "dv"], writes=[TK[6]])
                kb.op("dve", lambda: nc.vector.tensor_tensor(out=v_(0), in0=v_(0), in1=v_(6), op=ALU.mult), reads=[TK[0], TK[6]], writes=[TK[0]])
                kb.op("dve", lambda: nc.vector.scalar_tensor_tensor(out=v_(6), in0=v_(1), scalar=pv[:, rkc:rkc + 1], in1=v_(0), op0=ALU.mult, op1=ALU.mult),
                      reads=[TK[1], TK[0], "pv"], writes=[TK[6]])
                for blk in range(W // 512):
                    c0 = blk * 512
                    pb = blk % 4
                    kb.op("pe", lambda: nc.tensor.matmul(ps[pb][:, 0:512], lhsT=bones[:], rhs=T[6][:, c0:c0 + 512], start=True, stop=True),
                          reads=["bones", TK[6]], writes=[PS[pb]])
                    kb.op("dve", lambda: nc.vector.tensor_tensor(out=T[7][:, c0:c0 + 512], in0=ps[pb][:, 0:512], in1=T[2][:, c0:c0 + 512], op=ALU.mult),
                          reads=[PS[pb], TK[2]], writes=[TK[7]])
                kb.dma("pool", wk[(20 + d * 4 + ct) * 128:(20 + d * 4 + ct + 1) * 128, g0:g1], v_(7), reads=[TK[7]], semkey="R_bst")
                if d == 0:
                    kb.op("dve", lambda: nc.vector.tensor_tensor_scan(out=v_(6), data0=maskf[:, 0:W], data1=v_(3), initial=0.0, op0=ALU.mult, op1=ALU.add),
                          reads=["R_maskf", TK[3]], writes=[TK[6]])
                else:
                    kb.op("dve", lambda: nc.vector.tensor_tensor_scan(out=v_(6)[:, ::-1], data0=maskb[:, 0:W][:, ::-1], data1=v_(3)[:, ::-1], initial=0.0,
                                                                      op0=ALU.mult, op1=ALU.add), reads=["R_maskb", TK[3]], writes=[TK[6]])
                cum3 = T[6][:, 0:W].rearrange("p (c t) -> p c t", t=CH)
                endc = cum3[:, :, CH - 1:CH] if d == 0 else cum3[:, :, 0:1]
                kb.op("act", lambda: nc.scalar.activation(out=v_(7), in_=v_(6), func=AF.Exp), reads=[TK[6]], writes=[TK[7]])
                kb.op("dve", lambda: nc.vector.tensor_tensor(out=v_(1), in0=v_(1), in1=v_(7), op=ALU.mult), reads=[TK[1], TK[7]], writes=[TK[1]])
                e3 = T[7][:, 0:W].rearrange("p (c t) -> p c t", t=CH)
                ende = e3[:, :, CH - 1:CH] if d == 0 else e3[:, :, 0:1]
                kb.op("dve", lambda: nc.vector.tensor_copy(out=gC[:, 0:nch].unsqueeze(2), in_=ende), reads=[TK[7]], writes=["R_gC"])
                kb.op("dve", lambda: nc.vector.tensor_tensor(out=v_(7), in0=v_(6), in1=v_(3), op=ALU.subtract), reads=[TK[6], TK[3], "R_gC"], writes=[TK[7]])
                kb.op("act", lambda: nc.scalar.activation(out=v_(7), in_=v_(7), func=AF.Exp), reads=[TK[7]], writes=[TK[7]])
                kb.op("dve", lambda: nc.vector.scalar_tensor_tensor(out=v_(8), in0=v_(5), scalar=-1.0, in1=v_(7), op0=ALU.mult, op1=ALU.mult),
                      reads=[TK[5], TK[7]], writes=[TK[8]])
                kb.op("act", lambda: nc.scalar.activation(out=v_(7), in_=v_(6), func=AF.Exp, scale=-1.0), reads=[TK[6], TK[8]], writes=[TK[7]])
                kb.op("dve", lambda: nc.vector.tensor_tensor(out=v_(9), in0=v_(4), in1=v_(5), op=ALU.mult), reads=[TK[4], TK[5]], writes=[TK[9]])
                kb.op("dve", lambda: nc.vector.tensor_tensor(out=v_(3), in0=v_(9), in1=v_(7), op=ALU.mult), reads=[TK[9], TK[7]], writes=[TK[3]])
                kb.op("dve", lambda: nc.vector.tensor_tensor(out=v_(4), in0=v_(0), in1=v_(7), op=ALU.mult), reads=[TK[0], TK[7]], writes=[TK[4]])
                kb.op("dve", lambda: nc.vector.tensor_tensor(out=e3, in0=endc.to_broadcast([128, nch, CH]), in1=cum3, op=ALU.subtract),
                      reads=[TK[6], TK[3], TK[4]], writes=[TK[7]])
                kb.op("act", lambda: nc.scalar.activation(out=v_(7), in_=v_(7), func=AF.Exp), reads=[TK[7]], writes=[TK[7]])
                kb.op("dve", lambda: nc.vector.tensor_tensor(out=v_(9), in0=v_(9), in1=v_(7), op=ALU.mult), reads=[TK[9], TK[7]], writes=[TK[9]])
                kb.op("dve", lambda: nc.vector.tensor_tensor(out=v_(0), in0=v_(0), in1=v_(7), op=ALU.mult), reads=[TK[0], TK[7]], writes=[TK[0]])
                aF, bF, kF, rF, beF, keF, vF = T[8], T[3], T[4], T[1], T[9], T[0], T[2]
                aK, bK, kK, rK, beK, keK, vK = TK[8], TK[3], TK[4], TK[1], TK[9], TK[0], TK[2]
                corder = range(nch) if d == 0 else range(nch - 1, -1, -1)
                for c in corder:
                    sl = slice(c * CH, (c + 1) * CH)
                    tglob = g0 + c * CH
                    si = 0 if tglob < NS else (1 if tglob < NS + NP else 2)
                    for qi, (src, sk) in enumerate(((aF, aK), (vF, vK), (beF, beK), (keF, keK))):
                        kb.op("pe", lambda: nc.tensor.transpose(out=ps[6][:, qi * 128:(qi + 1) * 128], in_=src[:, sl], identity=ident[:]),
                              reads=[sk, "ident"], writes=[PS[6]])
                    evac(tok[:].rearrange("p q c -> p (q c)"), ps[6][:, 0:512], [PS[6]], ["R_tok"])
                    yt_ = ytok[yn % 2]
                    ytk = "R_y%d" % (yn % 2)
                    yn += 1
                    for hh in range(2):
                        hp = slice(hh * 64, (hh + 1) * 64)
                        hc = slice(hh * 64, (hh + 1) * 64)
                        xt_ = XT[hh]
                        xtk = "R_XT%d" % hh
                        for qi, (lh, lk, rh, rk_) in enumerate(((bF, bK, aF, aK), (bF, bK, rF, rK), (kF, kK, aF, aK), (kF, kK, rF, rK))):
                            kb.op("pe", lambda: nc.tensor.matmul(ps[hh][:, qi * 128:(qi + 1) * 128], lhsT=lh[hp, sl], rhs=rh[hp, sl], start=True, stop=True),
                                  reads=[lk, rk_], writes=[PS[hh]])
                        kb.op("dve", lambda: nc.vector.tensor_tensor(out=xt_[:], in0=ps[hh][:, 0:512], in1=mask4[:, d, :], op=ALU.mult),
                              reads=[PS[hh], "mask4"], writes=[xtk])
                        kb.op("pe", lambda: nc.tensor.matmul(ps[2 + hh][:, 0:128], lhsT=aF[hp, sl], rhs=bF[hp, sl], start=True, stop=True),
                              reads=[aK, bK], writes=[PS[2 + hh]])
                        L_, LT_ = Lm[0], LT[0]
                        kb.op("dve", lambda: nc.vector.tensor_tensor(out=L_[:], in0=ps[2 + hh][:, 0:128], in1=maskL[:, d, :], op=ALU.mult),
                              reads=[PS[2 + hh], "maskL"], writes=["R_L0"])
                        kb.op("act", lambda: nc.scalar.activation(out=LT_[:], in_=xt_[:, 0:128], func=AF.Copy), reads=[xtk], writes=["R_LT0"])
                        kb.op("pe", lambda: nc.tensor.matmul(ps[2 + hh][:, 128:192], lhsT=xt_[:, 256:384], rhs=tok[:, 1, hc], start=True, stop=True),
                              reads=[xtk, "R_tok"], writes=[PS[2 + hh]])
                        kb.op("act", lambda: nc.scalar.activation(out=X[0][:, 0:64], in_=tok[:, 0, hc], func=AF.Copy), reads=["R_tok"], writes=["R_X0"])
                        kb.op("dve", lambda: nc.vector.tensor_copy(out=X[0][:, 64:128], in_=ps[2 + hh][:, 128:192]), reads=[PS[2 + hh]], writes=["R_X0"])
                        cur = 0
                        for lev in range(7):
                            nxt = 1 - cur
                            pbx = 4 + lev % 2
                            kb.op("pe", lambda: nc.tensor.matmul(ps[pbx][:, 0:128], lhsT=LT[cur][:], rhs=X[cur][:], start=True, stop=True),
                                  reads=["R_LT%d" % cur, "R_X%d" % cur], writes=[PS[pbx]])
                            kb.op("dve", lambda: nc.vector.tensor_tensor(out=X[nxt][:], in0=ps[pbx][:, 0:128], in1=X[cur][:], op=ALU.add),
                                  reads=[PS[pbx], "R_X%d" % cur], writes=["R_X%d" % nxt])
                            if lev < 6:
                                kb.op("pe", lambda: nc.tensor.matmul(ps[pbx][:, 128:256], lhsT=Lm[cur][:], rhs=LT[cur][:], start=True, stop=True),
                                      reads=["R_L%d" % cur, "R_LT%d" % cur], writes=[PS[pbx]])
                                kb.op("pe", lambda: nc.tensor.matmul(ps[pbx][:, 256:384], lhsT=LT[cur][:], rhs=Lm[cur][:], start=True, stop=True),
                                      reads=["R_L%d" % cur, "R_LT%d" % cur], writes=[PS[pbx]])
                                kb.op("act", lambda: nc.scalar.activation(out=LT[nxt][:], in_=ps[pbx][:, 128:256], func=AF.Copy), reads=[PS[pbx]], writes=["R_LT%d" % nxt])
                                kb.op("act", lambda: nc.scalar.activation(out=Lm[nxt][:], in_=ps[pbx][:, 256:384], func=AF.Copy), reads=[PS[pbx]], writes=["R_L%d" % nxt])
                            cur = nxt
                        Xf = X[cur]
                        Xk = "R_X%d" % cur
                        W1 = Xf[:, 0:64]
                        W2 = Xf[:, 64:128]
                        pz = 2 + hh
                        kb.op("pe", lambda: nc.tensor.matmul(ps[pz][hp, 192:256], lhsT=W1, rhs=tok[:, 2, hc], start=True, stop=True),
                              reads=[Xk, "R_tok"], writes=[PS[pz]])
                        kb.op("dve", lambda: nc.vector.scalar_tensor_tensor(out=PT[hp, :], in0=ident[hp, hc], scalar=gC[hp, c:c + 1], in1=ps[pz][hp, 192:256],
                                                                            op0=ALU.mult, op1=ALU.add), reads=["ident", "R_gC", PS[pz]], writes=["R_PT%d" % hh])
                        kb.op("pe", lambda: nc.tensor.matmul(ps[pz][hp, 256:320], lhsT=tok[:, 2, hc], rhs=W2, start=True, stop=False),
                              reads=[Xk, "R_tok"], writes=[PS[pz]])
                        kb.op("pe", lambda: nc.tensor.matmul(ps[pz][hp, 256:320], lhsT=tok[:, 3, hc], rhs=tok[:, 1, hc], start=False, stop=True),
                              reads=["R_tok"], writes=[PS[pz]])
                        kb.op("act", lambda: nc.scalar.activation(out=Qt[hp, :], in_=ps[pz][hp, 256:320], func=AF.Copy), reads=[PS[pz]], writes=["R_Q%d" % hh])
                        kb.op("pe", lambda: nc.tensor.matmul(ps[pz][hp, 320:448], lhsT=W1, rhs=xt_[:, 128:256], start=True, stop=True),
                              reads=[Xk, xtk], writes=[PS[pz]])
                        kb.op("dve", lambda: nc.vector.tensor_tensor(out=GT[hp, :], in0=ps[pz][hp, 320:448], in1=rF[hp, sl], op=ALU.add),
                              reads=[PS[pz], rK], writes=["R_GT%d" % hh])
                        py = 7
                        kb.op("pe", lambda: nc.tensor.matmul(ps[py][:, hh * 64:(hh + 1) * 64], lhsT=xt_[:, 128:256], rhs=W2, start=True, stop=False),
                              reads=[xtk, Xk], writes=[PS[py]])
                        kb.op("pe", lambda: nc.tensor.matmul(ps[py][:, hh * 64:(hh + 1) * 64], lhsT=xt_[:, 384:512], rhs=tok[:, 1, hc], start=False, stop=False),
                              reads=[xtk, "R_tok"], writes=[PS[py]])
                        kb.op("pe", lambda: nc.tensor.matmul(ps[py][:, hh * 64:(hh + 1) * 64], lhsT=GT[hp, :], rhs=Mst[hp, si, :], start=False, stop=True),
                              reads=["R_GT%d" % hh, "R_M%d" % hh], writes=[PS[py]])
                        kb.op("act", lambda: nc.scalar.activation(out=yt_[:, hh * 64:(hh + 1) * 64], in_=ps[py][:, hh * 64:(hh + 1) * 64], func=AF.Copy),
                              reads=[PS[py]], writes=[ytk])
                        kb.op("pe", lambda: nc.tensor.matmul(ps[pz][hp, 448:512], lhsT=PT[hp, :], rhs=Mst[hp, si, :], start=True, stop=True),
                              reads=["R_PT%d" % hh, "R_M%d" % hh], writes=[PS[pz]])
                        kb.op("dve", lambda: nc.vector.tensor_tensor(out=Mst[hp, si, :], in0=ps[pz][hp, 448:512], in1=Qt[hp, :], op=ALU.add),
                              reads=[PS[pz], "R_Q%d" % hh], writes=["R_M%d" % hh])
                    yT_ = ytT[(yn - 1) % 2]
                    yTk = "R_yT%d" % ((yn - 1) % 2)
                    kb.op("pe", lambda: nc.tensor.transpose(out=ps[6][:, 0:128], in_=yt_[:], identity=ident[:]), reads=[ytk, "ident"], writes=[PS[6]])
                    evac(yT_[:], ps[6][:, 0:128], [PS[6]], [yTk])
                    kb.dma("pool", ywk[d, ct * 128:(ct + 1) * 128, tglob:tglob + CH], yT_[:], reads=[yTk], semkey="R_yst%d" % ((yn - 1) % 2))
            for hh in range(2):
                for pi in range(2):
                    kb.dma("pool", wkv_out[pi, j, d, 2 * ct + hh], Mst[hh * 64:(hh + 1) * 64, 1 + pi, :], reads=["R_M%d" % hh], semkey="R_mst%d" % hh, is_output=True)
    kb.end_phase()


def rwkv_post(kb, nc, ps, PS, pack, pv, j, y_dram, scr, evac, ident, cst):
    wk = scr["wk"]
    ywk = scr["ywk"]
    bones = cst["bones"]
    kb.begin_phase()
    ya = kb.sb("O_ya", [128, NT])
    yb = kb.sb("O_yb", [128, NT])
    mt = kb.sb("O_mt", [128, NT])
    sq = kb.sb("O_sq", [128, NT])
    b0 = kb.sb("O_b0", [128, NT])
    b1 = kb.sb("O_b1", [128, NT])
    gg = kb.sb("O_g", [128, NT])
    ob = kb.sb("O_ob", [128, NT], BF16)
    for ct in range(4):
        kb.dma("sp", ya[:], ywk[0, ct * 128:(ct + 1) * 128, :], writes=["O_ya"])
        kb.dma("sp", yb[:], ywk[1, ct * 128:(ct + 1) * 128, :], writes=["O_yb"])
        kb.dma("sp", b0[:], wk[(20 + ct) * 128:(21 + ct) * 128, :], writes=["O_b0"])
        kb.dma("sp", b1[:], wk[(24 + ct) * 128:(25 + ct) * 128, :], writes=["O_b1"])
        kb.dma("sp", gg[:], wk[(16 + ct) * 128:(17 + ct) * 128, :], writes=["O_g"])
        kb.op("dve", lambda: nc.vector.tensor_tensor(out=ya[:], in0=ya[:], in1=yb[:], op=ALU.add), reads=["O_ya", "O_yb"], writes=["O_ya"])
        for blk in range(NT // 512):
            c0 = blk * 512
            pb = blk % 4
            kb.op("pe", lambda: nc.tensor.matmul(ps[pb][:, 0:512], lhsT=bones[:], rhs=ya[:, c0:c0 + 512], start=True, stop=True),
                  reads=["bones", "O_ya"], writes=[PS[pb]])
            kb.op("act", lambda: nc.scalar.activation(out=mt[:, c0:c0 + 512], in_=ps[pb][:, 0:512], func=AF.Copy, scale=1.0 / 64),
                  reads=[PS[pb]], writes=["O_mt"])
        kb.op("dve", lambda: nc.vector.tensor_tensor(out=ya[:], in0=ya[:], in1=mt[:], op=ALU.subtract), reads=["O_ya", "O_mt"], writes=["O_ya"])
        kb.op("dve", lambda: nc.vector.tensor_tensor(out=sq[:], in0=ya[:], in1=ya[:], op=ALU.mult), reads=["O_ya"], writes=["O_sq"])
        for blk in range(NT // 512):
            c0 = blk * 512
            pb = blk % 4
            kb.op("pe", lambda: nc.tensor.matmul(ps[pb][:, 0:512], lhsT=bones[:], rhs=sq[:, c0:c0 + 512], start=True, stop=True),
                  reads=["bones", "O_sq"], writes=[PS[pb]])
            kb.op("act", lambda: nc.scalar.activation(out=mt[:, c0:c0 + 512], in_=ps[pb][:, 0:512], func=AF.Sqrt, scale=1.0 / 64, bias=cst["eps"][:, 2:3]),
                  reads=[PS[pb], "pv_eps"], writes=["O_mt"])
        kb.op("dve", lambda: nc.vector.reciprocal(out=mt[:], in_=mt[:]), reads=["O_mt"], writes=["O_mt"])
        kb.op("dve", lambda: nc.vector.tensor_tensor(out=ya[:], in0=ya[:], in1=mt[:], op=ALU.mult), reads=["O_ya", "O_mt"], writes=["O_ya"])
        gc = pack.col("wkv_gn_g", j, ct)
        bc = pack.col("wkv_gn_b", j, ct)
        kb.op("act", lambda: nc.scalar.activation(out=ya[:], in_=ya[:], func=AF.Identity, scale=pv[:, gc:gc + 1], bias=pv[:, bc:bc + 1]),
              reads=["O_ya", "pv"], writes=["O_ya"])
        kb.op("dve", lambda: nc.vector.tensor_tensor(out=b0[:], in0=b0[:], in1=b1[:], op=ALU.add), reads=["O_b0", "O_b1"], writes=["O_b0"])
        kb.op("dve", lambda: nc.vector.tensor_tensor(out=b0[:], in0=b0[:], in1=ya[:], op=ALU.add), reads=["O_b0", "O_ya"], writes=["O_b0"])
        kb.op("dve", lambda: nc.vector.tensor_tensor(out=ob[:], in0=b0[:], in1=gg[:], op=ALU.mult), reads=["O_b0", "O_g"], writes=["O_ob"])
        kb.dma("pool", y_dram[512 + ct * 128:512 + (ct + 1) * 128, :], ob[:], reads=["O_ob"], semkey="O_yst")
    kb.end_phase()


def _topk16_batch(kb, nc, srcs, srckeys, scr3, scrk, m16, i16, outk):
    G = len(srcs)
    for g in range(G):
        kb.op("dve", lambda: nc.vector.max(out=m16[:, g, 0:8], in_=srcs[g]), reads=srckeys[g], writes=["%s_a%d" % (outk, g)])
    for g in range(G):
        kb.op("dve", lambda: nc.vector.match_replace(out=scr3[:, g, :], in_to_replace=m16[:, g, 0:8], in_values=srcs[g], imm_value=-1e30),
              reads=srckeys[g] + ["%s_a%d" % (outk, g)], writes=["%s%d" % (scrk, g)])
    for g in range(G):
        kb.op("dve", lambda: nc.vector.max(out=m16[:, g, 8:16], in_=scr3[:, g, :]), reads=["%s%d" % (scrk, g)], writes=["%s_b%d" % (outk, g)])
    for g in range(G):
        kb.op("dve", lambda: nc.vector.max_index(out=i16[:, g, 0:8], in_max=m16[:, g, 0:8], in_values=srcs[g]),
              reads=srckeys[g] + ["%s_a%d" % (outk, g)], writes=["%s_ia%d" % (outk, g)])
    for g in range(G):
        kb.op("dve", lambda: nc.vector.max_index(out=i16[:, g, 8:16], in_max=m16[:, g, 8:16], in_values=scr3[:, g, :]),
              reads=["%s%d" % (scrk, g), "%s_b%d" % (outk, g)], writes=["%s_ib%d" % (outk, g)])
    return [("%s_%s%d" % (outk, sfx, g)) for g in range(G) for sfx in ("a", "b", "ia", "ib")]


def peer_select(kb, nc, ps, PS, l, peer_wq, keysT, h2_dram, sel_dram, ident, iota, evac, load_w_bf16):
    kb.begin_phase()
    wq = kb.sb("Q_wq", [128, 8, 2048], BF16)
    load_w_bf16(wq, "Q_wq", peer_wq[l], 2048)
    kT = kb.sb("Q_kT", [128, 16, 128], BF16)
    kb.dma("pool", kT[:], keysT[l], writes=["Q_kT"], max_dma_last_dim=4096)
    h2b = [kb.sb("Q_h2_%d" % i, [128, 8, TT], BF16) for i in range(2)]
    qT = kb.sb("Q_qT", [128, 16, TT], BF16)
    scrA = kb.sb("Q_scrA", [128, 16, 128])
    scrB = kb.sb("Q_scrB", [128, 8, 256])
    v16 = kb.sb("Q_v16", [128, 16, 16])
    i16 = kb.sb("Q_i16", [128, 16, 16], U32)
    i16f = kb.sb("Q_i16f", [128, 16, 16])
    cand = kb.sb("Q_cand", [128, 8, 256])
    sc16 = kb.sb("Q_sc16", [128, 8, 16])
    ci16 = kb.sb("Q_ci16", [128, 8, 16], U32)
    ai = kb.sb("Q_ai", [128, 8, 16], U32)
    bi = kb.sb("Q_bi", [128, 8, 16], U32)
    af = kb.sb("Q_af", [128, 8, 16])
    bf = kb.sb("Q_bf", [128, 8, 16])
    oh = kb.sb("Q_oh", [128, 8, 16, 16])
    ssum = kb.sb("Q_ssum", [128, 8])
    seltm = kb.sb("Q_seltm", [128, 3, 128])
    selT = [kb.sb("Q_selT%d" % i, [128, 3, TT]) for i in range(2)]
    io16 = iota[:, 0:16].unsqueeze(1).unsqueeze(1).to_broadcast([128, 8, 16, 16])
    for t in range(NTILES):
        t0 = t * TT
        h2 = h2b[t % 2]
        h2k = "Q_h2_%d" % (t % 2)
        sT = selT[t % 2]
        sTk = "Q_selT%d" % (t % 2)
        kb.dma("sp", h2[:], h2_dram[:, t0:t0 + TT].rearrange("(k p) t -> p k t", p=128), writes=[h2k])
        for hp in range(16):
            pb = 4 + hp % 2
            for k in range(8):
                kb.op("pe", lambda: nc.tensor.matmul(ps[pb][:, 0:TT], lhsT=wq[:, k, hp * 128:(hp + 1) * 128], rhs=h2[:, k, :],
                                                     start=(k == 0), stop=(k == 7)), reads=["Q_wq", h2k], writes=[PS[pb]])
            evac(qT[:, hp, :], ps[pb][:, 0:TT], [PS[pb]], ["Q_qT"])
        for sub in range(2):
            for hp in range(16):
                pb = hp // 4
                kb.op("pe", lambda: nc.tensor.matmul(ps[pb][:, (hp % 4) * 128:(hp % 4 + 1) * 128], lhsT=qT[:, hp, sub * 128:(sub + 1) * 128],
                                                     rhs=kT[:, hp, :], start=True, stop=True), reads=["Q_qT", "Q_kT"], writes=[PS[pb]])
            vkeys = _topk16_batch(kb, nc, [ps[hp // 4][:, (hp % 4) * 128:(hp % 4 + 1) * 128] for hp in range(16)],
                                  [[PS[hp // 4]] for hp in range(16)], scrA, "Q_scrA",
                                  v16, i16, "Q_v16")
            kb.op("dve", lambda: nc.vector.tensor_copy(out=i16f[:], in_=i16[:]), reads=vkeys, writes=["Q_i16f"])
            kb.op("dve", lambda: nc.vector.tensor_tensor(out=cand[:].rearrange("p h (a b) -> p h a b", a=16),
                                                         in0=v16[:, 0::2, :].unsqueeze(3).to_broadcast([128, 8, 16, 16]),
                                                         in1=v16[:, 1::2, :].unsqueeze(2).to_broadcast([128, 8, 16, 16]), op=ALU.add),
                  reads=vkeys, writes=["Q_cand"])
            skeys = _topk16_batch(kb, nc, [cand[:, h, :] for h in range(8)], [["Q_cand"] for h in range(8)],
                                  scrB, "Q_scrB", sc16, ci16, "Q_sc16")
            kb.op("dve", lambda: nc.vector.tensor_tensor(out=af[:], in0=sc16[:], in1=sc16[:, :, 0:1].to_broadcast([128, 8, 16]), op=ALU.subtract),
                  reads=skeys, writes=["Q_af"])
            kb.op("act", lambda: nc.scalar.activation(out=af[:], in_=af[:], func=AF.Exp), reads=["Q_af"], writes=["Q_af"])
            kb.op("dve", lambda: nc.vector.tensor_reduce(out=ssum[:], in_=af[:], axis=AX.X, op=ALU.add), reads=["Q_af"], writes=["Q_ssum"])
            kb.op("dve", lambda: nc.vector.reciprocal(out=ssum[:], in_=ssum[:]), reads=["Q_ssum"], writes=["Q_ssum"])
            kb.op("dve", lambda: nc.vector.tensor_tensor(out=seltm[:, 2, :].rearrange("p (h k) -> p h k", h=8), in0=af[:],
                                                         in1=ssum[:].unsqueeze(2).to_broadcast([128, 8, 16]), op=ALU.mult),
                  reads=["Q_af", "Q_ssum"], writes=["Q_seltm"])
            kb.op("dve", lambda: nc.vector.tensor_single_scalar(out=ai[:], in_=ci16[:], scalar=4, op=ALU.logical_shift_right),
                  reads=skeys, writes=["Q_ai"])
            kb.op("dve", lambda: nc.vector.tensor_single_scalar(out=bi[:], in_=ci16[:], scalar=15, op=ALU.bitwise_and),
                  reads=skeys, writes=["Q_bi"])
            kb.op("dve", lambda: nc.vector.tensor_copy(out=af[:], in_=ai[:]), reads=["Q_ai", "Q_seltm"], writes=["Q_af"])
            kb.op("dve", lambda: nc.vector.tensor_copy(out=bf[:], in_=bi[:]), reads=["Q_bi"], writes=["Q_bf"])
            for which, xf in ((0, af), (1, bf)):
                xk_ = "Q_af" if which == 0 else "Q_bf"
                kb.op("dve", lambda: nc.vector.tensor_tensor(out=oh[:], in0=io16, in1=xf[:].unsqueeze(3).to_broadcast([128, 8, 16, 16]), op=ALU.is_equal),
                      reads=["iota", xk_], writes=["Q_oh"])
                kb.op("dve", lambda: nc.vector.tensor_tensor(out=oh[:], in0=oh[:],
                                                             in1=i16f[:, which::2, :].unsqueeze(2).to_broadcast([128, 8, 16, 16]), op=ALU.mult),
                      reads=["Q_oh", "Q_i16f"], writes=["Q_oh"])
                kb.op("dve", lambda: nc.vector.tensor_reduce(out=seltm[:, which, :].rearrange("p (h k) -> p h k", h=8), in_=oh[:], axis=AX.X, op=ALU.add),
                      reads=["Q_oh"], writes=["Q_seltm"])
            for s_ in range(3):
                kb.op("pe", lambda: nc.tensor.transpose(out=ps[6][:, s_ * 128:(s_ + 1) * 128], in_=seltm[:, s_, :], identity=ident[:]),
                      reads=["Q_seltm", "ident"], writes=[PS[6]])
            evac(sT[:, :, sub * 128:(sub + 1) * 128], ps[6][:, 0:384].rearrange("p (s t) -> p s t", s=3), [PS[6]], [sTk])
        kb.dma("pool", sel_dram[:, :, t0:t0 + TT], sT[:], reads=[sTk], semkey="Q_selst%d" % (t % 2))
    kb.end_phase()


def peer_alloc(kb, nc):
    st = {}
    st["sel"] = [kb.sb("P_sel%d" % i, [128, 3, TT]) for i in range(2)]
    st["A"] = kb.sb("P_A", [128, 64, 128], BF16)
    st["B"] = kb.sb("P_B", [128, 64, 128], BF16)
    st["G"] = kb.sb("P_G", [128, TT, 128], BF16)
    st["UV"] = [kb.sb("P_UV%d" % i, [128, 2, 2048], BF16) for i in range(3)]
    st["act"] = [kb.sb("P_act%d" % i, [128, 2, TT]) for i in range(2)]
    st["ga"] = [kb.sb("P_ga%d" % i, [128, 2, TT], BF16) for i in range(2)]
    st["n"] = 0
    return st


def peer_mix(kb, nc, ps, PS, st, l, t, h2, h2k, uv_bf, sel_dram, iota, zer_b, evac):
    t0 = t * TT
    sel = st["sel"][t % 2]
    selk = "P_sel%d" % (t % 2)
    A, B, G = st["A"], st["B"], st["G"]
    kb.dma("sp", sel[:], sel_dram[:, :, t0:t0 + TT], writes=[selk])
    io = iota[:, :].unsqueeze(1).to_broadcast([128, 64, 128])
    for q in range(TT // 64):
        c0 = q * 64
        kb.op("dve", lambda: nc.vector.tensor_tensor(out=A[:], in0=io, in1=sel[:, 0, c0:c0 + 64].unsqueeze(2).to_broadcast([128, 64, 128]), op=ALU.is_equal),
              reads=["iota", selk], writes=["P_A"])
        kb.op("dve", lambda: nc.vector.tensor_tensor(out=A[:], in0=A[:], in1=sel[:, 2, c0:c0 + 64].unsqueeze(2).to_broadcast([128, 64, 128]), op=ALU.mult),
              reads=["P_A", selk], writes=["P_A"])
        kb.op("dve", lambda: nc.vector.tensor_tensor(out=B[:], in0=io, in1=sel[:, 1, c0:c0 + 64].unsqueeze(2).to_broadcast([128, 64, 128]), op=ALU.is_equal),
              reads=["iota", selk], writes=["P_B"])
        for g4 in range(16):
            pb = 6 + g4 % 2
            for tk in range(4):
                tok = g4 * 4 + tk
                kb.op("pe", lambda: nc.tensor.matmul(ps[pb][:, tk * 128:(tk + 1) * 128], lhsT=A[:, tok, :], rhs=B[:, tok, :], start=True, stop=True),
                      reads=["P_A", "P_B"], writes=[PS[pb]])
            evac(G[:, c0 + g4 * 4:c0 + g4 * 4 + 4, :].rearrange("p t e -> p (t e)"), ps[pb][:, 0:512], [PS[pb]], ["P_G"])
    for pb in range(4):
        kb.op("pe", lambda: nc.tensor.matmul(ps[pb][:, 0:512], lhsT=zer_b[:, 0:128], rhs=zer_b[:, 0:512], start=True, stop=False, skip_group_check=True),
              reads=["zer_b"], writes=[PS[pb]])
    for m in range(64):
        n = st["n"]
        st["n"] += 1
        UV = st["UV"][n % 3]
        UVk = "P_UV%d" % (n % 3)
        kb.dma("sp", UV[:], uv_bf[l, 2 * m:2 * m + 2].rearrange("e p c -> p e c"), writes=[UVk])
        pb = 4 + n % 2
        for c2 in range(2):
            for k in range(8):
                kb.op("pe", lambda: nc.tensor.matmul(ps[pb][:, c2 * TT:(c2 + 1) * TT], lhsT=UV[:, c2, k * 128:(k + 1) * 128], rhs=h2[:, k, :],
                                                     start=(k == 0), stop=(k == 7)), reads=[UVk, h2k], writes=[PS[pb]])
        ac = st["act"][n % 2]
        ack = "P_act%d" % (n % 2)
        ga = st["ga"][n % 2]
        gak = "P_ga%d" % (n % 2)
        kb.op("act", lambda: nc.scalar.activation(out=ac[:].rearrange("p c t -> p (c t)"), in_=ps[pb][:, 0:2 * TT], func=AF.Gelu_apprx_tanh),
              reads=[PS[pb]], writes=[ack])
        kb.op("dve", lambda: nc.vector.tensor_tensor(out=ga[:], in0=ac[:], in1=G[:, :, 2 * m:2 * m + 2].rearrange("p t e -> p e t"), op=ALU.mult),
              reads=[ack, "P_G"], writes=[gak])
        for c2 in range(2):
            for dc in range(8):
                kb.op("pe", lambda: nc.tensor.matmul(ps[dc // 2][:, (dc % 2) * TT:(dc % 2 + 1) * TT], lhsT=UV[:, c2, 1024 + dc * 128:1024 + (dc + 1) * 128],
                                                     rhs=ga[:, c2, :], start=False, stop=(m == 63 and c2 == 1), skip_group_check=True),
                      reads=[UVk, gak], writes=[PS[dc // 2]])


def _prep_shared(inp):
    sh = {}
    sh["posT"] = np.ascontiguousarray(_grid_pos_embed(NS).T)
    sh["ident"] = np.eye(128, dtype=np.float32)
    sh["iota"] = np.ascontiguousarray(np.broadcast_to(np.arange(128, dtype=np.float32)[None, :], (128, 128)))
    for k in ("w_mod", "w_in_e", "w_out_e", "w_in_o", "w_out_o", "peer_wq"):
        sh[k] = np.ascontiguousarray(inp[k], dtype=np.float32)
    sh["keysT"] = np.ascontiguousarray(np.transpose(inp["peer_keys"], (0, 4, 1, 2, 3)).reshape(DEPTH, 128, 16, 128))
    if not KSKIP_PEER:
        u = np.asarray(inp["peer_u"]).reshape(DEPTH, 128, 128, 8, 128)
        uv = np.empty((DEPTH, 128, 128, 2048), np.float32)
        uv[..., 0:1024] = np.transpose(u, (0, 2, 4, 3, 1)).reshape(DEPTH, 128, 128, 1024)
        v = np.asarray(inp["peer_v"]).reshape(DEPTH, 128, 128, 1024)
        uv[..., 1024:2048] = np.transpose(v, (0, 2, 1, 3))
        sh["peer_uv"] = uv
    else:
        sh["peer_uv"] = np.zeros((DEPTH, 1, 128, 2048), np.float32)
    c = np.arange(128, dtype=np.int64)
    ang = 2.0 * np.pi * ((c[:, None] * c[None, :]) % 128).astype(np.float64) / 128.0
    sh["dftc"] = np.ascontiguousarray(np.concatenate([np.cos(ang), np.sin(ang)], 1).astype(np.float32))
    if any(l_ % 2 == 0 for l_ in range(KSTART, KDEPTH)):
        sh["dftS"] = np.ascontiguousarray(np.stack(_dft_tables(NS)))
    else:
        sh["dftS"] = np.zeros((2, 128, 128), np.float32)
    sh["dftP"] = np.ascontiguousarray(np.stack(_dft_tables(NP)))
    wbd = np.zeros((2, 2, 2, 4, 128, 128), np.float32)
    for ai, nm in enumerate(("lru_wa", "lru_wx")):
        w = np.asarray(inp[nm])
        for ct in range(4):
            for hh in range(2):
                wbd[:, :, ai, ct, hh * 64:(hh + 1) * 64, hh * 64:(hh + 1) * 64] = w[:, :, 2 * ct + hh]
    sh["lru_wbd"] = wbd
    for k in ("wkv_w2", "wkv_a2", "wkv_g2"):
        sh[k] = np.ascontiguousarray(inp[k], dtype=np.float32)
    r = np.arange(128)
    su = (r[None, :] > r[:, None]).astype(np.float32)
    iu = (r[None, :] >= r[:, None]).astype(np.float32)
    sl = (r[None, :] < r[:, None]).astype(np.float32)
    il = (r[None, :] <= r[:, None]).astype(np.float32)
    cm = np.zeros((128, 2, 768), np.float32)
    cm[:, 0, :] = np.concatenate([su, iu, su, iu, sl, np.zeros((128, 128), np.float32)], 1)
    cm[:, 1, :] = np.concatenate([sl, il, sl, il, su, np.zeros((128, 128), np.float32)], 1)
    sh["cmask"] = cm
    bo = np.zeros((128, 128), np.float32)
    bo[:64, :64] = 1.0
    bo[64:, 64:] = 1.0
    sh["bones"] = bo
    return sh


def _make_pack(inp, b):
    pk = Pack()
    pk.add("b_mod", inp["b_mod"].reshape(DEPTH, 48, 128))
    for nm in ("ln1_g", "ln1_b", "ln2_g", "ln2_b", "sconv_w", "sconv_b", "lru_conv_w", "lru_conv_b", "lru_ba", "lru_bx",
               "lru_lambda", "wkv_mu", "wkv_w0", "wkv_a0", "wkv_kk", "wkv_ka", "wkv_gn_g", "wkv_gn_b"):
        pk.add(nm, inp[nm])
    pk.add("wkv_rk", np.asarray(inp["wkv_rk"]).reshape(2, 512))
    pk.add("state_lru", inp["state_lru"][b])
    return pk


def kernel(**inputs):
    inp = {k: np.asarray(v) for k, v in inputs.items()}
    sh = _prep_shared(inp)
    packs = [_make_pack(inp, c % 4) for c in range(8)]
    kb = build(packs[0])
    print("kernel: instructions", kb.ninst, "sems", len(kb.sems), flush=True)
    in_maps = []
    for c in range(8):
        b = c % 4
        m = dict(sh)
        xs = inp["x_sample"][b]
        xp = inp["x_prompt"][2 * c:2 * c + 2].reshape(2 * NP, D)
        m["xT"] = np.ascontiguousarray(np.concatenate([xs, xp], 0).T.astype(np.float32))
        cv = np.stack([inp["c"][b], inp["c_ctx"]], -1).astype(np.float32)
        m["cv"] = np.ascontiguousarray(cv.reshape(8, 128, 2).transpose(1, 0, 2))
        m["pvec"] = packs[c].array()
        m["wkv0"] = np.ascontiguousarray(np.swapaxes(inp["state_wkv"][b], -1, -2).astype(np.float32))
        in_maps.append(m)
    res = run_bass_kernel_spmd(kb.nc, in_maps, core_ids=list(range(8)))
    R = res.results
    y_prompt = np.zeros((16, NP, D), np.float32)
    y_sample = np.zeros((4, NS, D), np.float32)
    lru_new = np.zeros((16, 2, 2, 512), np.float32)
    wkv_new = np.zeros((16, 2, 2, 8, 64, 64), np.float32)
    for c in range(8):
        yT = np.asarray(R[c]["yT"])
        if c < 4:
            y_sample[c] = yT[:, :NS].T
        y_prompt[2 * c] = yT[:, NS:NS + NP].T
        y_prompt[2 * c + 1] = yT[:, NS + NP:].T
        lo = np.asarray(R[c]["lru_new"]).reshape(128, 2, 2, 2, 4)
        lru_new[2 * c:2 * c + 2] = np.transpose(lo, (1, 2, 3, 4, 0)).reshape(2, 2, 2, 512)
        wo = np.asarray(R[c]["wkv_new"])
        wkv_new[2 * c:2 * c + 2] = np.transpose(wo, (0, 1, 2, 3, 5, 4))
    if KDEBUG:
        _DBG.clear()
        _DBG.update({k: np.asarray(v) for k, v in R[0].items()})
    return (y_prompt, y_sample, lru_new, wkv_new)
```

```python
import math
import os
from contextlib import ExitStack

import numpy as np
import concourse.bass as bass
import concourse.mybir as mybir
from concourse.bass_utils import run_bass_kernel_spmd

F32 = mybir.dt.float32
BF16 = mybir.dt.bfloat16
U32 = mybir.dt.uint32
I32 = mybir.dt.int32
AF = mybir.ActivationFunctionType
ALU = mybir.AluOpType
AX = mybir.AxisListType

D = 1024
NS = 4096
NP = 256
NT = NS + 2 * NP
TT = 256
NTILES = NT // TT
SEQS = [(0, NS), (NS, NS + NP), (NS + NP, NT)]
DEPTH = 4
ALPHA = (2 * DEPTH) ** 0.25
LN_EPS = 1e-6
N_KEYS = 128
DECAY_SCALE = math.exp(-0.5)
WKV_GN_EPS = 64e-5
EVEN_IN = 2048
ODD_IN = 2944

KDEPTH = int(os.environ.get("KDEPTH", "4"))
KDEBUG = int(os.environ.get("KDEBUG", "0"))
KSKIP_PEER = int(os.environ.get("KSKIP_PEER", "0"))
KSTART = int(os.environ.get("KSTART", "0"))
KODD = int(os.environ.get("KODD", "15"))
_DBG = {}


class Key:
    __slots__ = ("w", "r", "dsem", "dcnt", "name")

    def __init__(self, name):
        self.name = name
        self.w = {}
        self.r = {}
        self.dsem = None
        self.dcnt = 0


class KB:
    def __init__(self):
        self.nc = bass.Bass("TRN2", target_bir_lowering=False)
        self.es = ExitStack()
        nc = self.nc
        self.eng = {"pe": nc.tensor, "act": nc.scalar, "dve": nc.vector, "pool": nc.gpsimd, "sp": nc.sync}
        self.sem = {}
        self.cnt = {}
        self.sems = {}
        for e in ("pe", "act", "dve", "pool"):
            s = self.es.enter_context(nc.semaphore("c_" + e))
            self.sem[e] = s
            self.cnt[e] = 0
            self.sems[id(s)] = s
        self.waited = {e: {} for e in self.eng}
        self.keys = {}
        self.ninst = 0
        self.out_events = []
        self.phase = None
        self.loop_dma = None
        self.depth = 0
        self.lsems = []
        self.ldma_pool = []

    def key(self, name):
        k = self.keys.get(name)
        if k is None:
            k = Key(name)
            self.keys[name] = k
        return k

    def sb(self, name, shape, dt=F32, glob=False):
        st = self.es if (glob or self.phase is None) else self.phase
        self.uid = getattr(self, "uid", 0) + 1
        return st.enter_context(self.nc.sbuf_tensor("%s_u%d" % (name, self.uid), shape, dt))

    def _waits(self, e, reads, writes):
        need = {}
        for kn in reads:
            for s, v in self.key(kn).w.items():
                if need.get(s, 0) < v:
                    need[s] = v
        for kn in writes:
            k = self.key(kn)
            for s, v in k.w.items():
                if need.get(s, 0) < v:
                    need[s] = v
            for s, v in k.r.items():
                if need.get(s, 0) < v:
                    need[s] = v
        wd = self.waited[e]
        own = id(self.sem[e]) if e in self.sem else None
        for s, v in need.items():
            if e == "pe" and s == own:
                continue
            if wd.get(s, 0) < v:
                self.eng[e].wait_ge(self.sems[s], v)
                wd[s] = v

    def op(self, e, fn, reads=(), writes=()):
        self._waits(e, reads, writes)
        ins = fn()
        self.cnt[e] += 1
        c = self.cnt[e]
        ins.then_inc(self.sem[e], 1)
        s = id(self.sem[e])
        for kn in writes:
            k = self.key(kn)
            k.w = {s: c}
            k.r = {}
        for kn in reads:
            k = self.key(kn)
            if k.r.get(s, 0) < c:
                k.r[s] = c
        self.ninst += 1
        return ins

    def dma(self, q, out, in_, reads=(), writes=(), semkey=None, is_output=False, **kw):
        self._waits(q, reads, writes)
        kn = semkey if semkey is not None else (writes[0] if writes else reads[0])
        if self.loop_dma is None:
            k = self.key(kn)
            if k.dsem is None:
                k.dsem = self.es.enter_context(self.nc.semaphore("d_%d" % len(self.sems)))
                self.sems[id(k.dsem)] = k.dsem
            ins = self.eng[q].dma_start(out=out, in_=in_, **kw)
            k.dcnt += 16
            ins.then_inc(k.dsem, 16)
            s = id(k.dsem)
            dc = k.dcnt
            if is_output:
                self.out_events.append((s, dc))
        else:
            ent = self.loop_dma.get(kn)
            if ent is None:
                pool = self.ldma_pool[self.depth - 1]
                idx = len(self.loop_dma)
                if idx >= len(pool):
                    sm = self.es.enter_context(self.nc.semaphore("ld%d_%d" % (self.depth, idx)))
                    self.sems[id(sm)] = sm
                    pool.append(sm)
                ent = [pool[idx], 0]
                self.loop_dma[kn] = ent
            ins = self.eng[q].dma_start(out=out, in_=in_, **kw)
            ent[1] += 16
            ins.then_inc(ent[0], 16)
            s = id(ent[0])
            dc = ent[1]
        for wn in writes:
            w = self.key(wn)
            w.w = {s: dc}
            w.r = {}
        for rn in reads:
            r = self.key(rn)
            if r.r.get(s, 0) < dc:
                r.r[s] = dc
        self.ninst += 1
        return ins

    def _scope_targets(self):
        tgt = {}
        for e, s in self.sem.items():
            if self.cnt[e] > 0:
                tgt[id(s)] = self.cnt[e]
        if self.loop_dma is None:
            for k in self.keys.values():
                if k.dsem is not None and k.dcnt > 0:
                    tgt[id(k.dsem)] = k.dcnt
        else:
            for (sm, c) in self.loop_dma.values():
                if c > 0:
                    tgt[id(sm)] = c
        return tgt

    def barrier(self):
        tgt = self._scope_targets()
        for e in self.eng:
            wd = self.waited[e]
            for s, v in tgt.items():
                if wd.get(s, 0) < v:
                    self.eng[e].wait_ge(self.sems[s], v)
                    wd[s] = v
        self.nc.all_engine_barrier()
        for k in self.keys.values():
            k.w = {}
            k.r = {}

    def loop(self, n):
        return _Loop(self, n)

    def begin_phase(self):
        self.phase = ExitStack()

    def end_phase(self):
        self.barrier()
        self.phase.close()
        self.phase = None

    def finish(self):
        self.barrier()


class _Loop:
    def __init__(self, kb, n):
        self.kb = kb
        self.n = n

    def __enter__(self):
        kb = self.kb
        kb.barrier()
        self.saved = (kb.sem, kb.cnt, kb.waited, kb.loop_dma, kb.depth)
        d = kb.depth + 1
        kb.depth = d
        while len(kb.lsems) < d:
            lv = {}
            for e in ("pe", "act", "dve", "pool"):
                sm = kb.es.enter_context(kb.nc.semaphore("l%d_%s" % (len(kb.lsems), e)))
                kb.sems[id(sm)] = sm
                lv[e] = sm
            kb.lsems.append(lv)
            kb.ldma_pool.append([])
        kb.sem = kb.lsems[d - 1]
        kb.cnt = {e: 0 for e in kb.sem}
        kb.waited = {e: {} for e in kb.eng}
        kb.loop_dma = {}
        self.cm = kb.nc.Fori(0, self.n)
        return self.cm.__enter__()

    def __exit__(self, *a):
        kb = self.kb
        kb.barrier()
        for sm in list(kb.sem.values()) + [v[0] for v in kb.loop_dma.values()]:
            kb.nc.gpsimd.sem_clear(sm)
        kb.nc.all_engine_barrier()
        r = self.cm.__exit__(*a)
        kb.sem, kb.cnt, kb.waited, kb.loop_dma, kb.depth = self.saved
        for k in kb.keys.values():
            k.w = {}
            k.r = {}
        return r


class Pack:
    def __init__(self):
        self.cols = []
        self.off = {}
        self.n = 0

    def add(self, name, arr):
        a = np.asarray(arr, np.float32)
        C = a.shape[-1]
        lead = a.shape[:-1]
        cols = a.reshape(-1, 128).T
        self.off[name] = (self.n, lead, C // 128)
        self.cols.append(cols)
        self.n += cols.shape[1]

    def col(self, name, *idx):
        base, lead, nt = self.off[name]
        flat = 0
        for i, d in zip(idx[:-1], lead):
            flat = flat * d + i
        return base + flat * nt + idx[-1]

    def array(self):
        return np.ascontiguousarray(np.concatenate(self.cols, axis=1))


def _dft_tables(S):
    s = np.arange(S, dtype=np.int64)
    m = (s[:, None] * s[None, :]) % S
    ang = 2.0 * np.pi * m.astype(np.float64) / S
    sc = 1.0 / math.sqrt(S * 128.0)
    return (np.cos(ang) * sc).astype(np.float32), (-np.sin(ang) * sc).astype(np.float32)


def _sincos(pos, dim):
    omega = 1.0 / (10000.0 ** (np.arange(dim // 2, dtype=np.float32) / (dim // 2)))
    ang = pos.astype(np.float32)[:, None] * omega[None, :]
    return np.concatenate([np.sin(ang), np.cos(ang)], -1).astype(np.float32)


def _grid_pos_embed(n_tok):
    rows = n_tok // 64
    half = D // 2
    er = _sincos(np.arange(rows), half)
    ec = _sincos(np.arange(64), half)
    emb = np.concatenate([np.broadcast_to(er[:, None, :], (rows, 64, half)),
                          np.broadcast_to(ec[None, :, :], (rows, 64, half))], -1)
    return emb.reshape(rows * 64, D)


def build(pack):
    kb = KB()
    nc = kb.nc
    ins = {}

    def din(name, shape, dt=F32):
        ins[name] = nc.dram_tensor(name, list(shape), dt, kind="ExternalInput").ap()
        return ins[name]

    def dout(name, shape, dt=F32):
        return nc.dram_tensor(name, list(shape), dt, kind="ExternalOutput").ap()

    def dscr(name, shape, dt=F32):
        ap = nc.dram_tensor(name, list(shape), dt, kind="Internal").ap()
        scr_list.append((name, ap, list(shape), dt))
        return ap

    scr_list = []
    xT_in = din("xT", [D, NT])
    posT = din("posT", [D, NS])
    cv_in = din("cv", [128, 8, 2])
    pv_in = din("pvec", [128, pack.n])
    ident_in = din("ident", [128, 128])
    iota_in = din("iota", [128, 128])
    w_mod = din("w_mod", [DEPTH, D, 6 * D])
    w_in_e = din("w_in_e", [2, D, EVEN_IN])
    w_out_e = din("w_out_e", [2, D, D])
    w_in_o = din("w_in_o", [2, D, ODD_IN])
    w_out_o = din("w_out_o", [2, D, D])
    peer_wq = din("peer_wq", [DEPTH, D, 2048])
    keysT = din("keysT", [DEPTH, 128, 16, 128])
    NE2 = 1 if KSKIP_PEER else 128
    peer_uv = din("peer_uv", [DEPTH, NE2, 128, 2048])
    dftc = din("dftc", [128, 256])
    EVEN_NEEDED = any(l_ % 2 == 0 for l_ in range(KSTART, KDEPTH))
    dftS = din("dftS", [2, NS, NS] if EVEN_NEEDED else [2, 128, 128])
    dftP = din("dftP", [2, NP, NP])
    lru_wbd = din("lru_wbd", [2, 2, 2, 4, 128, 128])
    din("wkv_w2", [2, 2, 64, 512])
    din("wkv_a2", [2, 2, 64, 512])
    din("wkv_g2", [2, 128, 512])
    din("wkv0", [2, 2, 8, 64, 64])
    cmask_in = din("cmask", [128, 2, 768])
    bones_in = din("bones", [128, 128])

    yT_out = dout("yT", [D, NT])
    lru_out = dout("lru_new", [128, 32])
    wkv_out = dout("wkv_new", [2, 2, 2, 8, 64, 64])

    x_dram = dscr("x_scr", [D, NT])
    z_dram = dscr("z_scr", [ODD_IN, NT])
    y_dram = dscr("y_scr", [D, NT], BF16)
    h2_dram = dscr("h2_scr", [D, NT], BF16)
    sel_dram = nc.dram_tensor("sel_scr", [128, 3, NT], F32, kind="Internal").ap()
    uv_bf = nc.dram_tensor("uv_bf", [DEPTH, 128, 128, 2048], BF16, kind="Internal").ap()
    dftS_bf = nc.dram_tensor("dftS_bf", [2, NS, NS], BF16, kind="Internal").ap()
    scr = {"wk": nc.dram_tensor("wk_scr", [28 * 128, NT], F32, kind="Internal").ap(),
           "ywk": nc.dram_tensor("ywk_scr", [2, 512, NT], F32, kind="Internal").ap()}

    ps = [kb.es.enter_context(nc.psum_tensor("ps%d" % i, [128, 512], F32)) for i in range(8)]
    PS = ["ps%d" % i for i in range(8)]
    pv = kb.sb("pv", [128, pack.n])
    ident = kb.sb("ident", [128, 128])
    identb = kb.sb("identb", [128, 128], BF16)
    iota = kb.sb("iota", [128, 128])
    ones_s = kb.sb("ones_s", [128, 128])
    zer_b = kb.sb("zer_b", [128, 512], BF16)
    modT = kb.sb("modT", [128, DEPTH, 48, 2])
    scv = kb.sb("scv", [128, 8, 2])

    lruo = kb.sb("lruo", [128, 32])
    cmask = kb.sb("cmask", [128, 2, 768])
    bones = kb.sb("bones", [128, 128])
    dv = kb.sb("dv", [128, 96])
    DV = {"cneg": 0, "mu1": 16, "muh": 46, "ka1": 76, "one": 92}
    kb.op("dve", lambda: nc.vector.memset(lruo[:], 0.0), writes=["lruo"])
    kb.dma("sp", cmask[:], cmask_in, writes=["mask4", "maskL"])
    kb.dma("sp", bones[:], bones_in, writes=["bones"])
    kb.dma("sp", pv[:], pv_in, writes=["pv"])
    kb.dma("sp", ident[:], ident_in, writes=["ident"])
    kb.dma("sp", iota[:], iota_in, writes=["iota"])
    kb.dma("sp", scv[:], cv_in, writes=["scv"])
    kb.op("dve", lambda: nc.vector.memset(ones_s[:], 1.0 / D), writes=["ones_s"])
    kb.op("dve", lambda: nc.vector.memset(zer_b[:], 0.0), writes=["zer_b"])
    kb.op("dve", lambda: nc.vector.tensor_copy(out=identb[:], in_=ident[:]), reads=["ident"], writes=["identb"])
    kb.op("act", lambda: nc.scalar.activation(out=scv[:], in_=scv[:], func=AF.Silu), reads=["scv"], writes=["scv"])

    def pcol(name, *idx):
        c = pack.col(name, *idx)
        return pv[:, c:c + 1]

    if not KSKIP_PEER:
        for l in range(KDEPTH):
            for q in range(4):
                kb.dma("pool", uv_bf[l, q * 32:(q + 1) * 32].rearrange("a b c -> (a b) c"),
                       peer_uv[l, q * 32:(q + 1) * 32].rearrange("a b c -> (a b) c"), semkey="castu", max_dma_last_dim=4096)
    for t in range(2 if EVEN_NEEDED else 0):
        for q in range(8):
            kb.dma("pool", dftS_bf[t, q * 512:(q + 1) * 512, :], dftS[t, q * 512:(q + 1) * 512, :],
                   semkey="castd", max_dma_last_dim=4096)

    kb.begin_phase()
    wm = [kb.sb("wm%d" % i, [128, 8, 512]) for i in range(2)]
    n = 0
    for l in range(KDEPTH):
        for g in range(12):
            wt = wm[n % 2]
            wk = "wm%d" % (n % 2)
            kb.dma("sp", wt[:], w_mod[l, :, g * 512:(g + 1) * 512].rearrange("(k p) n -> p k n", p=128), writes=[wk])
            pb = n % 2
            for oc in range(4):
                for k in range(8):
                    kb.op("pe", lambda: nc.tensor.matmul(ps[pb][:, oc * 2:oc * 2 + 2], lhsT=wt[:, k, oc * 128:(oc + 1) * 128],
                                                         rhs=scv[:, k, :], start=(k == 0), stop=(k == 7)),
                          reads=[wk, "scv"], writes=[PS[pb]])
            for oc in range(4):
                occ = g * 4 + oc
                bc = pack.col("b_mod", l, occ, 0)
                add1 = 1.0 if (8 <= occ < 16 or 32 <= occ < 40) else 0.0
                kb.op("dve", lambda: nc.vector.tensor_scalar(out=modT[:, l, occ, :], in0=ps[pb][:, oc * 2:oc * 2 + 2],
                                                             scalar1=pv[:, bc:bc + 1], scalar2=add1, op0=ALU.add, op1=ALU.add),
                      reads=[PS[pb], "pv"], writes=["modT"])
            n += 1
    kb.end_phase()

    def mcol(l, oc, v):
        return modT[:, l, oc, v:v + 1]

    def ln_normalize(xt, xk, sq, sqk, xn, xnk, tmp, tmpk, pA, pB):
        kb.op("act", lambda: nc.scalar.activation(out=sq[:], in_=xt[:], func=AF.Square), reads=[xk], writes=[sqk])
        for k in range(8):
            kb.op("pe", lambda: nc.tensor.matmul(ps[pA][:, 0:TT], lhsT=ones_s[:], rhs=xt[:, k, :], start=(k == 0), stop=(k == 7)),
                  reads=["ones_s", xk], writes=[PS[pA]])
        for k in range(8):
            kb.op("pe", lambda: nc.tensor.matmul(ps[pB][:, 0:TT], lhsT=ones_s[:], rhs=sq[:, k, :], start=(k == 0), stop=(k == 7)),
                  reads=["ones_s", sqk], writes=[PS[pB]])
        mean = tmp[:, 0, :]
        var = tmp[:, 1, :]
        rstd = tmp[:, 2, :]
        nmr = tmp[:, 3, :]
        kb.op("act", lambda: nc.scalar.activation(out=mean, in_=ps[pA][:, 0:TT], func=AF.Copy), reads=[PS[pA]], writes=[tmpk])
        kb.op("dve", lambda: nc.vector.tensor_tensor(out=var, in0=mean, in1=mean, op=ALU.mult), reads=[tmpk], writes=[tmpk])
        kb.op("dve", lambda: nc.vector.tensor_tensor(out=var, in0=ps[pB][:, 0:TT], in1=var, op=ALU.subtract), reads=[PS[pB], tmpk], writes=[tmpk])
        kb.op("act", lambda: nc.scalar.activation(out=var, in_=var, func=AF.Sqrt, bias=pv_eps[:, 0:1], scale=1.0), reads=[tmpk, "pv_eps"], writes=[tmpk])
        kb.op("dve", lambda: nc.vector.reciprocal(out=rstd, in_=var), reads=[tmpk], writes=[tmpk])
        kb.op("dve", lambda: nc.vector.scalar_tensor_tensor(out=nmr, in0=mean, scalar=-1.0, in1=rstd, op0=ALU.mult, op1=ALU.mult),
              reads=[tmpk], writes=[tmpk])
        kb.op("dve", lambda: nc.vector.tensor_tensor(out=xn[:], in0=xt[:], in1=rstd.unsqueeze(1).to_broadcast([128, 8, TT]), op=ALU.mult),
              reads=[xk, tmpk], writes=[xnk])
        kb.op("dve", lambda: nc.vector.tensor_tensor(out=xn[:], in0=xn[:], in1=nmr.unsqueeze(1).to_broadcast([128, 8, TT]), op=ALU.add),
              reads=[xnk, tmpk], writes=[xnk])

    pv_eps = kb.sb("pv_eps", [128, 4])
    kb.op("dve", lambda: nc.vector.memset(pv_eps[:, 0:1], LN_EPS), writes=["pv_eps"])
    kb.op("dve", lambda: nc.vector.memset(pv_eps[:, 1:2], 1.0), writes=["pv_eps"])
    kb.op("dve", lambda: nc.vector.memset(pv_eps[:, 2:3], WKV_GN_EPS), writes=["pv_eps"])
    kb.op("dve", lambda: nc.vector.memset(dv[:, 92:96], 1.0), writes=["dv"])
    lam0 = pack.col("lru_lambda", 0, 0, 0)
    kb.op("act", lambda: nc.scalar.activation(out=dv[:, 0:16], in_=pv[:, lam0:lam0 + 16], func=AF.Exp, scale=-1.0), reads=["pv", "dv"], writes=["dv"])
    kb.op("act", lambda: nc.scalar.activation(out=dv[:, 0:16], in_=dv[:, 0:16], func=AF.Ln, bias=pv_eps[:, 1:2], scale=1.0), reads=["dv", "pv_eps"], writes=["dv"])
    kb.op("dve", lambda: nc.vector.tensor_scalar(out=dv[:, 0:16], in0=dv[:, 0:16], scalar1=-8.0, scalar2=None, op0=ALU.mult), reads=["dv"], writes=["dv"])
    mu0 = pack.col("wkv_mu", 0, 0)
    kb.op("dve", lambda: nc.vector.tensor_scalar(out=dv[:, 16:46], in0=pv[:, mu0:mu0 + 30], scalar1=-1.0, scalar2=1.0, op0=ALU.mult, op1=ALU.add), reads=["pv", "dv"], writes=["dv"])
    kb.op("dve", lambda: nc.vector.tensor_scalar(out=dv[:, 46:76], in0=pv[:, mu0:mu0 + 30], scalar1=0.5, scalar2=None, op0=ALU.mult), reads=["pv", "dv"], writes=["dv"])
    ka0 = pack.col("wkv_ka", 0, 0, 0)
    kb.op("dve", lambda: nc.vector.tensor_scalar(out=dv[:, 76:92], in0=pv[:, ka0:ka0 + 16], scalar1=-1.0, scalar2=1.0, op0=ALU.mult, op1=ALU.add), reads=["pv", "dv"], writes=["dv"])
    cst = {"mask4": cmask[:, :, 0:512], "maskL": cmask[:, :, 512:640], "bones": bones, "eps": pv_eps}

    def load_w_bf16(wt, wk, src, ncols):
        for k in range(8):
            kb.dma("pool", wt[:, k, 0:ncols], src[k * 128:(k + 1) * 128, :], writes=[wk], max_dma_last_dim=4096)

    evac_rr = [0]

    def evac(out_ap, in_ap, reads, writes):
        evac_rr[0] += 1
        if evac_rr[0] % 2:
            kb.op("act", lambda: nc.scalar.activation(out=out_ap, in_=in_ap, func=AF.Copy), reads=reads, writes=writes)
        else:
            kb.op("dve", lambda: nc.vector.tensor_copy(out=out_ap, in_=in_ap), reads=reads, writes=writes)

    for l in range(KSTART, KDEPTH):
        j = l // 2
        even = (l % 2 == 0)
        nin = EVEN_IN if even else ODD_IN
        noc = nin // 128
        x_src = xT_in if l == KSTART else x_dram

        kb.begin_phase()
        win = kb.sb("win", [128, 8, ODD_IN], BF16)
        load_w_bf16(win, "win", (w_in_e if even else w_in_o)[j], nin)
        xa = [kb.sb("A_x%d" % i, [128, 8, TT]) for i in range(2)]
        pa = [kb.sb("A_p%d" % i, [128, 8, TT]) for i in range(2)]
        sqa = kb.sb("A_sq", [128, 8, TT])
        xna = kb.sb("A_xn", [128, 8, TT])
        tmpa = kb.sb("A_tmp", [128, 4, TT])
        ha = [kb.sb("A_h%d" % i, [128, 8, TT], BF16) for i in range(2)]
        zst = [kb.sb("A_z%d" % i, [128, 4, TT]) for i in range(2)]
        zn = 0
        for t in range(NTILES):
            t0 = t * TT
            v = 0 if t < NS // TT else 1
            xt = xa[t % 2]
            xk = "A_x%d" % (t % 2)
            kb.dma("sp", xt[:], x_src[:, t0:t0 + TT].rearrange("(k p) t -> p k t", p=128), writes=[xk])
            if l == 0 and v == 0 and KSTART == 0:
                pt = pa[t % 2]
                pk = "A_p%d" % (t % 2)
                kb.dma("sp", pt[:], posT[:, t0:t0 + TT].rearrange("(k p) t -> p k t", p=128), writes=[pk])
                kb.op("dve", lambda: nc.vector.tensor_tensor(out=xt[:], in0=xt[:], in1=pt[:], op=ALU.add), reads=[xk, pk], writes=[xk])
            if l == KSTART:
                kb.dma("pool", x_dram[:, t0:t0 + TT].rearrange("(k p) t -> p k t", p=128), xt[:], reads=[xk], semkey="A_xst%d" % (t % 2))
            ln_normalize(xt, xk, sqa, "A_sq", xna, "A_xn", tmpa, "A_tmp", 6, 7)
            h = ha[t % 2]
            hk = "A_h%d" % (t % 2)
            for k in range(8):
                kb.op("act", lambda: nc.scalar.activation(out=h[:, k, :], in_=xna[:, k, :], func=AF.Identity,
                                                          scale=mcol(l, 8 + k, v), bias=mcol(l, k, v)),
                      reads=["A_xn", "modT"], writes=[hk])
            for g0 in range(0, noc, 4):
                gn = min(4, noc - g0)
                zs = zst[zn % 2]
                zk = "A_z%d" % (zn % 2)
                for gi in range(gn):
                    oc = g0 + gi
                    pb = (oc // 2) % 4
                    half = oc % 2
                    for k in range(8):
                        kb.op("pe", lambda: nc.tensor.matmul(ps[pb][:, half * TT:(half + 1) * TT], lhsT=win[:, k, oc * 128:(oc + 1) * 128],
                                                             rhs=h[:, k, :], start=(k == 0), stop=(k == 7)),
                              reads=["win", hk], writes=[PS[pb]])
                    evac(zs[:, gi, :], ps[pb][:, half * TT:(half + 1) * TT], [PS[pb]], [zk])
                kb.dma("pool", z_dram[g0 * 128:(g0 + gn) * 128, t0:t0 + TT].rearrange("(g p) t -> p g t", p=128), zs[:, 0:gn, :],
                       reads=[zk], semkey="A_zst%d" % (zn % 2))
                zn += 1
        kb.end_phase()

        if even:
            phase_b_even(kb, nc, ps, PS, pack, pv, j, z_dram, y_dram, dftc, dftS_bf, dftP, evac)
        else:
            phase_b_odd(kb, nc, ps, PS, pack, pv, dv, DV, j, z_dram, y_dram, ins, scr, lruo, wkv_out, evac, ident, cst)

        kb.begin_phase()
        wout = kb.sb("wout", [128, 8, D], BF16)
        load_w_bf16(wout, "wout", (w_out_e if even else w_out_o)[j], D)
        yc = [kb.sb("C_y%d" % i, [128, 8, TT], BF16) for i in range(2)]
        xc = [kb.sb("C_x%d" % i, [128, 8, TT]) for i in range(2)]
        x1p = kb.sb("C_x1p", [128, 8, TT])
        sqc = kb.sb("C_sq", [128, 8, TT])
        xnc = kb.sb("C_xn", [128, 8, TT])
        x1s = [kb.sb("C_x1_%d" % i, [128, 8, TT]) for i in range(2)]
        tmpc = kb.sb("C_tmp", [128, 4, TT])
        h2s = [kb.sb("C_h2_%d" % i, [128, 8, TT], BF16) for i in range(2)]
        for t in range(NTILES):
            t0 = t * TT
            v = 0 if t < NS // TT else 1
            yt = yc[t % 2]
            yk = "C_y%d" % (t % 2)
            xt = xc[t % 2]
            xk = "C_x%d" % (t % 2)
            x1 = x1s[t % 2]
            x1k = "C_x1_%d" % (t % 2)
            h2 = h2s[t % 2]
            h2k = "C_h2_%d" % (t % 2)
            kb.dma("sp", yt[:], y_dram[:, t0:t0 + TT].rearrange("(k p) t -> p k t", p=128), writes=[yk])
            kb.dma("sp", xt[:], x_dram[:, t0:t0 + TT].rearrange("(k p) t -> p k t", p=128), writes=[xk])
            kb.op("act", lambda: nc.scalar.activation(out=xt[:], in_=xt[:], func=AF.Copy, scale=ALPHA), reads=[xk], writes=[xk])
            for oc in range(8):
                pb = 4 + (oc // 2) % 2
                half = oc % 2
                for k in range(8):
                    kb.op("pe", lambda: nc.tensor.matmul(ps[pb][:, half * TT:(half + 1) * TT], lhsT=wout[:, k, oc * 128:(oc + 1) * 128],
                                                         rhs=yt[:, k, :], start=(k == 0), stop=(k == 7)),
                          reads=["wout", yk], writes=[PS[pb]])
                kb.op("dve", lambda: nc.vector.scalar_tensor_tensor(out=x1p[:, oc, :], in0=ps[pb][:, half * TT:(half + 1) * TT],
                                                                    scalar=mcol(l, 16 + oc, v), in1=xt[:, oc, :], op0=ALU.mult, op1=ALU.add),
                      reads=[PS[pb], "modT", xk], writes=["C_x1p"])
            ln_normalize(x1p, "C_x1p", sqc, "C_sq", xnc, "C_xn", tmpc, "C_tmp", 6, 7)
            for k in range(8):
                kb.op("act", lambda: nc.scalar.activation(out=x1[:, k, :], in_=xnc[:, k, :], func=AF.Identity,
                                                          scale=pcol("ln1_g", l, k), bias=pcol("ln1_b", l, k)),
                      reads=["C_xn", "pv"], writes=[x1k])
            kb.dma("pool", x_dram[:, t0:t0 + TT].rearrange("(k p) t -> p k t", p=128), x1[:], reads=[x1k], semkey="C_x1st%d" % (t % 2))
            ln_normalize(x1, x1k, sqc, "C_sq", xnc, "C_xn", tmpc, "C_tmp", 6, 7)
            for k in range(8):
                kb.op("act", lambda: nc.scalar.activation(out=h2[:, k, :], in_=xnc[:, k, :], func=AF.Identity,
                                                          scale=mcol(l, 32 + k, v), bias=mcol(l, 24 + k, v)),
                      reads=["C_xn", "modT"], writes=[h2k])
            kb.dma("pool", h2_dram[:, t0:t0 + TT].rearrange("(k p) t -> p k t", p=128), h2[:], reads=[h2k], semkey="C_h2st%d" % (t % 2))
        kb.end_phase()

        if not KSKIP_PEER:
            peer_select(kb, nc, ps, PS, l, peer_wq, keysT, h2_dram, sel_dram, ident, iota, evac, load_w_bf16)

        kb.begin_phase()
        h2b = [kb.sb("P_h2_%d" % i, [128, 8, TT], BF16) for i in range(2)]
        x1b = [kb.sb("P_x1_%d" % i, [128, 8, TT]) for i in range(2)]
        sqc = kb.sb("P_sq", [128, 8, TT])
        xnc = kb.sb("P_xn", [128, 8, TT])
        tmpc = kb.sb("P_tmp", [128, 4, TT])
        xo = kb.sb("P_xo", [128, 8, TT])
        pst = peer_alloc(kb, nc) if not KSKIP_PEER else None
        for t in range(NTILES):
            t0 = t * TT
            v = 0 if t < NS // TT else 1
            h2 = h2b[t % 2]
            h2k = "P_h2_%d" % (t % 2)
            x1 = x1b[t % 2]
            x1k = "P_x1_%d" % (t % 2)
            kb.dma("sp", x1[:], x_dram[:, t0:t0 + TT].rearrange("(k p) t -> p k t", p=128), writes=[x1k])
            kb.op("act", lambda: nc.scalar.activation(out=x1[:], in_=x1[:], func=AF.Copy, scale=ALPHA), reads=[x1k], writes=[x1k])
            if not KSKIP_PEER:
                kb.dma("sp", h2[:], h2_dram[:, t0:t0 + TT].rearrange("(k p) t -> p k t", p=128), writes=[h2k])
                peer_mix(kb, nc, ps, PS, pst, l, t, h2, h2k, uv_bf, sel_dram, iota, zer_b, evac)
                for oc in range(8):
                    pb = oc // 2
                    half = oc % 2
                    kb.op("dve", lambda: nc.vector.scalar_tensor_tensor(out=x1[:, oc, :], in0=ps[pb][:, half * TT:(half + 1) * TT],
                                                                        scalar=mcol(l, 40 + oc, v), in1=x1[:, oc, :], op0=ALU.mult, op1=ALU.add),
                          reads=[PS[pb], "modT", x1k], writes=[x1k])
            ln_normalize(x1, x1k, sqc, "P_sq", xnc, "P_xn", tmpc, "P_tmp", 6, 7)
            for k in range(8):
                kb.op("act", lambda: nc.scalar.activation(out=xo[:, k, :], in_=xnc[:, k, :], func=AF.Identity,
                                                          scale=pcol("ln2_g", l, k), bias=pcol("ln2_b", l, k)),
                      reads=["P_xn", "pv"], writes=["P_xo"])
            last = (l == KDEPTH - 1)
            dst = yT_out if last else x_dram
            kb.dma("pool", dst[:, t0:t0 + TT].rearrange("(k p) t -> p k t", p=128), xo[:], reads=["P_xo"],
                   semkey="P_xst", is_output=last)
        kb.end_phase()

    kb.dma("pool", lru_out, lruo[:], reads=["lruo"], semkey="lruost", is_output=True)
    if KDEBUG:
        kb.barrier()
        for (name, ap, shape, dt) in scr_list:
            d = nc.dram_tensor("dbg_" + name, [shape[0], 512], dt, kind="ExternalOutput").ap()
            kb.dma("sp", d[:, 0:256], ap[:, 0:256], semkey="dbgtap", is_output=True)
            kb.dma("sp", d[:, 256:512], ap[:, NS:NS + 256], semkey="dbgtap", is_output=True)
    kb.finish()
    return kb


def _dbg_tap(kb, nc, name, tile, key, dt):
    shp = list(tile.shape)
    d = nc.dram_tensor(name, shp, dt, kind="ExternalOutput").ap()
    kb.dma("pool", d, tile[:], reads=[key], semkey=name, is_output=True)


def phase_b_even(kb, nc, ps, PS, pack, pv, j, z_dram, y_dram, dftc, dftS_bf, dftP, evac):
    NTB = NT // 128
    pqstack = ExitStack()
    pq = pqstack.enter_context(nc.sbuf_tensor("E_pq_j%d" % j, [128, 4, NTB, 256], BF16))
    ccb = pqstack.enter_context(nc.sbuf_tensor("E_ccb_j%d" % j, [128, 256], BF16))
    tpb = pqstack.enter_context(nc.sbuf_tensor("E_tpb_j%d" % j, [128, 2, 2, NP], BF16))
    kb.begin_phase()
    cc32 = kb.sb("E_cc32", [128, 256])
    kb.dma("sp", cc32[:], dftc, writes=["E_cc32"])
    kb.op("dve", lambda: nc.vector.tensor_copy(out=ccb[:], in_=cc32[:]), reads=["E_cc32"], writes=["E_ccb"])
    tp32 = kb.sb("E_tp32", [128, 2, 2, NP])
    kb.dma("sp", tp32[:], dftP.rearrange("t (b p) s -> p t b s", p=128), writes=["E_tp32"])
    kb.op("dve", lambda: nc.vector.tensor_copy(out=tpb[:], in_=tp32[:]), reads=["E_tp32"], writes=["E_tpb"])
    a32 = [kb.sb("E_a32_%d" % i, [128, NT]) for i in range(2)]
    ab = [kb.sb("E_ab_%d" % i, [128, NT], BF16) for i in range(2)]
    for g in range(4):
        at = a32[g % 2]
        ak = "E_a32_%d" % (g % 2)
        bt = ab[g % 2]
        bk = "E_ab_%d" % (g % 2)
        kb.dma("sp", at[:], z_dram[g * 128:(g + 1) * 128, :], writes=[ak])
        kb.op("act", lambda: nc.scalar.activation(out=bt[:], in_=at[:], func=AF.Copy), reads=[ak], writes=[bk])
        for tb in range(NTB):
            pb = tb % 4
            kb.op("pe", lambda: nc.tensor.matmul(ps[pb][:, 0:256], lhsT=bt[:, tb * 128:(tb + 1) * 128], rhs=ccb[:], start=True, stop=True),
                  reads=[bk, "E_ccb"], writes=[PS[pb]])
            evac(pq[:, g, tb, :], ps[pb][:, 0:256], [PS[pb]], ["E_pq%d" % g])
    kb.end_phase()
    kb.begin_phase()
    tab = [kb.sb("E_tab%d" % i, [128, 2, 32, 256], BF16) for i in range(2)]
    yst = [kb.sb("E_y%d" % i, [128, 4, 256], BF16) for i in range(2)]
    for sb_ in range(NS // 256):
        tt_ = tab[sb_ % 2]
        tk = "E_tab%d" % (sb_ % 2)
        for tbl in range(2):
            kb.dma("sp", tt_[:, tbl, :, :], dftS_bf[tbl, :, sb_ * 256:(sb_ + 1) * 256].rearrange("(b p) s -> p b s", p=128), writes=[tk])
        ys = yst[sb_ % 2]
        ysk = "E_y%d" % (sb_ % 2)
        for g in range(4):
            pb = 4 + g % 2
            for tb in range(32):
                for tbl in range(2):
                    kb.op("pe", lambda: nc.tensor.matmul(ps[pb][:, 0:256], lhsT=pq[:, g, tb, tbl * 128:(tbl + 1) * 128], rhs=tt_[:, tbl, tb, :],
                                                         start=(tb == 0 and tbl == 0), stop=(tb == 31 and tbl == 1)),
                          reads=["E_pq%d" % g, tk], writes=[PS[pb]])
            evac(ys[:, g, :], ps[pb][:, 0:256], [PS[pb]], [ysk])
        kb.dma("pool", y_dram[0:512, sb_ * 256:(sb_ + 1) * 256].rearrange("(g p) t -> p g t", p=128), ys[:], reads=[ysk], semkey="E_yst%d" % (sb_ % 2))
    for pi in range(2):
        ys = yst[pi % 2]
        ysk = "E_y%d" % (pi % 2)
        for g in range(4):
            pb = 4 + g % 2
            for tb in range(2):
                for tbl in range(2):
                    kb.op("pe", lambda: nc.tensor.matmul(ps[pb][:, 0:256], lhsT=pq[:, g, 32 + 2 * pi + tb, tbl * 128:(tbl + 1) * 128],
                                                         rhs=tpb[:, tbl, tb, :], start=(tb == 0 and tbl == 0), stop=(tb == 1 and tbl == 1)),
                          reads=["E_pq%d" % g, "E_tpb"], writes=[PS[pb]])
            evac(ys[:, g, :], ps[pb][:, 0:256], [PS[pb]], [ysk])
        c0 = NS + pi * NP
        kb.dma("pool", y_dram[0:512, c0:c0 + NP].rearrange("(g p) t -> p g t", p=128), ys[:], reads=[ysk], semkey="E_yst%d" % (pi % 2))
    kb.end_phase()
    pqstack.close()

    kb.begin_phase()
    bg = kb.sb("S_bg", [128, NT])
    cg = kb.sb("S_cg", [128, NT])
    xi = kb.sb("S_xi", [128, NT])
    yy = kb.sb("S_y", [128, NT])
    yb = kb.sb("S_yb", [128, NT], BF16)
    for ct in range(4):
        kb.dma("sp", bg[:], z_dram[512 + ct * 128:512 + (ct + 1) * 128, :], writes=["S_bg"])
        kb.dma("sp", cg[:], z_dram[1024 + ct * 128:1024 + (ct + 1) * 128, :], writes=["S_cg"])
        kb.dma("sp", xi[:], z_dram[1536 + ct * 128:1536 + (ct + 1) * 128, :], writes=["S_xi"])
        kb.op("dve", lambda: nc.vector.tensor_tensor(out=xi[:], in0=xi[:], in1=cg[:], op=ALU.mult), reads=["S_xi", "S_cg"], writes=["S_xi"])
        w0 = pack.col("sconv_w", j, 0, ct)
        w1 = pack.col("sconv_w", j, 1, ct)
        w2 = pack.col("sconv_w", j, 2, ct)
        bb = pack.col("sconv_b", j, ct)
        kb.op("act", lambda: nc.scalar.activation(out=yy[:], in_=xi[:], func=AF.Identity, scale=pv[:, w1:w1 + 1], bias=pv[:, bb:bb + 1]),
              reads=["S_xi", "pv"], writes=["S_y"])
        for (s0, s1) in SEQS:
            kb.op("dve", lambda: nc.vector.scalar_tensor_tensor(out=yy[:, s0 + 1:s1], in0=xi[:, s0:s1 - 1], scalar=pv[:, w0:w0 + 1],
                                                                in1=yy[:, s0 + 1:s1], op0=ALU.mult, op1=ALU.add),
                  reads=["S_xi", "S_y", "pv"], writes=["S_y"])
            kb.op("dve", lambda: nc.vector.scalar_tensor_tensor(out=yy[:, s0:s1 - 1], in0=xi[:, s0 + 1:s1], scalar=pv[:, w2:w2 + 1],
                                                                in1=yy[:, s0:s1 - 1], op0=ALU.mult, op1=ALU.add),
                  reads=["S_xi", "S_y", "pv"], writes=["S_y"])
        kb.op("dve", lambda: nc.vector.tensor_tensor(out=yb[:], in0=yy[:], in1=bg[:], op=ALU.mult), reads=["S_y", "S_bg"], writes=["S_yb"])
        kb.dma("pool", y_dram[512 + ct * 128:512 + (ct + 1) * 128, :], yb[:], reads=["S_yb"], semkey="S_yst")
    kb.end_phase()


def phase_b_odd(kb, nc, ps, PS, pack, pv, dv, DV, j, z_dram, y_dram, ins, scr, lruo, wkv_out, evac, ident, cst):
    if KODD & 1:
        lru_part(kb, nc, ps, PS, pack, pv, dv, DV, j, z_dram, y_dram, ins["lru_wbd"], lruo)
    if KODD & 2:
        rwkv_prep(kb, nc, ps, PS, pack, pv, dv, DV, j, z_dram, ins, scr, evac)
    if KODD & 4:
        rwkv_scan(kb, nc, ps, PS, pack, pv, dv, DV, j, z_dram, ins, scr, wkv_out, evac, ident, cst)
    if KODD & 8:
        rwkv_post(kb, nc, ps, PS, pack, pv, j, y_dram, scr, evac, ident, cst)


def lru_part(kb, nc, ps, PS, pack, pv, dv, DV, j, z_dram, y_dram, lru_wbd, lruo):
    kb.begin_phase()
    w32 = kb.sb("L_w32", [128, 16, 128])
    wb = kb.sb("L_wb", [128, 16, 128], BF16)
    kb.dma("sp", w32[:], lru_wbd[j].rearrange("d a c i o -> i (d a c) o"), writes=["L_w32"])
    kb.op("dve", lambda: nc.vector.tensor_copy(out=wb[:], in_=w32[:]), reads=["L_w32"], writes=["L_wb"])
    xb = kb.sb("L_xb", [128, NT])
    gb = kb.sb("L_gb", [128, NT])
    xc = kb.sb("L_xc", [128, NT])
    xcb = kb.sb("L_xcb", [128, NT], BF16)
    T2 = kb.sb("L_T2", [128, NT])
    T3 = kb.sb("L_T3", [128, NT])
    T4 = kb.sb("L_T4", [128, NT])
    ys = kb.sb("L_ys", [128, NT])
    yb = kb.sb("L_yb", [128, NT], BF16)
    for ct in range(4):
        kb.dma("sp", xb[:], z_dram[ct * 128:(ct + 1) * 128, :], writes=["L_xb"])
        kb.dma("sp", gb[:], z_dram[512 + ct * 128:512 + (ct + 1) * 128, :], writes=["L_gb"])
        for d in range(2):
            cw = [pack.col("lru_conv_w", j, d, tap, ct) for tap in range(4)]
            cb = pack.col("lru_conv_b", j, d, ct)
            kb.op("act", lambda: nc.scalar.activation(out=xc[:], in_=xb[:], func=AF.Identity, scale=pv[:, cw[3]:cw[3] + 1], bias=pv[:, cb:cb + 1]),
                  reads=["L_xb", "pv"], writes=["L_xc"])
            for tap in range(3):
                sh = 3 - tap
                for (s0, s1) in SEQS:
                    if d == 0:
                        o_, i_ = xc[:, s0 + sh:s1], xb[:, s0:s1 - sh]
                    else:
                        o_, i_ = xc[:, s0:s1 - sh], xb[:, s0 + sh:s1]
                    kb.op("dve", lambda: nc.vector.scalar_tensor_tensor(out=o_, in0=i_, scalar=pv[:, cw[tap]:cw[tap] + 1], in1=o_, op0=ALU.mult, op1=ALU.add),
                          reads=["L_xb", "L_xc", "pv"], writes=["L_xc"])
            kb.op("act", lambda: nc.scalar.activation(out=xcb[:], in_=xc[:], func=AF.Copy), reads=["L_xc"], writes=["L_xcb"])
            ba = pack.col("lru_ba", j, d, ct)
            bx = pack.col("lru_bx", j, d, ct)
            for blk in range(NT // 512):
                c0 = blk * 512
                for ai, (dst, dk, bcol) in enumerate(((T2, "L_T2", ba), (T3, "L_T3", bx))):
                    pb = (2 * blk + ai) % 4
                    kb.op("pe", lambda: nc.tensor.matmul(ps[pb][:, 0:512], lhsT=wb[:, (d * 2 + ai) * 4 + ct, :], rhs=xcb[:, c0:c0 + 512], start=True, stop=True),
                          reads=["L_wb", "L_xcb"], writes=[PS[pb]])
                    kb.op("act", lambda: nc.scalar.activation(out=dst[:, c0:c0 + 512], in_=ps[pb][:, 0:512], func=AF.Sigmoid, bias=pv[:, bcol:bcol + 1], scale=1.0),
                          reads=[PS[pb], "pv"], writes=[dk])
            cn = DV["cneg"] + (j * 2 + d) * 4 + ct
            kb.op("act", lambda: nc.scalar.activation(out=T2[:], in_=T2[:], func=AF.Exp, scale=dv[:, cn:cn + 1]), reads=["L_T2", "dv"], writes=["L_T2"])
            kb.op("dve", lambda: nc.vector.tensor_tensor(out=T4[:], in0=T2[:], in1=T2[:], op=ALU.mult), reads=["L_T2"], writes=["L_T4"])
            kb.op("act", lambda: nc.scalar.activation(out=T4[:], in_=T4[:], func=AF.Sqrt, scale=-1.0, bias=dv[:, DV["one"]:DV["one"] + 1]),
                  reads=["L_T4", "dv"], writes=["L_T4"])
            kb.op("dve", lambda: nc.vector.tensor_tensor(out=T3[:], in0=T3[:], in1=xc[:], op=ALU.mult), reads=["L_T3", "L_xc"], writes=["L_T3"])
            kb.op("dve", lambda: nc.vector.tensor_tensor(out=T3[:], in0=T3[:], in1=T4[:], op=ALU.mult), reads=["L_T3", "L_T4"], writes=["L_T3"])
            for si, (s0, s1) in enumerate(SEQS):
                if si == 0:
                    hc = pack.col("state_lru", j, d, ct)
                    init = pv[:, hc:hc + 1]
                else:
                    init = 0.0
                if d == 0:
                    o_, a_, b_ = T4[:, s0:s1], T2[:, s0:s1], T3[:, s0:s1]
                else:
                    o_, a_, b_ = T4[:, s0:s1][:, ::-1], T2[:, s0:s1][:, ::-1], T3[:, s0:s1][:, ::-1]
                kb.op("dve", lambda: nc.vector.tensor_tensor_scan(out=o_, data0=a_, data1=b_, initial=init, op0=ALU.mult, op1=ALU.add),
                      reads=["L_T2", "L_T3", "pv"], writes=["L_T4"])
                if si > 0:
                    col = (si - 1) * 16 + j * 8 + d * 4 + ct
                    fin = T4[:, s1 - 1:s1] if d == 0 else T4[:, s0:s0 + 1]
                    kb.op("act", lambda: nc.scalar.activation(out=lruo[:, col:col + 1], in_=fin, func=AF.Copy), reads=["L_T4"], writes=["lruo"])
            if d == 0:
                kb.op("act", lambda: nc.scalar.activation(out=ys[:], in_=T4[:], func=AF.Copy), reads=["L_T4"], writes=["L_ys"])
            else:
                kb.op("dve", lambda: nc.vector.tensor_tensor(out=ys[:], in0=ys[:], in1=T4[:], op=ALU.add), reads=["L_ys", "L_T4"], writes=["L_ys"])
        kb.op("act", lambda: nc.scalar.activation(out=gb[:], in_=gb[:], func=AF.Gelu_apprx_tanh), reads=["L_gb"], writes=["L_gb"])
        kb.op("dve", lambda: nc.vector.tensor_tensor(out=yb[:], in0=ys[:], in1=gb[:], op=ALU.mult), reads=["L_ys", "L_gb"], writes=["L_yb"])
        kb.dma("pool", y_dram[ct * 128:(ct + 1) * 128, :], yb[:], reads=["L_yb"], semkey="L_yst")
    kb.end_phase()


WK0 = 1024
PADW = NT + 6
PSEQ = [(1, 0, NS), (NS + 3, NS, NS + NP), (NS + NP + 5, NS + NP, NT)]


def rwkv_prep(kb, nc, ps, PS, pack, pv, dv, DV, j, z_dram, ins, scr, evac):
    wk = scr["wk"]
    kb.begin_phase()
    zp = [kb.sb("W_zp%d" % i, [128, PADW]) for i in range(2)]
    sm = kb.sb("W_sm", [128, PADW])
    zo = [kb.sb("W_zo%d" % i, [128, PADW]) for i in range(2)]
    for i in range(2):
        kb.op("dve", lambda: nc.vector.memset(zp[i][:], 0.0), writes=["W_zp%d" % i])
    for q in range(15):
        z_ = zp[q % 2]
        zk = "W_zp%d" % (q % 2)
        o_ = zo[q % 2]
        ok = "W_zo%d" % (q % 2)
        r0 = WK0 + q * 128
        for (p0, s0, s1) in PSEQ:
            kb.dma("sp", z_[:, p0:p0 + (s1 - s0)], z_dram[r0:r0 + 128, s0:s1], writes=[zk])
        kb.op("dve", lambda: nc.vector.tensor_tensor(out=sm[:, 0:PADW - 2], in0=z_[:, 0:PADW - 2], in1=z_[:, 2:PADW], op=ALU.add), reads=[zk], writes=["W_sm"])
        m1 = DV["mu1"] + j * 15 + q
        mh = DV["muh"] + j * 15 + q
        kb.op("act", lambda: nc.scalar.activation(out=o_[:, 0:PADW - 2], in_=z_[:, 1:PADW - 1], func=AF.Copy, scale=dv[:, m1:m1 + 1]),
              reads=[zk, "dv"], writes=[ok])
        kb.op("dve", lambda: nc.vector.scalar_tensor_tensor(out=o_[:, 0:PADW - 2], in0=sm[:, 0:PADW - 2], scalar=dv[:, mh:mh + 1], in1=o_[:, 0:PADW - 2],
                                                            op0=ALU.mult, op1=ALU.add), reads=["W_sm", ok, "dv"], writes=[ok])
        for (p0, s0, s1) in PSEQ:
            kb.dma("pool", z_dram[r0:r0 + 128, s0:s1], o_[:, p0 - 1:p0 - 1 + (s1 - s0)], reads=[ok], semkey="W_zst%d" % (q % 2))
    kb.end_phase()

    kb.begin_phase()
    w32 = kb.sb("W_w32", [128, 3, 512])
    wlr = kb.sb("W_wlr", [128, 3, 512], BF16)
    kb.dma("sp", w32[:, 0, :], ins["wkv_w2"][j].rearrange("d r c -> (d r) c"), writes=["W_w32"])
    kb.dma("sp", w32[:, 1, :], ins["wkv_a2"][j].rearrange("d r c -> (d r) c"), writes=["W_w32"])
    kb.dma("sp", w32[:, 2, :], ins["wkv_g2"][j], writes=["W_w32"])
    kb.op("dve", lambda: nc.vector.tensor_copy(out=wlr[:], in_=w32[:]), reads=["W_w32"], writes=["W_wlr"])
    lr32 = kb.sb("W_lr32", [128, NT])
    lrb = [kb.sb("W_lrb%d" % i, [128, NT], BF16) for i in range(3)]
    for i, (fn, r0) in enumerate(((AF.Tanh, WK0 + 1536), (AF.Copy, WK0 + 1664), (AF.Sigmoid, WK0 + 1792))):
        kb.dma("sp", lr32[:], z_dram[r0:r0 + 128, :], writes=["W_lr32"])
        kb.op("act", lambda: nc.scalar.activation(out=lrb[i][:], in_=lr32[:], func=fn), reads=["W_lr32"], writes=["W_lrb%d" % i])
    ot = [kb.sb("W_ot%d" % i, [128, NT]) for i in range(2)]
    n = 0
    for kind in range(3):
        for d in range(2 if kind < 2 else 1):
            for ct in range(4):
                o_ = ot[n % 2]
                ok = "W_ot%d" % (n % 2)
                for blk in range(NT // 512):
                    c0 = blk * 512
                    pb = blk % 4
                    if kind < 2:
                        lhs = wlr[d * 64:(d + 1) * 64, kind, ct * 128:(ct + 1) * 128]
                        rhs = lrb[kind][d * 64:(d + 1) * 64, c0:c0 + 512]
                    else:
                        lhs = wlr[:, 2, ct * 128:(ct + 1) * 128]
                        rhs = lrb[2][:, c0:c0 + 512]
                    kb.op("pe", lambda: nc.tensor.matmul(ps[pb][:, 0:512], lhsT=lhs, rhs=rhs, start=True, stop=True),
                          reads=["W_wlr", "W_lrb%d" % kind], writes=[PS[pb]])
                    if kind == 0:
                        bc = pack.col("wkv_w0", j, d, ct)
                        kb.op("act", lambda: nc.scalar.activation(out=o_[:, c0:c0 + 512], in_=ps[pb][:, 0:512], func=AF.Sigmoid, bias=pv[:, bc:bc + 1], scale=1.0),
                              reads=[PS[pb], "pv"], writes=[ok])
                    elif kind == 1:
                        bc = pack.col("wkv_a0", j, d, ct)
                        kb.op("act", lambda: nc.scalar.activation(out=o_[:, c0:c0 + 512], in_=ps[pb][:, 0:512], func=AF.Sigmoid, bias=pv[:, bc:bc + 1], scale=1.0),
                              reads=[PS[pb], "pv"], writes=[ok])
                    else:
                        evac(o_[:, c0:c0 + 512], ps[pb][:, 0:512], [PS[pb]], [ok])
                if kind == 0:
                    kb.op("dve", lambda: nc.vector.tensor_scalar(out=o_[:], in0=o_[:], scalar1=-DECAY_SCALE, scalar2=None, op0=ALU.mult), reads=[ok], writes=[ok])
                    row = (d * 4 + ct) * 128
                elif kind == 1:
                    row = (8 + d * 4 + ct) * 128
                else:
                    row = (16 + ct) * 128
                kb.dma("pool", wk[row:row + 128, :], o_[:], reads=[ok], semkey="W_ost%d" % (n % 2))
                n += 1
    kb.end_phase()


SEGS = [(0, 2048), (2048, 4096), (4096, 4608)]
CH = 128


def rwkv_scan(kb, nc, ps, PS, pack, pv, dv, DV, j, z_dram, ins, scr, wkv_out, evac, ident, cst):
    wk = scr["wk"]
    ywk = scr["ywk"]
    kb.begin_phase()
    SW = 2048
    T = [kb.sb("R_T%d" % i, [128, SW]) for i in range(10)]
    TK = ["R_T%d" % i for i in range(10)]
    gC = kb.sb("R_gC", [128, SW // CH])
    maskf = kb.sb("R_maskf", [128, SW])
    maskb = kb.sb("R_maskb", [128, SW])
    kb.op("dve", lambda: nc.vector.memset(maskf[:], 1.0), writes=["R_maskf"])
    kb.op("dve", lambda: nc.vector.memset(maskf[:].rearrange("p (c t) -> p c t", t=CH)[:, :, 0:1], 0.0), writes=["R_maskf"])
    kb.op("dve", lambda: nc.vector.memset(maskb[:], 1.0), writes=["R_maskb"])
    kb.op("dve", lambda: nc.vector.memset(maskb[:].rearrange("p (c t) -> p c t", t=CH)[:, :, CH - 1:CH], 0.0), writes=["R_maskb"])
    Mst = kb.sb("R_M", [128, 3, 64])
    tok = kb.sb("R_tok", [128, 4, 128])
    XT = [kb.sb("R_XT%d" % i, [128, 512]) for i in range(2)]
    Lm = [kb.sb("R_L%d" % i, [128, 128]) for i in range(2)]
    LT = [kb.sb("R_LT%d" % i, [128, 128]) for i in range(2)]
    X = [kb.sb("R_X%d" % i, [128, 128]) for i in range(2)]
    PT = kb.sb("R_PT", [128, 64])
    Qt = kb.sb("R_Q", [128, 64])
    GT = kb.sb("R_GT", [128, 128])
    ytok = [kb.sb("R_y%d" % i, [128, 128]) for i in range(2)]
    ytT = [kb.sb("R_yT%d" % i, [128, 128]) for i in range(2)]
    mask4 = cst["mask4"]
    maskL = cst["maskL"]
    bones = cst["bones"]
    yn = 0
    for ct in range(4):
        for d in range(2):
            kkc = pack.col("wkv_kk", j, d, ct)
            kac = pack.col("wkv_ka", j, d, ct)
            ka1 = DV["ka1"] + (j * 2 + d) * 4 + ct
            rkc = pack.col("wkv_rk", j, ct)
            for hh in range(2):
                kb.dma("sp", Mst[hh * 64:(hh + 1) * 64, 0, :], ins["wkv0"][j, d, 2 * ct + hh], writes=["R_M%d" % hh])
                kb.op("dve", lambda: nc.vector.memset(Mst[hh * 64:(hh + 1) * 64, 1:3, :], 0.0), writes=["R_M%d" % hh])
            segs = SEGS if d == 0 else [SEGS[1], SEGS[0], SEGS[2]]
            for (g0, g1) in segs:
                W = g1 - g0
                nch = W // CH
                v_ = lambda i: T[i][:, 0:W]
                kb.dma("sp", v_(0), z_dram[WK0 + 512 + ct * 128:WK0 + 512 + (ct + 1) * 128, g0:g1], writes=[TK[0]])
                kb.dma("sp", v_(1), z_dram[WK0 + ct * 128:WK0 + (ct + 1) * 128, g0:g1], writes=[TK[1]])
                kb.dma("sp", v_(2), z_dram[WK0 + 1024 + ct * 128:WK0 + 1024 + (ct + 1) * 128, g0:g1], writes=[TK[2]])
                kb.dma("sp", v_(3), wk[(d * 4 + ct) * 128:(d * 4 + ct + 1) * 128, g0:g1], writes=[TK[3]])
                kb.dma("sp", v_(4), wk[(8 + d * 4 + ct) * 128:(8 + d * 4 + ct + 1) * 128, g0:g1], writes=[TK[4]])
                kb.op("act", lambda: nc.scalar.activation(out=v_(5), in_=v_(0), func=AF.Copy, scale=pv[:, kkc:kkc + 1]), reads=[TK[0], "pv"], writes=[TK[5]])
                kb.op("dve", lambda: nc.vector.tensor_tensor(out=v_(6), in0=v_(5), in1=v_(5), op=ALU.mult), reads=[TK[5]], writes=[TK[6]])
                for blk in range(W // 512):
                    c0 = blk * 512
                    pb = blk % 4
                    kb.op("pe", lambda: nc.tensor.matmul(ps[pb][:, 0:512], lhsT=bones[:], rhs=T[6][:, c0:c0 + 512], start=True, stop=True),
                          reads=["bones", TK[6]], writes=[PS[pb]])
                    kb.op("dve", lambda: nc.vector.tensor_scalar(out=T[7][:, c0:c0 + 512], in0=ps[pb][:, 0:512], scalar1=1e-24, scalar2=None, op0=ALU.max),
                          reads=[PS[pb]], writes=[TK[7]])
                kb.op("act", lambda: nc.scalar.activation(out=v_(7), in_=v_(7), func=AF.Sqrt), reads=[TK[7]], writes=[TK[7]])
                kb.op("dve", lambda: nc.vector.reciprocal(out=v_(7), in_=v_(7)), reads=[TK[7]], writes=[TK[7]])
                kb.op("dve", lambda: nc.vector.tensor_tensor(out=v_(5), in0=v_(5), in1=v_(7), op=ALU.mult), reads=[TK[5], TK[7]], writes=[TK[5]])
                kb.op("act", lambda: nc.scalar.activation(out=v_(6), in_=v_(4), func=AF.Identity, scale=pv[:, kac:kac + 1], bias=dv[:, ka1:ka1 + 1]),
                      reads=[TK[4], "pv", "dv"], writes=[TK[6]])
                kb.op("dve", lambda: nc.vector.tensor_tensor(out=v_(0), in0=v_(0), in1=v_(6), op=ALU.mult), reads=[TK[0], TK[6]], writes=[TK[0]])
                kb.op("dve", lambda: nc.vector.scalar_tensor_tensor(out=v_(6), in0=v_(1), scalar=pv[:, rkc:rkc + 1], in1=v_(0), op0=ALU.mult, op1=ALU.mult),
                      reads=[TK[1], TK[0], "pv"], writes=[TK[6]])
                for blk in range(W // 512):
                    c0 = blk * 512
                    pb = blk % 4
                    kb.op("pe", lambda: nc.tensor.matmul(ps[pb][:, 0:512], lhsT=bones[:], rhs=T[6][:, c0:c0 + 512], start=True, stop=True),
                          reads=["bones", TK[6]], writes=[PS[pb]])
                    kb.op("dve", lambda: nc.vector.tensor_tensor(out=T[7][:, c0:c0 + 512], in0=ps[pb][:, 0:512], in1=T[2][:, c0:c0 + 512], op=ALU.mult),
                          reads=[PS[pb], TK[2]], writes=[TK[7]])
                kb.dma("pool", wk[(20 + d * 4 + ct) * 128:(20 + d * 4 + ct + 1) * 128, g0:g1], v_(7), reads=[TK[7]], semkey="R_bst")
                if d == 0:
                    kb.op("dve", lambda: nc.vector.tensor_tensor_scan(out=v_(6), data0=maskf[:, 0:W], data1=v_(3), initial=0.0, op0=ALU.mult, op1=ALU.add),
                          reads=["R_maskf", TK[3]], writes=[TK[6]])
                else:
                    kb.op("dve", lambda: nc.vector.tensor_tensor_scan(out=v_(6)[:, ::-1], data0=maskb[:, 0:W][:, ::-1], data1=v_(3)[:, ::-1], initial=0.0,
                                                                      op0=ALU.mult, op1=ALU.add), reads=["R_maskb", TK[3]], writes=[TK[6]])
                cum3 = T[6][:, 0:W].rearrange("p (c t) -> p c t", t=CH)
                endc = cum3[:, :, CH - 1:CH] if d == 0 else cum3[:, :, 0:1]
                kb.op("act", lambda: nc.scalar.activation(out=v_(7), in_=v_(6), func=AF.Exp), reads=[TK[6]], writes=[TK[7]])
                kb.op("dve", lambda: nc.vector.tensor_tensor(out=v_(1), in0=v_(1), in1=v_(7), op=ALU.mult), reads=[TK[1], TK[7]], writes=[TK[1]])
                e3 = T[7][:, 0:W].rearrange("p (c t) -> p c t", t=CH)
                ende = e3[:, :, CH - 1:CH] if d == 0 else e3[:, :, 0:1]
                kb.op("dve", lambda: nc.vector.tensor_copy(out=gC[:, 0:nch].unsqueeze(2), in_=ende), reads=[TK[7]], writes=["R_gC"])
                kb.op("dve", lambda: nc.vector.tensor_tensor(out=v_(7), in0=v_(6), in1=v_(3), op=ALU.subtract), reads=[TK[6], TK[3], "R_gC"], writes=[TK[7]])
                kb.op("act", lambda: nc.scalar.activation(out=v_(7), in_=v_(7), func=AF.Exp), reads=[TK[7]], writes=[TK[7]])
                kb.op("dve", lambda: nc.vector.scalar_tensor_tensor(out=v_(8), in0=v_(5), scalar=-1.0, in1=v_(7), op0=ALU.mult, op1=ALU.mult),
                      reads=[TK[5], TK[7]], writes=[TK[8]])
                kb.op("act", lambda: nc.scalar.activation(out=v_(7), in_=v_(6), func=AF.Exp, scale=-1.0), reads=[TK[6], TK[8]], writes=[TK[7]])
                kb.op("dve", lambda: nc.vector.tensor_tensor(out=v_(9), in0=v_(4), in1=v_(5), op=ALU.mult), reads=[TK[4], TK[5]], writes=[TK[9]])
                kb.op("dve", lambda: nc.vector.tensor_tensor(out=v_(3), in0=v_(9), in1=v_(7), op=ALU.mult), reads=[TK[9], TK[7]], writes=[TK[3]])
                kb.op("dve", lambda: nc.vector.tensor_tensor(out=v_(4), in0=v_(0), in1=v_(7), op=ALU.mult), reads=[TK[0], TK[7]], writes=[TK[4]])
                kb.op("dve", lambda: nc.vector.tensor_tensor(out=e3, in0=endc.to_broadcast([128, nch, CH]), in1=cum3, op=ALU.subtract),
                      reads=[TK[6], TK[3], TK[4]], writes=[TK[7]])
                kb.op("act", lambda: nc.scalar.activation(out=v_(7), in_=v_(7), func=AF.Exp), reads=[TK[7]], writes=[TK[7]])
                kb.op("dve", lambda: nc.vector.tensor_tensor(out=v_(9), in0=v_(9), in1=v_(7), op=ALU.mult), reads=[TK[9], TK[7]], writes=[TK[9]])
                kb.op("dve", lambda: nc.vector.tensor_tensor(out=v_(0), in0=v_(0), in1=v_(7), op=ALU.mult), reads=[TK[0], TK[7]], writes=[TK[0]])
                aF, bF, kF, rF, beF, keF, vF = T[8], T[3], T[4], T[1], T[9], T[0], T[2]
                aK, bK, kK, rK, beK, keK, vK = TK[8], TK[3], TK[4], TK[1], TK[9], TK[0], TK[2]
                corder = range(nch) if d == 0 else range(nch - 1, -1, -1)
                for c in corder:
                    sl = slice(c * CH, (c + 1) * CH)
                    tglob = g0 + c * CH
                    si = 0 if tglob < NS else (1 if tglob < NS + NP else 2)
                    for qi, (src, sk) in enumerate(((aF, aK), (vF, vK), (beF, beK), (keF, keK))):
                        kb.op("pe", lambda: nc.tensor.transpose(out=ps[6][:, qi * 128:(qi + 1) * 128], in_=src[:, sl], identity=ident[:]),
                              reads=[sk, "ident"], writes=[PS[6]])
                    evac(tok[:].rearrange("p q c -> p (q c)"), ps[6][:, 0:512], [PS[6]], ["R_tok"])
                    yt_ = ytok[yn % 2]
                    ytk = "R_y%d" % (yn % 2)
                    yn += 1
                    for hh in range(2):
                        hp = slice(hh * 64, (hh + 1) * 64)
                        hc = slice(hh * 64, (hh + 1) * 64)
                        xt_ = XT[hh]
                        xtk = "R_XT%d" % hh
                        for qi, (lh, lk, rh, rk_) in enumerate(((bF, bK, aF, aK), (bF, bK, rF, rK), (kF, kK, aF, aK), (kF, kK, rF, rK))):
                            kb.op("pe", lambda: nc.tensor.matmul(ps[hh][:, qi * 128:(qi + 1) * 128], lhsT=lh[hp, sl], rhs=rh[hp, sl], start=True, stop=True),
                                  reads=[lk, rk_], writes=[PS[hh]])
                        kb.op("dve", lambda: nc.vector.tensor_tensor(out=xt_[:], in0=ps[hh][:, 0:512], in1=mask4[:, d, :], op=ALU.mult),
                              reads=[PS[hh], "mask4"], writes=[xtk])
                        kb.op("pe", lambda: nc.tensor.matmul(ps[2 + hh][:, 0:128], lhsT=aF[hp, sl], rhs=bF[hp, sl], start=True, stop=True),
                              reads=[aK, bK], writes=[PS[2 + hh]])
                        L_, LT_ = Lm[0], LT[0]
                        kb.op("dve", lambda: nc.vector.tensor_tensor(out=L_[:], in0=ps[2 + hh][:, 0:128], in1=maskL[:, d, :], op=ALU.mult),
                              reads=[PS[2 + hh], "maskL"], writes=["R_L0"])
                        kb.op("act", lambda: nc.scalar.activation(out=LT_[:], in_=xt_[:, 0:128], func=AF.Copy), reads=[xtk], writes=["R_LT0"])
                        kb.op("pe", lambda: nc.tensor.matmul(ps[2 + hh][:, 128:192], lhsT=xt_[:, 256:384], rhs=tok[:, 1, hc], start=True, stop=True),
                              reads=[xtk, "R_tok"], writes=[PS[2 + hh]])
                        kb.op("act", lambda: nc.scalar.activation(out=X[0][:, 0:64], in_=tok[:, 0, hc], func=AF.Copy), reads=["R_tok"], writes=["R_X0"])
                        kb.op("dve", lambda: nc.vector.tensor_copy(out=X[0][:, 64:128], in_=ps[2 + hh][:, 128:192]), reads=[PS[2 + hh]], writes=["R_X0"])
                        cur = 0
                        for lev in range(7):
                            nxt = 1 - cur
                            pbx = 4 + lev % 2
                            kb.op("pe", lambda: nc.tensor.matmul(ps[pbx][:, 0:128], lhsT=LT[cur][:], rhs=X[cur][:], start=True, stop=True),
                                  reads=["R_LT%d" % cur, "R_X%d" % cur], writes=[PS[pbx]])
                            kb.op("dve", lambda: nc.vector.tensor_tensor(out=X[nxt][:], in0=ps[pbx][:, 0:128], in1=X[cur][:], op=ALU.add),
                                  reads=[PS[pbx], "R_X%d" % cur], writes=["R_X%d" % nxt])
                            if lev < 6:
                                kb.op("pe", lambda: nc.tensor.matmul(ps[pbx][:, 128:256], lhsT=Lm[cur][:], rhs=LT[cur][:], start=True, stop=True),
                                      reads=["R_L%d" % cur, "R_LT%d" % cur], writes=[PS[pbx]])
                                kb.op("pe", lambda: nc.tensor.matmul(ps[pbx][:, 256:384], lhsT=LT[cur][:], rhs=Lm[cur][:], start=True, stop=True),
                                      reads=["R_L%d" % cur, "R_LT%d" % cur], writes=[PS[pbx]])
                                kb.op("act", lambda: nc.scalar.activation(out=LT[nxt][:], in_=ps[pbx][:, 128:256], func=AF.Copy), reads=[PS[pbx]], writes=["R_LT%d" % nxt])
                                kb.op("act", lambda: nc.scalar.activation(out=Lm[nxt][:], in_=ps[pbx][:, 256:384], func=AF.Copy), reads=[PS[pbx]], writes=["R_L%d" % nxt])
                            cur = nxt
                        Xf = X[cur]
                        Xk = "R_X%d" % cur
                        W1 = Xf[:, 0:64]
                        W2 = Xf[:, 64:128]
                        pz = 2 + hh
                        kb.op("pe", lambda: nc.tensor.matmul(ps[pz][hp, 192:256], lhsT=W1, rhs=tok[:, 2, hc], start=True, stop=True),
                              reads=[Xk, "R_tok"], writes=[PS[pz]])
                        kb.op("dve", lambda: nc.vector.scalar_tensor_tensor(out=PT[hp, :], in0=ident[hp, hc], scalar=gC[hp, c:c + 1], in1=ps[pz][hp, 192:256],
                                                                            op0=ALU.mult, op1=ALU.add), reads=["ident", "R_gC", PS[pz]], writes=["R_PT%d" % hh])
                        kb.op("pe", lambda: nc.tensor.matmul(ps[pz][hp, 256:320], lhsT=tok[:, 2, hc], rhs=W2, start=True, stop=False),
                              reads=[Xk, "R_tok"], writes=[PS[pz]])
                        kb.op("pe", lambda: nc.tensor.matmul(ps[pz][hp, 256:320], lhsT=tok[:, 3, hc], rhs=tok[:, 1, hc], start=False, stop=True),
                              reads=["R_tok"], writes=[PS[pz]])
                        kb.op("act", lambda: nc.scalar.activation(out=Qt[hp, :], in_=ps[pz][hp, 256:320], func=AF.Copy), reads=[PS[pz]], writes=["R_Q%d" % hh])
                        kb.op("pe", lambda: nc.tensor.matmul(ps[pz][hp, 320:448], lhsT=W1, rhs=xt_[:, 128:256], start=True, stop=True),
                              reads=[Xk, xtk], writes=[PS[pz]])
                        kb.op("dve", lambda: nc.vector.tensor_tensor(out=GT[hp, :], in0=ps[pz][hp, 320:448], in1=rF[hp, sl], op=ALU.add),
                              reads=[PS[pz], rK], writes=["R_GT%d" % hh])
                        py = 7
                        kb.op("pe", lambda: nc.tensor.matmul(ps[py][:, hh * 64:(hh + 1) * 64], lhsT=xt_[:, 128:256], rhs=W2, start=True, stop=False),
                              reads=[xtk, Xk], writes=[PS[py]])
                        kb.op("pe", lambda: nc.tensor.matmul(ps[py][:, hh * 64:(hh + 1) * 64], lhsT=xt_[:, 384:512], rhs=tok[:, 1, hc], start=False, stop=False),
                              reads=[xtk, "R_tok"], writes=[PS[py]])
                        kb.op("pe", lambda: nc.tensor.matmul(ps[py][:, hh * 64:(hh + 1) * 64], lhsT=GT[hp, :], rhs=Mst[hp, si, :], start=False, stop=True),
                              reads=["R_GT%d" % hh, "R_M%d" % hh], writes=[PS[py]])
                        kb.op("act", lambda: nc.scalar.activation(out=yt_[:, hh * 64:(hh + 1) * 64], in_=ps[py][:, hh * 64:(hh + 1) * 64], func=AF.Copy),
                              reads=[PS[py]], writes=[ytk])
                        kb.op("pe", lambda: nc.tensor.matmul(ps[pz][hp, 448:512], lhsT=PT[hp, :], rhs=Mst[hp, si, :], start=True, stop=True),
                              reads=["R_PT%d" % hh, "R_M%d" % hh], writes=[PS[pz]])
                        kb.op("dve", lambda: nc.vector.tensor_tensor(out=Mst[hp, si, :], in0=ps[pz][hp, 448:512], in1=Qt[hp, :], op=ALU.add),
                              reads=[PS[pz], "R_Q%d" % hh], writes=["R_M%d" % hh])
                    yT_ = ytT[(yn - 1) % 2]
                    yTk = "R_yT%d" % ((yn - 1) % 2)
                    kb.op("pe", lambda: nc.tensor.transpose(out=ps[6][:, 0:128], in_=yt_[:], identity=ident[:]), reads=[ytk, "ident"], writes=[PS[6]])
                    evac(yT_[:], ps[6][:, 0:128], [PS[6]], [yTk])
                    kb.dma("pool", ywk[d, ct * 128:(ct + 1) * 128, tglob:tglob + CH], yT_[:], reads=[yTk], semkey="R_yst%d" % ((yn - 1) % 2))
            for hh in range(2):
                for pi in range(2):
                    kb.dma("pool", wkv_out[pi, j, d, 2 * ct + hh], Mst[hh * 64:(hh + 1) * 64, 1 + pi, :], reads=["R_M%d" % hh], semkey="R_mst%d" % hh, is_output=True)
    kb.end_phase()


def rwkv_post(kb, nc, ps, PS, pack, pv, j, y_dram, scr, evac, ident, cst):
    wk = scr["wk"]
    ywk = scr["ywk"]
    bones = cst["bones"]
    kb.begin_phase()
    ya = kb.sb("O_ya", [128, NT])
    yb = kb.sb("O_yb", [128, NT])
    mt = kb.sb("O_mt", [128, NT])
    sq = kb.sb("O_sq", [128, NT])
    b0 = kb.sb("O_b0", [128, NT])
    b1 = kb.sb("O_b1", [128, NT])
    gg = kb.sb("O_g", [128, NT])
    ob = kb.sb("O_ob", [128, NT], BF16)
    for ct in range(4):
        kb.dma("sp", ya[:], ywk[0, ct * 128:(ct + 1) * 128, :], writes=["O_ya"])
        kb.dma("sp", yb[:], ywk[1, ct * 128:(ct + 1) * 128, :], writes=["O_yb"])
        kb.dma("sp", b0[:], wk[(20 + ct) * 128:(21 + ct) * 128, :], writes=["O_b0"])
        kb.dma("sp", b1[:], wk[(24 + ct) * 128:(25 + ct) * 128, :], writes=["O_b1"])
        kb.dma("sp", gg[:], wk[(16 + ct) * 128:(17 + ct) * 128, :], writes=["O_g"])
        kb.op("dve", lambda: nc.vector.tensor_tensor(out=ya[:], in0=ya[:], in1=yb[:], op=ALU.add), reads=["O_ya", "O_yb"], writes=["O_ya"])
        for blk in range(NT // 512):
            c0 = blk * 512
            pb = blk % 4
            kb.op("pe", lambda: nc.tensor.matmul(ps[pb][:, 0:512], lhsT=bones[:], rhs=ya[:, c0:c0 + 512], start=True, stop=True),
                  reads=["bones", "O_ya"], writes=[PS[pb]])
            kb.op("act", lambda: nc.scalar.activation(out=mt[:, c0:c0 + 512], in_=ps[pb][:, 0:512], func=AF.Copy, scale=1.0 / 64),
                  reads=[PS[pb]], writes=["O_mt"])
        kb.op("dve", lambda: nc.vector.tensor_tensor(out=ya[:], in0=ya[:], in1=mt[:], op=ALU.subtract), reads=["O_ya", "O_mt"], writes=["O_ya"])
        kb.op("dve", lambda: nc.vector.tensor_tensor(out=sq[:], in0=ya[:], in1=ya[:], op=ALU.mult), reads=["O_ya"], writes=["O_sq"])
        for blk in range(NT // 512):
            c0 = blk * 512
            pb = blk % 4
            kb.op("pe", lambda: nc.tensor.matmul(ps[pb][:, 0:512], lhsT=bones[:], rhs=sq[:, c0:c0 + 512], start=True, stop=True),
                  reads=["bones", "O_sq"], writes=[PS[pb]])
            kb.op("act", lambda: nc.scalar.activation(out=mt[:, c0:c0 + 512], in_=ps[pb][:, 0:512], func=AF.Sqrt, scale=1.0 / 64, bias=cst["eps"][:, 2:3]),
                  reads=[PS[pb], "pv_eps"], writes=["O_mt"])
        kb.op("dve", lambda: nc.vector.reciprocal(out=mt[:], in_=mt[:]), reads=["O_mt"], writes=["O_mt"])
        kb.op("dve", lambda: nc.vector.tensor_tensor(out=ya[:], in0=ya[:], in1=mt[:], op=ALU.mult), reads=["O_ya", "O_mt"], writes=["O_ya"])
        gc = pack.col("wkv_gn_g", j, ct)
        bc = pack.col("wkv_gn_b", j, ct)
        kb.op("act", lambda: nc.scalar.activation(out=ya[:], in_=ya[:], func=AF.Identity, scale=pv[:, gc:gc + 1], bias=pv[:, bc:bc + 1]),
              reads=["O_ya", "pv"], writes=["O_ya"])
        kb.op("dve", lambda: nc.vector.tensor_tensor(out=b0[:], in0=b0[:], in1=b1[:], op=ALU.add), reads=["O_b0", "O_b1"], writes=["O_b0"])
        kb.op("dve", lambda: nc.vector.tensor_tensor(out=b0[:], in0=b0[:], in1=ya[:], op=ALU.add), reads=["O_b0", "O_ya"], writes=["O_b0"])
        kb.op("dve", lambda: nc.vector.tensor_tensor(out=ob[:], in0=b0[:], in1=gg[:], op=ALU.mult), reads=["O_b0", "O_g"], writes=["O_ob"])
        kb.dma("pool", y_dram[512 + ct * 128:512 + (ct + 1) * 128, :], ob[:], reads=["O_ob"], semkey="O_yst")
    kb.end_phase()


def _topk16_batch(kb, nc, srcs, srckeys, scr3, scrk, m16, i16, outk):
    G = len(srcs)
    for g in range(G):
        kb.op("dve", lambda: nc.vector.max(out=m16[:, g, 0:8], in_=srcs[g]), reads=srckeys[g], writes=["%s_a%d" % (outk, g)])
    for g in range(G):
        kb.op("dve", lambda: nc.vector.match_replace(out=scr3[:, g, :], in_to_replace=m16[:, g, 0:8], in_values=srcs[g], imm_value=-1e30),
              reads=srckeys[g] + ["%s_a%d" % (outk, g)], writes=["%s%d" % (scrk, g)])
    for g in range(G):
        kb.op("dve", lambda: nc.vector.max(out=m16[:, g, 8:16], in_=scr3[:, g, :]), reads=["%s%d" % (scrk, g)], writes=["%s_b%d" % (outk, g)])
    for g in range(G):
        kb.op("dve", lambda: nc.vector.max_index(out=i16[:, g, 0:8], in_max=m16[:, g, 0:8], in_values=srcs[g]),
              reads=srckeys[g] + ["%s_a%d" % (outk, g)], writes=["%s_ia%d" % (outk, g)])
    for g in range(G):
        kb.op("dve", lambda: nc.vector.max_index(out=i16[:, g, 8:16], in_max=m16[:, g, 8:16], in_values=scr3[:, g, :]),
              reads=["%s%d" % (scrk, g), "%s_b%d" % (outk, g)], writes=["%s_ib%d" % (outk, g)])
    return [("%s_%s%d" % (outk, sfx, g)) for g in range(G) for sfx in ("a", "b", "ia", "ib")]


def peer_select(kb, nc, ps, PS, l, peer_wq, keysT, h2_dram, sel_dram, ident, iota, evac, load_w_bf16):
    kb.begin_phase()
    wq = kb.sb("Q_wq", [128, 8, 2048], BF16)
    load_w_bf16(wq, "Q_wq", peer_wq[l], 2048)
    kT = kb.sb("Q_kT", [128, 16, 128], BF16)
    kb.dma("pool", kT[:], keysT[l], writes=["Q_kT"], max_dma_last_dim=4096)
    h2b = [kb.sb("Q_h2_%d" % i, [128, 8, TT], BF16) for i in range(2)]
    qT = kb.sb("Q_qT", [128, 16, TT], BF16)
    scrA = kb.sb("Q_scrA", [128, 16, 128])
    scrB = kb.sb("Q_scrB", [128, 8, 256])
    v16 = kb.sb("Q_v16", [128, 16, 16])
    i16 = kb.sb("Q_i16", [128, 16, 16], U32)
    i16f = kb.sb("Q_i16f", [128, 16, 16])
    cand = kb.sb("Q_cand", [128, 8, 256])
    sc16 = kb.sb("Q_sc16", [128, 8, 16])
    ci16 = kb.sb("Q_ci16", [128, 8, 16], U32)
    ai = kb.sb("Q_ai", [128, 8, 16], U32)
    bi = kb.sb("Q_bi", [128, 8, 16], U32)
    af = kb.sb("Q_af", [128, 8, 16])
    bf = kb.sb("Q_bf", [128, 8, 16])
    oh = kb.sb("Q_oh", [128, 8, 16, 16])
    ssum = kb.sb("Q_ssum", [128, 8])
    seltm = kb.sb("Q_seltm", [128, 3, 128])
    selT = [kb.sb("Q_selT%d" % i, [128, 3, TT]) for i in range(2)]
    io16 = iota[:, 0:16].unsqueeze(1).unsqueeze(1).to_broadcast([128, 8, 16, 16])
    for t in range(NTILES):
        t0 = t * TT
        h2 = h2b[t % 2]
        h2k = "Q_h2_%d" % (t % 2)
        sT = selT[t % 2]
        sTk = "Q_selT%d" % (t % 2)
        kb.dma("sp", h2[:], h2_dram[:, t0:t0 + TT].rearrange("(k p) t -> p k t", p=128), writes=[h2k])
        for hp in range(16):
            pb = 4 + hp % 2
            for k in range(8):
                kb.op("pe", lambda: nc.tensor.matmul(ps[pb][:, 0:TT], lhsT=wq[:, k, hp * 128:(hp + 1) * 128], rhs=h2[:, k, :],
                                                     start=(k == 0), stop=(k == 7)), reads=["Q_wq", h2k], writes=[PS[pb]])
            evac(qT[:, hp, :], ps[pb][:, 0:TT], [PS[pb]], ["Q_qT"])
        for sub in range(2):
            for hp in range(16):
                pb = hp // 4
                kb.op("pe", lambda: nc.tensor.matmul(ps[pb][:, (hp % 4) * 128:(hp % 4 + 1) * 128], lhsT=qT[:, hp, sub * 128:(sub + 1) * 128],
                                                     rhs=kT[:, hp, :], start=True, stop=True), reads=["Q_qT", "Q_kT"], writes=[PS[pb]])
            vkeys = _topk16_batch(kb, nc, [ps[hp // 4][:, (hp % 4) * 128:(hp % 4 + 1) * 128] for hp in range(16)],
                                  [[PS[hp // 4]] for hp in range(16)], scrA, "Q_scrA",
                                  v16, i16, "Q_v16")
            kb.op("dve", lambda: nc.vector.tensor_copy(out=i16f[:], in_=i16[:]), reads=vkeys, writes=["Q_i16f"])
            kb.op("dve", lambda: nc.vector.tensor_tensor(out=cand[:].rearrange("p h (a b) -> p h a b", a=16),
                                                         in0=v16[:, 0::2, :].unsqueeze(3).to_broadcast([128, 8, 16, 16]),
                                                         in1=v16[:, 1::2, :].unsqueeze(2).to_broadcast([128, 8, 16, 16]), op=ALU.add),
                  reads=vkeys, writes=["Q_cand"])
            skeys = _topk16_batch(kb, nc, [cand[:, h, :] for h in range(8)], [["Q_cand"] for h in range(8)],
                                  scrB, "Q_scrB", sc16, ci16, "Q_sc16")
            kb.op("dve", lambda: nc.vector.tensor_tensor(out=af[:], in0=sc16[:], in1=sc16[:, :, 0:1].to_broadcast([128, 8, 16]), op=ALU.subtract),
                  reads=skeys, writes=["Q_af"])
            kb.op("act", lambda: nc.scalar.activation(out=af[:], in_=af[:], func=AF.Exp), reads=["Q_af"], writes=["Q_af"])
            kb.op("dve", lambda: nc.vector.tensor_reduce(out=ssum[:], in_=af[:], axis=AX.X, op=ALU.add), reads=["Q_af"], writes=["Q_ssum"])
            kb.op("dve", lambda: nc.vector.reciprocal(out=ssum[:], in_=ssum[:]), reads=["Q_ssum"], writes=["Q_ssum"])
            kb.op("dve", lambda: nc.vector.tensor_tensor(out=seltm[:, 2, :].rearrange("p (h k) -> p h k", h=8), in0=af[:],
                                                         in1=ssum[:].unsqueeze(2).to_broadcast([128, 8, 16]), op=ALU.mult),
                  reads=["Q_af", "Q_ssum"], writes=["Q_seltm"])
            kb.op("dve", lambda: nc.vector.tensor_single_scalar(out=ai[:], in_=ci16[:], scalar=4, op=ALU.logical_shift_right),
                  reads=skeys, writes=["Q_ai"])
            kb.op("dve", lambda: nc.vector.tensor_single_scalar(out=bi[:], in_=ci16[:], scalar=15, op=ALU.bitwise_and),
                  reads=skeys, writes=["Q_bi"])
            kb.op("dve", lambda: nc.vector.tensor_copy(out=af[:], in_=ai[:]), reads=["Q_ai", "Q_seltm"], writes=["Q_af"])
            kb.op("dve", lambda: nc.vector.tensor_copy(out=bf[:], in_=bi[:]), reads=["Q_bi"], writes=["Q_bf"])
            for which, xf in ((0, af), (1, bf)):
                xk_ = "Q_af" if which == 0 else "Q_bf"
                kb.op("dve", lambda: nc.vector.tensor_tensor(out=oh[:], in0=io16, in1=xf[:].unsqueeze(3).to_broadcast([128, 8, 16, 16]), op=ALU.is_equal),
                      reads=["iota", xk_], writes=["Q_oh"])
                kb.op("dve", lambda: nc.vector.tensor_tensor(out=oh[:], in0=oh[:],
                                                             in1=i16f[:, which::2, :].unsqueeze(2).to_broadcast([128, 8, 16, 16]), op=ALU.mult),
                      reads=["Q_oh", "Q_i16f"], writes=["Q_oh"])
                kb.op("dve", lambda: nc.vector.tensor_reduce(out=seltm[:, which, :].rearrange("p (h k) -> p h k", h=8), in_=oh[:], axis=AX.X, op=ALU.add),
                      reads=["Q_oh"], writes=["Q_seltm"])
            for s_ in range(3):
                kb.op("pe", lambda: nc.tensor.transpose(out=ps[6][:, s_ * 128:(s_ + 1) * 128], in_=seltm[:, s_, :], identity=ident[:]),
                      reads=["Q_seltm", "ident"], writes=[PS[6]])
            evac(sT[:, :, sub * 128:(sub + 1) * 128], ps[6][:, 0:384].rearrange("p (s t) -> p s t", s=3), [PS[6]], [sTk])
        kb.dma("pool", sel_dram[:, :, t0:t0 + TT], sT[:], reads=[sTk], semkey="Q_selst%d" % (t % 2))
    kb.end_phase()


def peer_alloc(kb, nc):
    st = {}
    st["sel"] = [kb.sb("P_sel%d" % i, [128, 3, TT]) for i in range(2)]
    st["A"] = kb.sb("P_A", [128, 64, 128], BF16)
    st["B"] = kb.sb("P_B", [128, 64, 128], BF16)
    st["G"] = kb.sb("P_G", [128, TT, 128], BF16)
    st["UV"] = [kb.sb("P_UV%d" % i, [128, 2, 2048], BF16) for i in range(3)]
    st["act"] = [kb.sb("P_act%d" % i, [128, 2, TT]) for i in range(2)]
    st["ga"] = [kb.sb("P_ga%d" % i, [128, 2, TT], BF16) for i in range(2)]
    st["n"] = 0
    return st


def peer_mix(kb, nc, ps, PS, st, l, t, h2, h2k, uv_bf, sel_dram, iota, zer_b, evac):
    t0 = t * TT
    sel = st["sel"][t % 2]
    selk = "P_sel%d" % (t % 2)
    A, B, G = st["A"], st["B"], st["G"]
    kb.dma("sp", sel[:], sel_dram[:, :, t0:t0 + TT], writes=[selk])
    io = iota[:, :].unsqueeze(1).to_broadcast([128, 64, 128])
    for q in range(TT // 64):
        c0 = q * 64
        kb.op("dve", lambda: nc.vector.tensor_tensor(out=A[:], in0=io, in1=sel[:, 0, c0:c0 + 64].unsqueeze(2).to_broadcast([128, 64, 128]), op=ALU.is_equal),
              reads=["iota", selk], writes=["P_A"])
        kb.op("dve", lambda: nc.vector.tensor_tensor(out=A[:], in0=A[:], in1=sel[:, 2, c0:c0 + 64].unsqueeze(2).to_broadcast([128, 64, 128]), op=ALU.mult),
              reads=["P_A", selk], writes=["P_A"])
        kb.op("dve", lambda: nc.vector.tensor_tensor(out=B[:], in0=io, in1=sel[:, 1, c0:c0 + 64].unsqueeze(2).to_broadcast([128, 64, 128]), op=ALU.is_equal),
              reads=["iota", selk], writes=["P_B"])
        for g4 in range(16):
            pb = 6 + g4 % 2
            for tk in range(4):
                tok = g4 * 4 + tk
                kb.op("pe", lambda: nc.tensor.matmul(ps[pb][:, tk * 128:(tk + 1) * 128], lhsT=A[:, tok, :], rhs=B[:, tok, :], start=True, stop=True),
                      reads=["P_A", "P_B"], writes=[PS[pb]])
            kb.op("act", lambda: nc.scalar.activation(out=G[:, c0 + g4 * 4:c0 + g4 * 4 + 4, :].rearrange("p t e -> p (t e)"), in_=ps[pb][:, 0:512], func=AF.Copy),
                  reads=[PS[pb]], writes=["P_G"])
    for pb in range(4):
        kb.op("pe", lambda: nc.tensor.matmul(ps[pb][:, 0:512], lhsT=zer_b[:, 0:128], rhs=zer_b[:, 0:512], start=True, stop=False, skip_group_check=True),
              reads=["zer_b"], writes=[PS[pb]])
    for m in range(64):
        n = st["n"]
        st["n"] += 1
        UV = st["UV"][n % 3]
        UVk = "P_UV%d" % (n % 3)
        kb.dma("sp", UV[:], uv_bf[l, 2 * m:2 * m + 2].rearrange("e p c -> p e c"), writes=[UVk])
        pb = 4 + n % 2
        for c2 in range(2):
            for k in range(8):
                kb.op("pe", lambda: nc.tensor.matmul(ps[pb][:, c2 * TT:(c2 + 1) * TT], lhsT=UV[:, c2, k * 128:(k + 1) * 128], rhs=h2[:, k, :],
                                                     start=(k == 0), stop=(k == 7)), reads=[UVk, h2k], writes=[PS[pb]])
        ac = st["act"][n % 2]
        ack = "P_act%d" % (n % 2)
        ga = st["ga"][n % 2]
        gak = "P_ga%d" % (n % 2)
        kb.op("act", lambda: nc.scalar.activation(out=ac[:].rearrange("p c t -> p (c t)"), in_=ps[pb][:, 0:2 * TT], func=AF.Gelu_apprx_tanh),
              reads=[PS[pb]], writes=[ack])
        kb.op("dve", lambda: nc.vector.tensor_tensor(out=ga[:], in0=ac[:], in1=G[:, :, 2 * m:2 * m + 2].rearrange("p t e -> p e t"), op=ALU.mult),
              reads=[ack, "P_G"], writes=[gak])
        for c2 in range(2):
            for dc in range(8):
                kb.op("pe", lambda: nc.tensor.matmul(ps[dc // 2][:, (dc % 2) * TT:(dc % 2 + 1) * TT], lhsT=UV[:, c2, 1024 + dc * 128:1024 + (dc + 1) * 128],
                                                     rhs=ga[:, c2, :], start=False, stop=(m == 63 and c2 == 1), skip_group_check=True),
                      reads=[UVk, gak], writes=[PS[dc // 2]])


def _prep_shared(inp):
    sh = {}
    sh["posT"] = np.ascontiguousarray(_grid_pos_embed(NS).T)
    sh["ident"] = np.eye(128, dtype=np.float32)
    sh["iota"] = np.ascontiguousarray(np.broadcast_to(np.arange(128, dtype=np.float32)[None, :], (128, 128)))
    for k in ("w_mod", "w_in_e", "w_out_e", "w_in_o", "w_out_o", "peer_wq"):
        sh[k] = np.ascontiguousarray(inp[k], dtype=np.float32)
    sh["keysT"] = np.ascontiguousarray(np.transpose(inp["peer_keys"], (0, 4, 1, 2, 3)).reshape(DEPTH, 128, 16, 128))
    if not KSKIP_PEER:
        u = np.asarray(inp["peer_u"]).reshape(DEPTH, 128, 128, 8, 128)
        uv = np.empty((DEPTH, 128, 128, 2048), np.float32)
        uv[..., 0:1024] = np.transpose(u, (0, 2, 4, 3, 1)).reshape(DEPTH, 128, 128, 1024)
        v = np.asarray(inp["peer_v"]).reshape(DEPTH, 128, 128, 1024)
        uv[..., 1024:2048] = np.transpose(v, (0, 2, 1, 3))
        sh["peer_uv"] = uv
    else:
        sh["peer_uv"] = np.zeros((DEPTH, 1, 128, 2048), np.float32)
    c = np.arange(128, dtype=np.int64)
    ang = 2.0 * np.pi * ((c[:, None] * c[None, :]) % 128).astype(np.float64) / 128.0
    sh["dftc"] = np.ascontiguousarray(np.concatenate([np.cos(ang), np.sin(ang)], 1).astype(np.float32))
    if any(l_ % 2 == 0 for l_ in range(KSTART, KDEPTH)):
        sh["dftS"] = np.ascontiguousarray(np.stack(_dft_tables(NS)))
    else:
        sh["dftS"] = np.zeros((2, 128, 128), np.float32)
    sh["dftP"] = np.ascontiguousarray(np.stack(_dft_tables(NP)))
    wbd = np.zeros((2, 2, 2, 4, 128, 128), np.float32)
    for ai, nm in enumerate(("lru_wa", "lru_wx")):
        w = np.asarray(inp[nm])
        for ct in range(4):
            for hh in range(2):
                wbd[:, :, ai, ct, hh * 64:(hh + 1) * 64, hh * 64:(hh + 1) * 64] = w[:, :, 2 * ct + hh]
    sh["lru_wbd"] = wbd
    for k in ("wkv_w2", "wkv_a2", "wkv_g2"):
        sh[k] = np.ascontiguousarray(inp[k], dtype=np.float32)
    r = np.arange(128)
    su = (r[None, :] > r[:, None]).astype(np.float32)
    iu = (r[None, :] >= r[:, None]).astype(np.float32)
    sl = (r[None, :] < r[:, None]).astype(np.float32)
    il = (r[None, :] <= r[:, None]).astype(np.float32)
    cm = np.zeros((128, 2, 768), np.float32)
    cm[:, 0, :] = np.concatenate([su, iu, su, iu, sl, np.zeros((128, 128), np.float32)], 1)
    cm[:, 1, :] = np.concatenate([sl, il, sl, il, su, np.zeros((128, 128), np.float32)], 1)
    sh["cmask"] = cm
    bo = np.zeros((128, 128), np.float32)
    bo[:64, :64] = 1.0
    bo[64:, 64:] = 1.0
    sh["bones"] = bo
    return sh


def _make_pack(inp, b):
    pk = Pack()
    pk.add("b_mod", inp["b_mod"].reshape(DEPTH, 48, 128))
    for nm in ("ln1_g", "ln1_b", "ln2_g", "ln2_b", "sconv_w", "sconv_b", "lru_conv_w", "lru_conv_b", "lru_ba", "lru_bx",
               "lru_lambda", "wkv_mu", "wkv_w0", "wkv_a0", "wkv_kk", "wkv_ka", "wkv_gn_g", "wkv_gn_b"):
        pk.add(nm, inp[nm])
    pk.add("wkv_rk", np.asarray(inp["wkv_rk"]).reshape(2, 512))
    pk.add("state_lru", inp["state_lru"][b])
    return pk


def kernel(**inputs):
    inp = {k: np.asarray(v) for k, v in inputs.items()}
    sh = _prep_shared(inp)
    packs = [_make_pack(inp, c % 4) for c in range(8)]
    kb = build(packs[0])
    print("kernel: instructions", kb.ninst, "sems", len(kb.sems), flush=True)
    in_maps = []
    for c in range(8):
        b = c % 4
        m = dict(sh)
        xs = inp["x_sample"][b]
        xp = inp["x_prompt"][2 * c:2 * c + 2].reshape(2 * NP, D)
        m["xT"] = np.ascontiguousarray(np.concatenate([xs, xp], 0).T.astype(np.float32))
        cv = np.stack([inp["c"][b], inp["c_ctx"]], -1).astype(np.float32)
        m["cv"] = np.ascontiguousarray(cv.reshape(8, 128, 2).transpose(1, 0, 2))
        m["pvec"] = packs[c].array()
        m["wkv0"] = np.ascontiguousarray(np.swapaxes(inp["state_wkv"][b], -1, -2).astype(np.float32))
        in_maps.append(m)
    res = run_bass_kernel_spmd(kb.nc, in_maps, core_ids=list(range(8)))
    R = res.results
    y_prompt = np.zeros((16, NP, D), np.float32)
    y_sample = np.zeros((4, NS, D), np.float32)
    lru_new = np.zeros((16, 2, 2, 512), np.float32)
    wkv_new = np.zeros((16, 2, 2, 8, 64, 64), np.float32)
    for c in range(8):
        yT = np.asarray(R[c]["yT"])
        if c < 4:
            y_sample[c] = yT[:, :NS].T
        y_prompt[2 * c] = yT[:, NS:NS + NP].T
        y_prompt[2 * c + 1] = yT[:, NS + NP:].T
        lo = np.asarray(R[c]["lru_new"]).reshape(128, 2, 2, 2, 4)
        lru_new[2 * c:2 * c + 2] = np.transpose(lo, (1, 2, 3, 4, 0)).reshape(2, 2, 2, 512)
        wo = np.asarray(R[c]["wkv_new"])
        wkv_new[2 * c:2 * c + 2] = np.transpose(wo, (0, 1, 2, 3, 5, 4))
    if KDEBUG:
        _DBG.clear()
        _DBG.update({k: np.asarray(v) for k, v in R[0].items()})
    return (y_prompt, y_sample, lru_new, wkv_new)
```
